# Optimizing a Trainium2 kernel written in Bass

```python
import math
import jax
import jax.numpy as jnp
from jax import lax
import numpy as np


D_MODEL = 1024
BATCH = 8
SEQ = 4096
DEPTH = 4

N_MIXERS = 4
HEAD_DIM = 64
N_HEADS = D_MODEL // HEAD_DIM
ROT_DIM = HEAD_DIM // 4
ROPE_THETA = 500000.0
Q_BLOCK = 128
D_FF = 2816
ALPHA = (2 * DEPTH) ** 0.25
BETA = (8 * DEPTH) ** -0.25
N_ADA = 9
LN_EPS = 1e-5
RMS_EPS = 1e-6

A_KV_RANK = 128
A_IDX_HEADS = 8
A_IDX_DIM = 32
A_TOPK_MAX = 256
A_IN = D_MODEL + A_KV_RANK + A_IDX_HEADS * A_IDX_DIM + A_IDX_DIM + A_IDX_HEADS
B_IN = 3 * D_MODEL + N_HEADS
C_KV_HEADS = 2
C_GROUP = N_HEADS // C_KV_HEADS
C_WINDOW = 128
C_IN = D_MODEL + 2 * C_KV_HEADS * HEAD_DIM
D_HEADS = N_HEADS // 2
D_QK_DIM = HEAD_DIM
D_V_DIM = 2 * HEAD_DIM
D_IN = 2 * D_HEADS * 2 * D_QK_DIM + D_HEADS * D_V_DIM

kernel_name = 'hybrid_interleaved_dsa_fox_swa_diff_macaron'


def layer_norm(x, g, b):
    xf = x.astype(jnp.float32)
    mu = jnp.mean(xf, axis=-1, keepdims=True)
    var = jnp.mean(jnp.square(xf - mu), axis=-1, keepdims=True)
    return ((xf - mu) * lax.rsqrt(var + LN_EPS) * g + b).astype(x.dtype)


def rms_norm(x, g):
    xf = x.astype(jnp.float32)
    return (xf * lax.rsqrt(jnp.mean(xf * xf, axis=-1, keepdims=True) + RMS_EPS) * g).astype(x.dtype)


def rope_tables(seq, rot_dim):
    inv = ROPE_THETA ** (-jnp.arange(0, rot_dim, 2, dtype=jnp.float32) / rot_dim)
    ang = jnp.arange(seq, dtype=jnp.float32)[:, None] * inv[None, :]
    return jnp.cos(ang), jnp.sin(ang)


def partial_rope(x, cos, sin):
    half = cos.shape[-1]
    x1, x2, rest = x[..., :half], x[..., half:2 * half], x[..., 2 * half:]
    c = cos[:, None, :].astype(x.dtype)
    s = sin[:, None, :].astype(x.dtype)
    return jnp.concatenate([x1 * c - x2 * s, x2 * c + x1 * s, rest], axis=-1)


def to_blocks(t):
    b, s = t.shape[:2]
    return jnp.moveaxis(t.reshape(b, s // Q_BLOCK, Q_BLOCK, *t.shape[2:]), 1, 0)


def from_blocks(t):
    nb, b, q = t.shape[:3]
    return jnp.moveaxis(t, 0, 1).reshape(b, nb * q, *t.shape[3:])


def swiglu(h, w_in, w_out):
    g, u = jnp.split(h @ w_in, 2, axis=-1)
    return (jax.nn.silu(g) * u) @ w_out


def mixer_dsa(h, w_in, kv_norm, w_kv_up, w_out, cos, sin, cos_i, sin_i):
    b, s, _ = h.shape
    k_sel = min(A_TOPK_MAX, s // 4)
    splits = [D_MODEL, D_MODEL + A_KV_RANK, D_MODEL + A_KV_RANK + A_IDX_HEADS * A_IDX_DIM,
              D_MODEL + A_KV_RANK + A_IDX_HEADS * A_IDX_DIM + A_IDX_DIM]
    q, c_kv, q_idx, k_idx, w_idx = jnp.split(h @ w_in, splits, axis=-1)
    q = partial_rope(q.reshape(b, s, N_HEADS, HEAD_DIM), cos, sin)
    kv = rms_norm(c_kv, kv_norm) @ w_kv_up
    k = partial_rope(kv[:, :, None, :HEAD_DIM], cos, sin)[:, :, 0]
    v = kv[..., HEAD_DIM:]
    q_idx = partial_rope(q_idx.reshape(b, s, A_IDX_HEADS, A_IDX_DIM), cos_i, sin_i)
    k_idx = partial_rope(k_idx[:, :, None, :], cos_i, sin_i)[:, :, 0]
    w_idx = w_idx * (A_IDX_HEADS ** -0.5 * A_IDX_DIM ** -0.5)
    key_pos = jnp.arange(s)

    def block(args):
        qx, qix, wix, qpos = args
        rel = jax.nn.relu(jnp.einsum('bqhd,bkd->bqhk', qix, k_idx))
        score = jnp.einsum('bqh,bqhk->bqk', wix, rel).astype(jnp.float32)
        score = jnp.where((key_pos[None, :] <= qpos[:, None])[None], score, -jnp.inf)
        _, sel = lax.top_k(score, k_sel)
        valid = sel <= qpos[None, :, None]
        kg = jax.vmap(lambda kb, ib: kb[ib])(k, sel)
        vg = jax.vmap(lambda vb, ib: vb[ib])(v, sel)
        logits = jnp.einsum('bqhd,bqkd->bhqk', qx, kg).astype(jnp.float32) * HEAD_DIM ** -0.5
        logits = jnp.where(valid[:, None], logits, -jnp.inf)
        p = jax.nn.softmax(logits, axis=-1).astype(vg.dtype)
        return jnp.einsum('bhqk,bqkd->bqhd', p, vg)

    out = lax.map(block, (to_blocks(q), to_blocks(q_idx), to_blocks(w_idx), key_pos.reshape(-1, Q_BLOCK)))
    return from_blocks(out).reshape(b, s, D_MODEL) @ w_out


def mixer_fox(h, w_in, f_bias, w_out):
    b, s, _ = h.shape
    q, k, v, f = jnp.split(h @ w_in, [D_MODEL, 2 * D_MODEL, 3 * D_MODEL], axis=-1)
    q = q.reshape(b, s, N_HEADS, HEAD_DIM)
    k = k.reshape(b, s, N_HEADS, HEAD_DIM)
    v = v.reshape(b, s, N_HEADS, HEAD_DIM)
    log_f = jax.nn.log_sigmoid((f + f_bias).astype(jnp.float32))
    cum = jnp.cumsum(log_f, axis=1)
    cum_k = jnp.moveaxis(cum, 1, 2)
    key_pos = jnp.arange(s)

    def block(args):
        qx, cq, qpos = args
        logits = jnp.einsum('bqhd,bkhd->bhqk', qx, k).astype(jnp.float32) * HEAD_DIM ** -0.5
        logits = logits + jnp.moveaxis(cq, 1, 2)[..., None] - cum_k[:, :, None, :]
        logits = jnp.where(key_pos[None, :] <= qpos[:, None], logits, -jnp.inf)
        p = jax.nn.softmax(logits, axis=-1).astype(v.dtype)
        return jnp.einsum('bhqk,bkhd->bqhd', p, v)

    out = lax.map(block, (to_blocks(q), to_blocks(cum), key_pos.reshape(-1, Q_BLOCK)))
    return from_blocks(out).reshape(b, s, D_MODEL) @ w_out


def mixer_swa(h, w_in, sinks, w_out, cos, sin):
    b, s, _ = h.shape
    nb = s // Q_BLOCK
    q, k, v = jnp.split(h @ w_in, [D_MODEL, D_MODEL + C_KV_HEADS * HEAD_DIM], axis=-1)
    q = partial_rope(q.reshape(b, s, N_HEADS, HEAD_DIM), cos, sin)
    k = partial_rope(k.reshape(b, s, C_KV_HEADS, HEAD_DIM), cos, sin)
    v = v.reshape(b, s, C_KV_HEADS, HEAD_DIM)

    def band(t):
        tb = t.reshape(b, nb, Q_BLOCK, C_KV_HEADS, HEAD_DIM)
        prev = jnp.pad(tb, ((0, 0), (1, 0), (0, 0), (0, 0), (0, 0)))[:, :-1]
        return jnp.moveaxis(jnp.concatenate([prev, tb], axis=2), 1, 0)

    qb = jnp.moveaxis(q.reshape(b, nb, Q_BLOCK, C_KV_HEADS, C_GROUP, HEAD_DIM), 1, 0)
    qi = jnp.arange(Q_BLOCK)[:, None] + Q_BLOCK
    ki = jnp.arange(2 * Q_BLOCK)[None, :]
    in_window = (ki <= qi) & (qi - ki < C_WINDOW)
    sink = sinks.reshape(C_KV_HEADS, C_GROUP).astype(jnp.float32)

    def block(args):
        qx, kx, vx, n = args
        logits = jnp.einsum('bqgrd,bkgd->bgrqk', qx, kx).astype(jnp.float32) * HEAD_DIM ** -0.5
        ok = in_window & ((ki >= Q_BLOCK) | (n > 0))
        logits = jnp.where(ok, logits, -jnp.inf)
        sink_col = jnp.broadcast_to(sink[None, :, :, None, None], logits.shape[:-1] + (1,))
        p = jax.nn.softmax(jnp.concatenate([logits, sink_col], axis=-1), axis=-1)[..., :-1]
        return jnp.einsum('bgrqk,bkgd->bqgrd', p.astype(vx.dtype), vx)

    out = lax.map(block, (qb, band(k), band(v), jnp.arange(nb)))
    return jnp.moveaxis(out, 0, 1).reshape(b, s, D_MODEL) @ w_out


def mixer_diff(h, w_in, lam, subln_g, w_out, cos, sin, lambda_init):
    b, s, _ = h.shape
    qk_w = D_HEADS * 2 * D_QK_DIM
    q, k, v = jnp.split(h @ w_in, [qk_w, 2 * qk_w], axis=-1)
    q = partial_rope(q.reshape(b, s, 2 * D_HEADS, D_QK_DIM), cos, sin).reshape(b, s, D_HEADS, 2, D_QK_DIM)
    k = partial_rope(k.reshape(b, s, 2 * D_HEADS, D_QK_DIM), cos, sin).reshape(b, s, D_HEADS, 2, D_QK_DIM)
    v = v.reshape(b, s, D_HEADS, D_V_DIM)
    lam_f = lam.astype(jnp.float32)
    lam_val = (jnp.exp(jnp.sum(lam_f[0] * lam_f[1])) - jnp.exp(jnp.sum(lam_f[2] * lam_f[3])) + lambda_init)
    key_pos = jnp.arange(s)

    def block(args):
        qx, qpos = args
        logits = jnp.einsum('bqhcd,bkhcd->bhcqk', qx, k).astype(jnp.float32) * D_QK_DIM ** -0.5
        logits = jnp.where(key_pos[None, :] <= qpos[:, None], logits, -jnp.inf)
        p = jax.nn.softmax(logits, axis=-1)
        a = p[:, :, 0] - lam_val * p[:, :, 1]
        return jnp.einsum('bhqk,bkhd->bqhd', a.astype(v.dtype), v)

    out = from_blocks(lax.map(block, (to_blocks(q), key_pos.reshape(-1, Q_BLOCK))))
    out = rms_norm(out, subln_g) * (1.0 - lambda_init)
    return out.reshape(b, s, D_HEADS * D_V_DIM) @ w_out


def modulated_post_norm(x, mod, j, fn, res_w, g, b):
    shift, scale, gate = mod[:, 3 * j], mod[:, 3 * j + 1], mod[:, 3 * j + 2]
    y = fn(x * (1.0 + scale) + shift)
    return layer_norm(ALPHA * x + res_w * (1.0 + gate) * y, g, b)


def setup_inputs(seed: int = 0) -> dict:
    key = jax.random.key(seed)
    ks = iter(jax.random.split(key, 32))

    def nrm(shape, scale):
        return jax.random.normal(next(ks), shape, jnp.float32) * scale

    n_of = [len(range(m, DEPTH, N_MIXERS)) for m in range(N_MIXERS)]
    na, nf, nw, nd = n_of
    d = D_MODEL
    return {
        'x': nrm((BATCH, SEQ, d), 1.0),
        'c': nrm((BATCH, d), 1.0),
        'ln_g': 1.0 + nrm((DEPTH, 3, d), 0.05),
        'ln_b': nrm((DEPTH, 3, d), 0.02),
        'w_ada': nrm((DEPTH, d, N_ADA * d), 0.2 * d ** -0.5),
        'b_ada': nrm((DEPTH, N_ADA * d), 0.02),
        'w_ffn_in': nrm((DEPTH, 2, d, 2 * D_FF), d ** -0.5),
        'w_ffn_out': nrm((DEPTH, 2, D_FF, d), BETA * D_FF ** -0.5),
        'dsa_w_in': nrm((na, d, A_IN), d ** -0.5),
        'dsa_kv_norm': 1.0 + nrm((na, A_KV_RANK), 0.05),
        'dsa_w_kv_up': nrm((na, A_KV_RANK, 2 * HEAD_DIM), A_KV_RANK ** -0.5),
        'dsa_w_out': nrm((na, d, d), BETA * d ** -0.5),
        'fox_w_in': nrm((nf, d, B_IN), d ** -0.5),
        'fox_f_bias': jax.random.uniform(next(ks), (nf, N_HEADS), jnp.float32, 1.0, 5.0),
        'fox_w_out': nrm((nf, d, d), BETA * d ** -0.5),
        'swa_w_in': nrm((nw, d, C_IN), d ** -0.5),
        'swa_sinks': nrm((nw, N_HEADS), 0.5),
        'swa_w_out': nrm((nw, d, d), BETA * d ** -0.5),
        'diff_w_in': nrm((nd, d, D_IN), d ** -0.5),
        'diff_lambda': nrm((nd, 4, D_QK_DIM), 0.1),
        'diff_subln': 1.0 + nrm((nd, D_V_DIM), 0.05),
        'diff_w_out': nrm((nd, D_HEADS * D_V_DIM, d), BETA * (D_HEADS * D_V_DIM) ** -0.5),
    }


def reference(x, c, ln_g, ln_b, w_ada, b_ada, w_ffn_in, w_ffn_out,
              dsa_w_in, dsa_kv_norm, dsa_w_kv_up, dsa_w_out,
              fox_w_in, fox_f_bias, fox_w_out,
              swa_w_in, swa_sinks, swa_w_out,
              diff_w_in, diff_lambda, diff_subln, diff_w_out):
    b, s, d = x.shape
    cos, sin = rope_tables(s, ROT_DIM)
    cos_i, sin_i = rope_tables(s, A_IDX_DIM // 4)
    cond = jax.nn.silu(c)
    for i in range(DEPTH):
        m, r = i % N_MIXERS, i // N_MIXERS
        mod = (cond @ w_ada[i] + b_ada[i]).reshape(b, N_ADA, 1, d)
        x = modulated_post_norm(x, mod, 0, lambda h: swiglu(h, w_ffn_in[i, 0], w_ffn_out[i, 0]),
                                0.5, ln_g[i, 0], ln_b[i, 0])
        if m == 0:
            mix = lambda h: mixer_dsa(h, dsa_w_in[r], dsa_kv_norm[r], dsa_w_kv_up[r], dsa_w_out[r],
                                      cos, sin, cos_i, sin_i)
        elif m == 1:
            mix = lambda h: mixer_fox(h, fox_w_in[r], fox_f_bias[r], fox_w_out[r])
        elif m == 2:
            mix = lambda h: mixer_swa(h, swa_w_in[r], swa_sinks[r], swa_w_out[r], cos, sin)
        else:
            lambda_init = 0.8 - 0.6 * math.exp(-0.3 * i)
            mix = lambda h: mixer_diff(h, diff_w_in[r], diff_lambda[r], diff_subln[r], diff_w_out[r],
                                       cos, sin, lambda_init)
        x = modulated_post_norm(x, mod, 1, mix, 1.0, ln_g[i, 1], ln_b[i, 1])
        x = modulated_post_norm(x, mod, 2, lambda h: swiglu(h, w_ffn_in[i, 1], w_ffn_out[i, 1]),
                                0.5, ln_g[i, 2], ln_b[i, 2])
    return x
```

```python
import math
from contextlib import ExitStack

import numpy as np
import ml_dtypes
import concourse.bass as bass
import concourse.mybir as mybir
from concourse.bass_utils import run_bass_kernel_spmd

F32 = mybir.dt.float32
BF16 = mybir.dt.bfloat16
ALU = mybir.AluOpType
AF = mybir.ActivationFunctionType
AX = mybir.AxisListType

D = 1024
DFF = 2816
NCH = 8
NHC = 22
DEPTH = 4
ALPHA = (2 * DEPTH) ** 0.25
LN_EPS = 1e-5 / (ALPHA * ALPHA)
RMS_EPS = 1e-6
NEG = -30000.0
NSEM_DMA = 8
ROPE_THETA = 500000.0


class Prog:
    def __init__(self, nc, stack):
        self.nc = nc
        self.eng = {"pe": nc.tensor, "act": nc.scalar, "dve": nc.vector,
                    "pool": nc.gpsimd, "sp": nc.sync}
        self.ops = []
        self.csem = {e: stack.enter_context(nc.semaphore("c_" + e)) for e in self.eng}
        self.dsem = {e: [stack.enter_context(nc.semaphore("d_%s%d" % (e, k))) for k in range(NSEM_DMA)]
                     for e in ("sp", "act", "pool")}
        self.ccount = {e: 0 for e in self.eng}
        self.dcount = {e: 0 for e in self.dsem}
        self.known = {e: {} for e in self.eng}
        self.n_ops = 0
        self.n_wait = 0

    def op(self, eng, fn, reads=(), writes=(), dma=False):
        self.ops.append((eng, fn, tuple(reads), tuple(writes), dma))

    class _Cap:
        def __init__(self, prog):
            self.prog = prog
            self.ops = []

        def __enter__(self):
            self.saved = self.prog.ops
            self.prog.ops = self.ops
            return self

        def __exit__(self, *a):
            self.prog.ops = self.saved

    def capture(self):
        return Prog._Cap(self)

    def splice(self, cap):
        self.ops.extend(cap.ops)

    def pipelined(self, units, depth):
        out = []
        n = len(units)
        for i in range(min(depth, n)):
            out.append(units[i][0])
        for i in range(n):
            if i + depth < n:
                out.append(units[i + depth][0])
            out.append(units[i][1])
        return out

    def merge(self, la, lb):
        ta = sum(getattr(c, "cost", 1.0) for c in la) or 1.0
        tb = sum(getattr(c, "cost", 1.0) for c in lb) or 1.0
        na, nb = len(la), len(lb)
        ia = ib = 0
        ca = cb = 0.0
        while ia < na or ib < nb:
            if ib < nb and (ia >= na or cb / tb <= ca / ta):
                self.splice(lb[ib])
                cb += getattr(lb[ib], "cost", 1.0)
                ib += 1
            else:
                self.splice(la[ia])
                ca += getattr(la[ia], "cost", 1.0)
                ia += 1

    def _wait(self, e, s, v):
        key = id(s)
        if self.known[e].get(key, 0) >= v:
            return
        self.eng[e].wait_ge(s, v)
        self.known[e][key] = v
        self.n_wait += 1

    def flush(self):
        ops = self.ops
        self.ops = []
        n = len(ops)
        if n == 0:
            return
        last_w = {}
        readers = {}
        need = [None] * n
        signal = [False] * n
        for j, (eng, fn, reads, writes, dma) in enumerate(ops):
            d = set()
            for b in reads:
                if b in last_w:
                    d.add(last_w[b])
            for b in writes:
                if b in last_w:
                    d.add(last_w[b])
                for r in readers.get(b, ()):
                    d.add(r)
            d.discard(j)
            lst = []
            for i in d:
                ei, dmai = ops[i][0], ops[i][4]
                if ei == "pe" and eng == "pe" and not dmai and not dma:
                    continue
                lst.append(i)
                signal[i] = True
            need[j] = lst
            for b in reads:
                readers.setdefault(b, []).append(j)
            for b in writes:
                last_w[b] = j
                readers[b] = []
        last_c = {}
        last_d = {}
        for j, (eng, fn, reads, writes, dma) in enumerate(ops):
            if fn is None:
                continue
            if dma:
                last_d.setdefault(eng, []).append(j)
            else:
                last_c[eng] = j
        for j in range(n):
            if ops[j][4] and ops[j][1] is not None:
                signal[j] = True
        bar = []
        for e, j in last_c.items():
            signal[j] = True
            bar.append(j)
        for e, lst in last_d.items():
            for j in lst[-NSEM_DMA:]:
                signal[j] = True
                bar.append(j)
        sig = [None] * n
        for j in range(n):
            if not signal[j]:
                continue
            e, dma = ops[j][0], ops[j][4]
            if dma:
                k = self.dcount[e]
                self.dcount[e] += 1
                sig[j] = (self.dsem[e][k % NSEM_DMA], 16 * (k // NSEM_DMA + 1), 16)
            else:
                self.ccount[e] += 1
                sig[j] = (self.csem[e], self.ccount[e], 1)
        for j in range(n):
            e, fn, reads, writes, dma = ops[j]
            best = {}
            for i in need[j]:
                s, v, _ = sig[i]
                if id(s) not in best or best[id(s)][1] < v:
                    best[id(s)] = (s, v)
            for s, v in best.values():
                self._wait(e, s, v)
            if fn is None:
                continue
            ins = fn()
            if sig[j] is not None:
                ins.then_inc(sig[j][0], sig[j][2])
        for e in self.eng:
            for j in bar:
                self._wait(e, sig[j][0], sig[j][1])
        self.n_ops += n


def _bf(a):
    return np.ascontiguousarray(a).astype(ml_dtypes.bfloat16)


class Ctx:
    pass


def build(S, cfg):
    NT = S // 128
    NB = S // 512
    nc = bass.Bass("TRN2", target_bir_lowering=False)
    g = Ctx()
    g.nc = nc
    g.S, g.NT, g.NB = S, NT, NB

    def din(name, shape, dt=F32):
        return nc.dram_tensor(name, list(shape), dt, kind="ExternalInput").ap()

    x_ext = din("x", [S, D])
    cT = din("cT", [128, NCH])
    ln_g = din("ln_g", [DEPTH, 3, D])
    ln_b = din("ln_b", [DEPTH, 3, D])
    w_ada = din("w_ada", [DEPTH, D, 9 * D])
    b_ada = din("b_ada", [DEPTH, 9 * D])
    w_ffn_in = din("w_ffn_in", [DEPTH, 2, D, 2 * DFF])
    w_ffn_out = din("w_ffn_out", [DEPTH, 2, DFF, D])
    ident_d = din("ident", [128, 128], BF16)
    identf_d = din("identf", [128, 128], F32)
    masks_d = din("masks", [128, 3, 128], BF16)
    fox_w_in = din("fox_w_in", [1, D, 3088])
    fox_f_bias = din("fox_f_bias", [1, 16])
    fox_w_out = din("fox_w_out", [1, D, D])
    swa_w_in = din("swa_w_in", [1, D, 1280])
    swa_w_sw = din("swa_w_sw", [1, D, 1152])
    swa_sinks = din("swa_sinks", [1, 16])
    swa_w_out = din("swa_w_out", [1, D, D])
    diff_w_in = din("diff_w_in", [1, D, 3072])
    diff_w_sw = din("diff_w_sw", [1, D, 2048])
    diff_lambda = din("diff_lambda", [1, 256])
    diff_subln = din("diff_subln", [1, 128])
    diff_w_out = din("diff_w_out", [1, D, D])
    dsa_w_in = din("dsa_w_in", [1, D, 1448])
    dsa_w_sw = din("dsa_w_sw", [1, D, 1312])
    dsa_kv_norm = din("dsa_kv_norm", [1, 128])
    dsa_w_kv_up = din("dsa_w_kv_up", [1, 128, 128])
    dsa_w_kv_sw = din("dsa_w_kv_sw", [1, 128, 64])
    dsa_w_out = din("dsa_w_out", [1, D, D])
    cosI_d = din("cosI", [128, S])
    sinI_d = din("sinI", [128, S])
    QI_d = nc.dram_tensor("QI_d", [256, S], BF16).ap()
    KI_d = nc.dram_tensor("KI_d", [32, S], BF16).ap()
    WI_d = nc.dram_tensor("WI_d", [S, 8], F32).ap()
    cosF_d = din("cosF", [128, S])
    sinF_d = din("sinF", [128, S])
    VW = 66
    VW2 = 130
    skind = "ExternalOutput" if cfg.get("debug") else "Internal"
    QT_d = nc.dram_tensor("QT_d", [D, S], BF16, kind=skind).ap()
    KT_d = nc.dram_tensor("KT_d", [D, S], BF16, kind=skind).ap()
    V_d = nc.dram_tensor("V_d", [S, 16 * VW], BF16, kind=skind).ap()
    attnT_d = nc.dram_tensor("attnT_d", [D, S], BF16, kind=skind).ap()
    cq3_d = nc.dram_tensor("cq3_d", [16, 3, S], BF16, kind=skind).ap()
    ck_d = nc.dram_tensor("ck_d", [S, 16], F32, kind=skind).ap()
    out_ext = nc.dram_tensor("out", [S, D], F32, kind="ExternalOutput").ap()
    xs = nc.dram_tensor("xs", [S, D], F32).ap()
    mod_d = nc.dram_tensor("mod_d", [DEPTH, 9 * D], F32).ap()

    with ExitStack() as top:
        p = Prog(nc, top)
        g.p = p

        uid = [0]

        def sbuf(st, name, shape, dt):
            uid[0] += 1
            return st.enter_context(nc.sbuf_tensor("%s_%d" % (name, uid[0]), list(shape), dt))

        class _View:
            def __init__(self, t, shape):
                self.t, self.shape = t, shape
                n = 1
                for d in shape[1:]:
                    n *= d
                self.n = n

            def _base(self):
                a = self.t[0:self.shape[0], 0:self.n]
                if len(self.shape) == 3:
                    a = a.rearrange("p (a b) -> p a b", b=self.shape[2])
                return a

            def __getitem__(self, key):
                return self._base()[key]

        def psum(st, name, shape, dt):
            uid[0] += 1
            per = 512 if dt == F32 else 1024
            t = st.enter_context(nc.psum_tensor("%s_%d" % (name, uid[0]), [128, per], dt))
            return _View(t, list(shape))

        ident = sbuf(top, "ident_sb", [128, 128], BF16)
        p.op("sp", lambda: nc.sync.dma_start(out=ident[:], in_=ident_d), writes=["ident"], dma=True)
        identf = sbuf(top, "identf_sb", [128, 128], F32)
        p.op("sp", lambda: nc.sync.dma_start(out=identf[:], in_=identf_d), writes=["identf"], dma=True)
        masks = sbuf(top, "masks_sb", [128, 3, 128], BF16)
        p.op("sp", lambda: nc.sync.dma_start(out=masks[:], in_=masks_d), writes=["masks"], dma=True)
        p.flush()

        def make_mod_chunks(ph, L):
            cond = sbuf(ph, "cond", [128, NCH], F32)
            condb = sbuf(ph, "condb", [128, NCH], BF16)
            CB = 1152
            NCB = 9 * D // CB
            wblk = [sbuf(ph, "wada%d" % i, [128, NCH, CB], BF16) for i in range(2)]
            brow = [sbuf(ph, "brow%d" % i, [1, CB], F32) for i in range(2)]
            mrow = [sbuf(ph, "mrow%d" % i, [1, CB], F32) for i in range(2)]
            pm = psum(ph, "pm", [1, 384], F32)
            p.op("sp", lambda: nc.sync.dma_start(out=cond[:], in_=cT), writes=["cond"], dma=True)
            p.op("act", lambda: nc.scalar.activation(out=condb[:], in_=cond[:], func=AF.Silu),
                 reads=["cond"], writes=["condb"])

            def loads(cb):
                wb, br = wblk[cb % 2], brow[cb % 2]
                src_ = w_ada[L, :, cb * CB:(cb + 1) * CB].rearrange("(k p) n -> p k n", p=128)
                p.op("pool", lambda: nc.gpsimd.dma_start(out=wb[:], in_=src_), writes=[("wblk", cb % 2)], dma=True)
                p.op("sp", lambda: nc.sync.dma_start(out=br[:], in_=b_ada[L:L + 1, cb * CB:(cb + 1) * CB]),
                     writes=[("brow", cb % 2)], dma=True)

            def chunk(cb):
                if cb == 0:
                    loads(0)
                    loads(1)
                    return
                cb -= 1
                wb, br, mr = wblk[cb % 2], brow[cb % 2], mrow[cb % 2]
                wk, bk, mk = ("wblk", cb % 2), ("brow", cb % 2), ("mrow", cb % 2)
                for sbi in range(CB // 384):
                    for k in range(NCH):
                        p.op("pe", lambda k=k, sbi=sbi: nc.tensor.matmul(
                            pm[:], condb[:, k:k + 1], wb[:, k, sbi * 384:(sbi + 1) * 384],
                            start=(k == 0), stop=(k == NCH - 1)),
                            reads=["condb", wk], writes=["pm"])
                    p.op("dve", lambda sbi=sbi: nc.vector.tensor_tensor(
                        out=mr[:, sbi * 384:(sbi + 1) * 384], in0=pm[:], in1=br[:, sbi * 384:(sbi + 1) * 384], op=ALU.add),
                        reads=["pm", bk], writes=[mk])
                p.op("sp", lambda: nc.sync.dma_start(out=mod_d[L:L + 1, cb * CB:(cb + 1) * CB], in_=mr[:]),
                     reads=[mk], writes=[("mod_d", L, cb)], dma=True)
                if cb + 2 < NCB:
                    loads(cb + 2)
            return [lambda cb=cb: chunk(cb) for cb in range(NCB + 1)]

        def phase_mod(layers):
            for L in layers:
                with ExitStack() as ph:
                    for ch in make_mod_chunks(ph, L):
                        ch()
                    p.flush()

        def load_in_vectors(ph, L, j):
            s1 = sbuf(ph, "s1", [128, NCH], F32)
            sh = sbuf(ph, "sh", [128, NCH], F32)
            o = 3 * j * D
            p.op("sp", lambda: nc.sync.dma_start(
                out=sh[:], in_=mod_d[L, o:o + D].rearrange("(c p) -> p c", p=128),
                allow_slow_non_contiguous=True), writes=["sh"], dma=True)
            p.op("sp", lambda: nc.sync.dma_start(
                out=s1[:], in_=mod_d[L, o + D:o + 2 * D].rearrange("(c p) -> p c", p=128),
                allow_slow_non_contiguous=True), writes=["s1"], dma=True)
            p.op("dve", lambda: nc.vector.tensor_scalar(out=s1[:], in0=s1[:], scalar1=1.0, scalar2=None,
                                                       op0=ALU.add), reads=["s1"], writes=["s1"])
            return s1, sh

        def load_out_vectors(ph, L, j, rw):
            G = sbuf(ph, "G", [128, D], F32)
            gb = sbuf(ph, "gb", [128, D], F32)
            bb = sbuf(ph, "bb", [128, D], F32)
            o = 3 * j * D
            p.op("sp", lambda: nc.sync.dma_start(
                out=G[:], in_=mod_d[L:L + 1, o + 2 * D:o + 3 * D].partition_broadcast(128)),
                writes=["G"], dma=True)
            p.op("sp", lambda: nc.sync.dma_start(
                out=gb[:], in_=ln_g[L, j:j + 1, :].partition_broadcast(128)), writes=["gb"], dma=True)
            p.op("sp", lambda: nc.sync.dma_start(
                out=bb[:], in_=ln_b[L, j:j + 1, :].partition_broadcast(128)), writes=["bb"], dma=True)
            p.op("dve", lambda: nc.vector.tensor_scalar(out=G[:], in0=G[:], scalar1=1.0, scalar2=rw / ALPHA,
                                                       op0=ALU.add, op1=ALU.mult), reads=["G"], writes=["G"])
            return G, gb, bb

        def load_mod_vectors(ph, L, j, rw):
            s1 = sbuf(ph, "s1", [128, NCH], F32)
            sh = sbuf(ph, "sh", [128, NCH], F32)
            G = sbuf(ph, "G", [128, D], F32)
            gb = sbuf(ph, "gb", [128, D], F32)
            bb = sbuf(ph, "bb", [128, D], F32)
            o = 3 * j * D
            p.op("sp", lambda: nc.sync.dma_start(
                out=sh[:], in_=mod_d[L, o:o + D].rearrange("(c p) -> p c", p=128),
                allow_slow_non_contiguous=True), writes=["sh"], dma=True)
            p.op("sp", lambda: nc.sync.dma_start(
                out=s1[:], in_=mod_d[L, o + D:o + 2 * D].rearrange("(c p) -> p c", p=128),
                allow_slow_non_contiguous=True), writes=["s1"], dma=True)
            p.op("sp", lambda: nc.sync.dma_start(
                out=G[:], in_=mod_d[L:L + 1, o + 2 * D:o + 3 * D].partition_broadcast(128)),
                writes=["G"], dma=True)
            p.op("sp", lambda: nc.sync.dma_start(
                out=gb[:], in_=ln_g[L, j:j + 1, :].partition_broadcast(128)), writes=["gb"], dma=True)
            p.op("sp", lambda: nc.sync.dma_start(
                out=bb[:], in_=ln_b[L, j:j + 1, :].partition_broadcast(128)), writes=["bb"], dma=True)
            p.op("dve", lambda: nc.vector.tensor_scalar(out=s1[:], in0=s1[:], scalar1=1.0, scalar2=None,
                                                       op0=ALU.add), reads=["s1"], writes=["s1"])
            p.op("dve", lambda: nc.vector.tensor_scalar(out=G[:], in0=G[:], scalar1=1.0, scalar2=rw / ALPHA,
                                                       op0=ALU.add, op1=ALU.mult), reads=["G"], writes=["G"])
            return s1, sh, G, gb, bb

        def make_epilogue(ph, G, gb, bb, src, dst):
            xres = [sbuf(ph, "xres%d" % i, [128, D], F32) for i in range(2)]
            tmpA = [sbuf(ph, "tmpA%d" % i, [128, D], F32) for i in range(2)]
            st6 = [sbuf(ph, "st6_%d" % i, [128, 2, 6], F32) for i in range(2)]
            mv = [sbuf(ph, "mv%d" % i, [128, 2], F32) for i in range(2)]
            sm = [sbuf(ph, "sm%d" % i, [128, 4], F32) for i in range(2)]
            nh = sbuf(ph, "neghalf", [128, 1], F32)
            p.op("pool", lambda: nc.gpsimd.memset(nh[:], -0.5), writes=["neghalf"])
            cnt = [0]
            pending = [None]

            def stage_a(t, po_list, par):
                xr, ta, s6, m, s = xres[par], tmpA[par], st6[par], mv[par], sm[par]
                kx, kt, ks = ("xres", par), ("tmpA", par), ("stat", par)
                p.op("sp", lambda: nc.sync.dma_start(out=xr[:], in_=src[t * 128:(t + 1) * 128, :]),
                     reads=[("x", t)], writes=[kx], dma=True)
                for (pa, pk, c0, w) in po_list:
                    p.op("dve", lambda pa=pa, c0=c0, w=w: nc.vector.tensor_tensor(
                        out=ta[:, c0:c0 + w], in0=pa, in1=G[:, c0:c0 + w], op=ALU.mult),
                        reads=[pk, "G"], writes=[kt])
                p.op("dve", lambda: nc.vector.tensor_tensor(out=ta[:], in0=ta[:], in1=xr[:], op=ALU.add),
                     reads=[kt, kx], writes=[kt])
                for hh in range(2):
                    p.op("dve", lambda hh=hh: nc.vector.bn_stats(out=s6[:, hh, :], in_=ta[:, hh * 512:(hh + 1) * 512]),
                         reads=[kt], writes=[ks])
                p.op("dve", lambda: nc.vector.bn_aggr(out=m[:], in_=s6[:].rearrange("p a b -> p (a b)")),
                     reads=[ks], writes=[ks])
                p.op("pool", lambda: nc.gpsimd.tensor_scalar(out=s[:, 0:1], in0=m[:, 1:2], scalar1=LN_EPS,
                                                            scalar2=None, op0=ALU.add),
                     reads=[ks], writes=[ks])
                p.op("pool", lambda: nc.gpsimd.tensor_tensor(out=s[:, 1:2], in0=s[:, 0:1], in1=nh[:], op=ALU.pow),
                     reads=[ks, "neghalf"], writes=[ks])
                p.op("pool", lambda: nc.gpsimd.tensor_scalar(out=s[:, 2:3], in0=m[:, 0:1], scalar1=-1.0,
                                                            scalar2=s[:, 1:2], op0=ALU.mult, op1=ALU.mult),
                     reads=[ks], writes=[ks])

            def stage_b(t, par):
                xr, ta, s = xres[par], tmpA[par], sm[par]
                kx, kt, ks = ("xres", par), ("tmpA", par), ("stat", par)
                p.op("act", lambda: nc.scalar.activation(out=xr[:], in_=ta[:], func=AF.Identity,
                                                         bias=s[:, 2:3], scale=s[:, 1:2]),
                     reads=[kt, ks, kx], writes=[kx])
                p.op("dve", lambda: nc.vector.tensor_tensor(out=xr[:], in0=xr[:], in1=gb[:], op=ALU.mult),
                     reads=[kx, "gb"], writes=[kx])
                p.op("pool", lambda: nc.gpsimd.tensor_tensor(out=xr[:], in0=xr[:], in1=bb[:], op=ALU.add),
                     reads=[kx, "bb"], writes=[kx])
                p.op("sp", lambda: nc.sync.dma_start(out=dst[t * 128:(t + 1) * 128, :], in_=xr[:]),
                     reads=[kx], writes=[("x", t), ("xo", t)], dma=True)

            def epi(t, po_list):
                par = cnt[0] % 2
                cnt[0] += 1
                stage_a(t, po_list, par)
                if pending[0] is not None:
                    stage_b(*pending[0])
                pending[0] = (t, par)

            def finish():
                if pending[0] is not None:
                    stage_b(*pending[0])
                    pending[0] = None
            epi.finish = finish
            return epi

        def make_hT_builder(ph, s1, sh, src, width):
            nt = width // 128
            xb = sbuf(ph, "xb", [128, nt, D], BF16)
            hT = sbuf(ph, "hT", [128, NCH, width], BF16)
            pT = [psum(ph, "pT%d" % i, [128, width], BF16) for i in range(2)]

            pre = set()

            def loads(tb):
                for t in range(nt):
                    tt = tb * nt + t
                    p.op("pool", lambda t=t, tt=tt: nc.gpsimd.dma_start(out=xb[:, t, :], in_=src[tt * 128:(tt + 1) * 128, :]),
                         reads=[("x", tt)], writes=[("xb", t)], dma=True)

            def preload(tb):
                pre.add(tb)
                loads(tb)

            def mk(tb):
                if tb in pre:
                    pre.discard(tb)
                else:
                    loads(tb)
                for c in range(NCH):
                    pt = pT[c % 2]
                    pk = ("pT", c % 2)
                    for t in range(nt):
                        p.op("pe", lambda c=c, t=t, pt=pt: nc.tensor.transpose(
                            pt[:, t * 128:(t + 1) * 128], xb[:, t, c * 128:(c + 1) * 128], ident[:]),
                            reads=[("xb", t), "ident"], writes=[pk])
                    p.op("act", lambda c=c, pt=pt: nc.scalar.activation(
                        out=hT[:, c, :], in_=pt[:], func=AF.Identity, bias=sh[:, c:c + 1], scale=s1[:, c:c + 1]),
                        reads=[pk, "s1", "sh"], writes=[("hT", c)])
            mk.preload = preload
            return hT, mk

        def phase_ffn(L, j, src, dst):
            fi = 0 if j == 0 else 1
            with ExitStack() as ph:
                win = sbuf(ph, "win", [128, NCH, 2 * DFF], BF16)
                wout = sbuf(ph, "wout", [128, NHC, D], BF16)
                act = sbuf(ph, "act", [128, NHC, 512], BF16)
                sg = [sbuf(ph, "sg%d" % i, [128, 512], BF16) for i in range(2)]
                s1, sh, G, gb, bb = load_mod_vectors(ph, L, j, 0.5)
                hT, mk_hT = make_hT_builder(ph, s1, sh, src, 512)
                mk_hT.preload(0)
                epi = make_epilogue(ph, G, gb, bb, src, dst)
                pg = [psum(ph, "pg%d" % i, [128, 512], F32) for i in range(2)]
                pu = [psum(ph, "pu%d" % i, [128, 512], F32) for i in range(2)]
                po = [psum(ph, "po%d" % i, [128, 512], F32) for i in range(2)]
                GRP = [(0, 2), (2, 6), (6, 14), (14, 22)]
                grp_of = {}
                for gi, (ma, mb) in enumerate(GRP):
                    for m in range(ma, mb):
                        grp_of[m] = gi
                    for gu in range(2):
                        c0 = gu * DFF + ma * 128
                        wd_ = (mb - ma) * 128
                        for k in range(NCH):
                            p.op("pool", lambda c0=c0, k=k, wd_=wd_: nc.gpsimd.dma_start(
                                out=win[:, k, c0:c0 + wd_], in_=w_ffn_in[L, fi, k * 128:(k + 1) * 128, c0:c0 + wd_]),
                                writes=[("win", gi)], dma=True)
                for m0 in range(0, NHC, 2):
                    p.op("pool", lambda m0=m0: nc.gpsimd.dma_start(
                        out=wout[:, m0:m0 + 2, :],
                        in_=w_ffn_out[L, fi, m0 * 128:(m0 + 2) * 128, :].rearrange("(m p) n -> p m n", p=128)),
                        writes=[("wout", m0 // 2)], dma=True)
                mk_hT(0)
                for tb in range(NB):
                    for m in range(NHC):
                        grp = grp_of[m]
                        a, b = pg[m % 2], pu[m % 2]
                        ka, kb = ("pg", m % 2), ("pu", m % 2)
                        for k in range(NCH):
                            p.op("pe", lambda a=a, k=k, m=m: nc.tensor.matmul(
                                a[:], win[:, k, m * 128:(m + 1) * 128], hT[:, k, :],
                                start=(k == 0), stop=(k == NCH - 1)),
                                reads=[("win", grp), ("hT", k)], writes=[ka])
                        for k in range(NCH):
                            p.op("pe", lambda b=b, k=k, m=m: nc.tensor.matmul(
                                b[:], win[:, k, DFF + m * 128:DFF + (m + 1) * 128], hT[:, k, :],
                                start=(k == 0), stop=(k == NCH - 1)),
                                reads=[("win", grp), ("hT", k)], writes=[kb])
                        s = sg[m % 2]
                        p.op("act", lambda a=a, s=s: nc.scalar.activation(out=s[:], in_=a[:], func=AF.Silu),
                             reads=[ka], writes=[("sg", m % 2)])
                        p.op("dve", lambda b=b, s=s, m=m: nc.vector.tensor_tensor(
                            out=act[:, m, :], in0=b[:], in1=s[:], op=ALU.mult),
                            reads=[kb, ("sg", m % 2)], writes=[("act", m)])
                    if tb + 1 < NB:
                        mk_hT(tb + 1)
                    for t in range(4):
                        tt = tb * 4 + t
                        for hh in range(2):
                            for m in range(NHC):
                                p.op("pe", lambda hh=hh, m=m, t=t: nc.tensor.matmul(
                                    po[hh][:], act[:, m, t * 128:(t + 1) * 128], wout[:, m, hh * 512:(hh + 1) * 512],
                                    start=(m == 0), stop=(m == NHC - 1)),
                                    reads=[("act", m), ("wout", m // 2)], writes=[("po", hh)])
                        epi(tt, [(po[0][:], ("po", 0), 0, 512), (po[1][:], ("po", 1), 512, 512)])
                epi.finish()
                p.flush()


        def phase_outproj(L, w_out_d, src, dst, mod_next=None):
            with ExitStack() as ph:
                mod_ch = make_mod_chunks(ph, mod_next) if mod_next is not None else []
                per_blk = (len(mod_ch) + NB - 1) // NB
                wo = sbuf(ph, "wo", [128, NCH, D], BF16)
                for c0 in range(0, NCH, 2):
                    p.op("pool", lambda c0=c0: nc.gpsimd.dma_start(
                        out=wo[:, c0:c0 + 2, :], in_=w_out_d[c0 * 128:(c0 + 2) * 128, :].rearrange("(c p) n -> p c n", p=128)),
                        writes=[("wo", c0 // 2)], dma=True)
                G, gb, bb = load_out_vectors(ph, L, 1, 1.0)
                epi = make_epilogue(ph, G, gb, bb, src, dst)
                aT = [sbuf(ph, "aT%d" % i, [128, NCH, 512], BF16) for i in range(2)]
                po = [psum(ph, "po%d" % i, [128, 512], F32) for i in range(4)]
                for tb in range(NB):
                    a = aT[tb % 2]
                    ka = ("aT", tb % 2)
                    p.op("sp", lambda a=a, tb=tb: nc.sync.dma_start(
                        out=a[:], in_=attnT_d[:, tb * 512:(tb + 1) * 512].rearrange("(c p) s -> p c s", p=128)),
                        writes=[ka], dma=True)
                    for t in range(4):
                        tt = tb * 4 + t
                        pl = []
                        for hh in range(2):
                            pi = (tt % 2) * 2 + hh
                            for c in range(NCH):
                                p.op("pe", lambda pi=pi, c=c, t=t, a=a, hh=hh: nc.tensor.matmul(
                                    po[pi][:], a[:, c, t * 128:(t + 1) * 128], wo[:, c, hh * 512:(hh + 1) * 512],
                                    start=(c == 0), stop=(c == NCH - 1)),
                                    reads=[ka, ("wo", c // 2)], writes=[("po", pi)])
                            pl.append((po[pi][:], ("po", pi), hh * 512, 512))
                        epi(tt, pl)
                    for _ in range(per_blk):
                        if mod_ch:
                            mod_ch.pop(0)()
                while mod_ch:
                    mod_ch.pop(0)()
                epi.finish()
                p.flush()

        def phase_fox_a(L, r, src):
            with ExitStack() as ph:
                w = sbuf(ph, "wfox", [128, NCH, 3088], BF16)
                for grp, (a, b) in enumerate([(0, 1024), (1024, 2048), (2048, 3072), (3072, 3088)]):
                    for k in range(NCH):
                        p.op("pool", lambda a=a, b=b, k=k: nc.gpsimd.dma_start(
                            out=w[:, k, a:b], in_=fox_w_in[r, k * 128:(k + 1) * 128, a:b]),
                            writes=[("w", grp)], dma=True)
                s1, sh = load_in_vectors(ph, L, 1)
                hT, mk_hT = make_hT_builder(ph, s1, sh, src, 512)
                qst = [sbuf(ph, "qst%d" % i, [128, 512], BF16) for i in range(4)]
                vst = [sbuf(ph, "vst%d" % i, [128, 16, VW], BF16) for i in range(2)]
                for i in range(2):
                    p.op("pool", lambda i=i: nc.gpsimd.memset(vst[i][:, :, 64:VW], 1.0), writes=[("vst", i)])
                fb = sbuf(ph, "fb", [16, 1], F32)
                p.op("sp", lambda: nc.sync.dma_start(out=fb[:], in_=fox_f_bias[r, :].rearrange("(h o) -> h o", o=1)),
                     writes=["fb"], dma=True)
                p.op("dve", lambda: nc.vector.tensor_scalar(out=fb[:], in0=fb[:], scalar1=-1.0, scalar2=None,
                                                           op0=ALU.mult), reads=["fb"], writes=["fb"])
                ones16 = sbuf(ph, "ones16", [16, 512], F32)
                p.op("pool", lambda: nc.gpsimd.memset(ones16[:], 1.0), writes=["ones16"])
                cumneg = sbuf(ph, "cumneg", [16, S], F32)
                ef = sbuf(ph, "ef", [16, 512], F32)
                r1 = sbuf(ph, "r1", [16, 512], F32)
                r2 = sbuf(ph, "r2", [16, 512], F32)
                c3 = sbuf(ph, "c3", [16, 3, 512], BF16)
                ckst = sbuf(ph, "ckst", [128, NT, 16], F32)
                pq = [psum(ph, "pq%d" % i, [128, 512], F32) for i in range(4)]
                pf = psum(ph, "pf", [16, 512], F32)
                ptr = psum(ph, "ptr", [128, 16], F32)
                cnt = [0]

                def proj_fm(col0, grp, c):
                    i = cnt[0] % 4
                    cnt[0] += 1
                    for k in range(NCH):
                        p.op("pe", lambda i=i, k=k: nc.tensor.matmul(
                            pq[i][:], w[:, k, col0:col0 + 128], hT[:, k, :], start=(k == 0), stop=(k == NCH - 1)),
                            reads=[("w", grp), ("hT", k)], writes=[("pq", i)])
                    return i

                for tb in range(NB):
                    mk_hT(tb)
                    for which, (base, dram) in enumerate([(0, QT_d), (1024, KT_d)]):
                        for c in range(NCH):
                            i = proj_fm(base + c * 128, which, c)
                            if which == 0:
                                p.op("act", lambda i=i: nc.scalar.copy(out=qst[i][:], in_=pq[i][:]),
                                     reads=[("pq", i)], writes=[("qst", i)])
                            else:
                                p.op("dve", lambda i=i: nc.vector.tensor_copy(out=qst[i][:], in_=pq[i][:]),
                                     reads=[("pq", i)], writes=[("qst", i)])
                            p.op("sp", lambda i=i, c=c, dram=dram, tb=tb: nc.sync.dma_start(
                                out=dram[c * 128:(c + 1) * 128, tb * 512:(tb + 1) * 512], in_=qst[i][:]),
                                reads=[("qst", i)], writes=[("qkd", which, c, tb)], dma=True)
                    for t in range(4):
                        tt = tb * 4 + t
                        vs = vst[tt % 2]
                        for hh in range(2):
                            i = cnt[0] % 4
                            cnt[0] += 1
                            for k in range(NCH):
                                p.op("pe", lambda i=i, k=k, t=t, hh=hh: nc.tensor.matmul(
                                    pq[i][:], hT[:, k, t * 128:(t + 1) * 128], w[:, k, 2048 + hh * 512:2048 + (hh + 1) * 512],
                                    start=(k == 0), stop=(k == NCH - 1)),
                                    reads=[("w", 2), ("hT", k)], writes=[("pq", i)])
                            eng = "dve" if hh == 0 else "act"
                            if hh == 0:
                                p.op("dve", lambda i=i, vs=vs, hh=hh: nc.vector.tensor_copy(
                                    out=vs[:, hh * 8:(hh + 1) * 8, 0:64], in_=pq[i][:].rearrange("p (h d) -> p h d", d=64)),
                                    reads=[("pq", i)], writes=[("vst", tt % 2)])
                            else:
                                p.op("act", lambda i=i, vs=vs, hh=hh: nc.scalar.copy(
                                    out=vs[:, hh * 8:(hh + 1) * 8, 0:64], in_=pq[i][:].rearrange("p (h d) -> p h d", d=64)),
                                    reads=[("pq", i)], writes=[("vst", tt % 2)])
                        p.op("sp", lambda vs=vs, tt=tt: nc.sync.dma_start(
                            out=V_d[tt * 128:(tt + 1) * 128, :], in_=vs[:].rearrange("p h d -> p (h d)")),
                            reads=[("vst", tt % 2)], writes=[("vd", tt)], dma=True)
                    for k in range(NCH):
                        p.op("pe", lambda k=k: nc.tensor.matmul(
                            pf[:], w[:, k, 3072:3088], hT[:, k, :], start=(k == 0), stop=(k == NCH - 1)),
                            reads=[("w", 3), ("hT", k)], writes=["pf"])
                    p.op("act", lambda: nc.scalar.activation(out=ef[:], in_=pf[:], func=AF.Exp, bias=fb[:], scale=-1.0),
                         reads=["pf", "fb"], writes=["ef"])
                    p.op("act", lambda: nc.scalar.activation(out=ef[:], in_=ef[:], func=AF.Ln, bias=1.0, scale=1.0),
                         reads=["ef"], writes=["ef"])
                    blk = slice(tb * 512, (tb + 1) * 512)
                    init = 0.0 if tb == 0 else cumneg[:, tb * 512 - 1:tb * 512]
                    p.op("dve", lambda blk=blk, init=init: nc.vector.tensor_tensor_scan(
                        out=cumneg[:, blk], data0=ones16[:], data1=ef[:], initial=init, op0=ALU.mult, op1=ALU.add),
                        reads=["ef", "ones16", "cumneg"], writes=["cumneg"])
                    p.op("dve", lambda blk=blk: nc.vector.tensor_scalar(out=r1[:], in0=cumneg[:, blk], scalar1=-8.0,
                                                                      scalar2=None, op0=ALU.mult),
                         reads=["cumneg"], writes=["r1"])
                    p.op("dve", lambda: nc.vector.tensor_copy(out=c3[:, 0, :], in_=r1[:]), reads=["r1"], writes=["c3"])
                    p.op("dve", lambda: nc.vector.tensor_tensor(out=r2[:], in0=r1[:], in1=c3[:, 0, :], op=ALU.subtract),
                         reads=["r1", "c3"], writes=["r2"])
                    p.op("dve", lambda: nc.vector.tensor_copy(out=c3[:, 1, :], in_=r2[:]), reads=["r2"], writes=["c3"])
                    p.op("dve", lambda: nc.vector.tensor_tensor(out=r1[:], in0=r2[:], in1=c3[:, 1, :], op=ALU.subtract),
                         reads=["r2", "c3"], writes=["r1"])
                    p.op("dve", lambda: nc.vector.tensor_copy(out=c3[:, 2, :], in_=r1[:]), reads=["r1"], writes=["c3"])
                    p.op("sp", lambda blk=blk: nc.sync.dma_start(out=cq3_d[:, :, blk], in_=c3[:]),
                         reads=["c3"], writes=[("cq3d", tb)], dma=True)
                    for t in range(4):
                        tt = tb * 4 + t
                        p.op("pe", lambda tt=tt: nc.tensor.transpose(
                            ptr[:], cumneg[:, tt * 128:(tt + 1) * 128], identf[0:16, 0:16]),
                            reads=["cumneg", "identf"], writes=["ptr"])
                        p.op("act", lambda tt=tt: nc.scalar.copy(out=ckst[:, tt, :], in_=ptr[:]),
                             reads=["ptr"], writes=["ckst"])
                p.op("sp", lambda: nc.sync.dma_start(out=ck_d.rearrange("(t p) h -> p t h", p=128), in_=ckst[:]),
                     reads=["ckst"], writes=["ckd"], dma=True)
                p.flush()

        def phase_fox_b(L, r):
            with ExitStack() as ph:
                Vall = sbuf(ph, "Vall", [128, NT, 16 * VW], BF16)
                ckT = sbuf(ph, "ckT", [128, NT, 16], F32)
                Qx = [sbuf(ph, "Qx%d" % i, [67, S], BF16) for i in range(2)]
                Kx = [sbuf(ph, "Kx%d" % i, [67, S], BF16) for i in range(2)]
                pt = [sbuf(ph, "pt%d" % i, [128, 512], BF16) for i in range(3)]
                apair = [sbuf(ph, "apair%d" % i, [128, NT, 128], BF16) for i in range(2)]
                rec = [sbuf(ph, "rec%d" % i, [128, 4], F32) for i in range(2)]
                tst = [sbuf(ph, "tst%d" % i, [128, 512], BF16) for i in range(2)]
                ps = [psum(ph, "ps%d" % i, [128, 512], F32) for i in range(3)]
                pacc = [psum(ph, "pacc%d" % i, [128, 4, 65], F32) for i in range(2)]
                ptr = psum(ph, "ptrb", [128, 512], BF16)
                for t0 in range(0, NT, 4):
                    p.op("sp", lambda t0=t0: nc.sync.dma_start(
                        out=Vall[:, t0:t0 + 4, :], in_=V_d[t0 * 128:(t0 + 4) * 128, :].rearrange("(t p) f -> p t f", p=128)),
                        writes=["Vall"], dma=True)
                p.op("sp", lambda: nc.sync.dma_start(out=ckT[:], in_=ck_d.rearrange("(t p) h -> p t h", p=128)),
                     writes=["ckT"], dma=True)
                for i in range(2):
                    p.op("pool", lambda i=i: nc.gpsimd.memset(Kx[i][64:67, :], 1.0), writes=[("Kx1", i)])
                cnt = 0
                fin = 0
                units = []

                def fox_loads(h):
                    par = h % 2
                    q_, k_ = Qx[par], Kx[par]
                    p.op("sp", lambda: nc.sync.dma_start(out=q_[0:64, :], in_=QT_d[h * 64:(h + 1) * 64, :]),
                         writes=[("Qx", par)], dma=True)
                    p.op("sp", lambda: nc.sync.dma_start(out=q_[64:67, :], in_=cq3_d[h]),
                         writes=[("Qx", par)], dma=True)
                    p.op("sp", lambda: nc.sync.dma_start(out=k_[0:64, :], in_=KT_d[h * 64:(h + 1) * 64, :]),
                         writes=[("Kx", par)], dma=True)

                fox_loads(0)
                for h in range(16):
                    par = h % 2
                    hp = h // 2
                    q_, k_ = Qx[par], Kx[par]
                    rq = [("Qx", par), ("Kx", par), ("Kx1", par)]
                    for qb in range(NB):
                        pa = pacc[fin % 2]
                        kpa = ("pacc", fin % 2)
                        rc = rec[fin % 2]
                        krc = ("rec", fin % 2)
                        fin += 1
                        for j in range(4 * qb + 4):
                            c0 = max(0, j - 4 * qb) * 128
                            i = cnt % 3
                            cnt += 1
                            psi, pti = ps[i], pt[i]
                            kps, kpt = ("ps", i), ("pt", i)
                            ks = slice(j * 128, (j + 1) * 128)
                            q0 = qb * 512
                            with p.capture() as front:
                                if qb == 0 and j == 0 and h + 1 < 16:
                                    fox_loads(h + 1)
                                if j >= 4 * qb:
                                    p.op("pe", lambda psi=psi, k_=k_, q_=q_, ks=ks, c0=c0, q0=q0: nc.tensor.matmul(
                                        psi[:, c0:c0 + 128], k_[:, ks], q_[:, q0 + c0:q0 + c0 + 128], start=True, stop=False,
                                        skip_group_check=True), reads=rq, writes=[kps])
                                    p.op("pe", lambda psi=psi, c0=c0: nc.tensor.matmul(
                                        psi[:, c0:c0 + 128], ident[:], masks[:, 0, :], start=False, stop=True,
                                        skip_group_check=True), reads=["ident", "masks"], writes=[kps])
                                    if c0 + 128 < 512:
                                        p.op("pe", lambda psi=psi, k_=k_, q_=q_, ks=ks, c0=c0, q0=q0: nc.tensor.matmul(
                                            psi[:, c0 + 128:512], k_[:, ks], q_[:, q0 + c0 + 128:q0 + 512], start=False, stop=True,
                                            skip_group_check=True), reads=rq, writes=[kps])
                                else:
                                    p.op("pe", lambda psi=psi, k_=k_, q_=q_, ks=ks, q0=q0: nc.tensor.matmul(
                                        psi[:, :], k_[:, ks], q_[:, q0:q0 + 512], start=True, stop=True),
                                        reads=rq, writes=[kps])
                            with p.capture() as back:
                                p.op("act", lambda psi=psi, pti=pti, c0=c0, j=j, h=h: nc.scalar.activation(
                                    out=pti[:, c0:512], in_=psi[:, c0:512], func=AF.Exp, bias=ckT[:, j, h:h + 1], scale=0.125),
                                    reads=[kps, "ckT"], writes=[kpt])
                                for t in range(c0 // 128, 4):
                                    p.op("pe", lambda pa=pa, pti=pti, t=t, j=j, h=h, qb=qb: nc.tensor.matmul(
                                        pa[:, t, :], pti[:, t * 128:(t + 1) * 128], Vall[:, j, h * VW:h * VW + 65],
                                        start=(j == 0 and t == 0), stop=(j == 4 * qb + t), skip_group_check=True),
                                        reads=[kpt, "Vall"], writes=[kpa])
                                if j == 4 * qb + 3:
                                    ap_ = apair[hp % 2]
                                    kap = ("apair", hp % 2)
                                    p.op("dve", lambda rc=rc, pa=pa: nc.vector.reciprocal(out=rc[:], in_=pa[:, :, 64]),
                                         reads=[kpa], writes=[krc])
                                    for t in range(4):
                                        p.op("dve", lambda ap_=ap_, pa=pa, rc=rc, t=t, qb=qb, par=par: nc.vector.tensor_scalar(
                                            out=ap_[:, qb * 4 + t, par * 64:(par + 1) * 64], in0=pa[:, t, 0:64],
                                            scalar1=rc[:, t:t + 1], scalar2=None, op0=ALU.mult),
                                            reads=[kpa, krc], writes=[kap])
                                    if par == 1 and qb == NB - 1:
                                        emit_pair_transposes(ap_, kap, ptr, tst, hp * 128)
                            units.append((front, back))
                for cap in p.pipelined(units, 2):
                    p.splice(cap)
                p.flush()

        def mixer_fox(L, r, src, dst, mod_next=None):
            phase_fox_a(L, r, src)
            phase_fox_b(L, r)
            phase_outproj(L, fox_w_out[r], src, dst, mod_next)


        def make_rope_proj(ph, w, wsw, hT, pq, wkey, wskey):
            t1 = [sbuf(ph, "rt1_%d" % i, [128, 512], F32) for i in range(2)]
            t2 = [sbuf(ph, "rt2_%d" % i, [128, 512], F32) for i in range(2)]
            cnt = [0]

            def fn(col0, col0s, cs, sn, cskeys, out_ap, out_key, M=128):
                i = cnt[0] % 2
                cnt[0] += 1
                pa, pb = pq[2 * i], pq[2 * i + 1]
                ka, kb = ("pq", 2 * i), ("pq", 2 * i + 1)
                for k in range(NCH):
                    p.op("pe", lambda k=k: nc.tensor.matmul(pa[0:M, :], w[:, k, col0:col0 + M], hT[:, k, :],
                                                             start=(k == 0), stop=(k == NCH - 1)),
                         reads=[wkey, ("hT", k)], writes=[ka])
                for k in range(NCH):
                    p.op("pe", lambda k=k: nc.tensor.matmul(pb[0:M, :], wsw[:, k, col0s:col0s + M], hT[:, k, :],
                                                             start=(k == 0), stop=(k == NCH - 1)),
                         reads=[wskey, ("hT", k)], writes=[kb])
                p.op("dve", lambda: nc.vector.tensor_tensor(out=t1[i][0:M, :], in0=pa[0:M, :], in1=cs, op=ALU.mult),
                     reads=[ka] + cskeys, writes=[("rt1", i)])
                p.op("dve", lambda: nc.vector.tensor_tensor(out=t2[i][0:M, :], in0=pb[0:M, :], in1=sn, op=ALU.mult),
                     reads=[kb] + cskeys, writes=[("rt2", i)])
                p.op("pool", lambda: nc.gpsimd.tensor_tensor(out=out_ap, in0=t1[i][0:M, :], in1=t2[i][0:M, :], op=ALU.add),
                     reads=[("rt1", i), ("rt2", i)], writes=[out_key])
            return fn

        def load_w(wt, dram, ncols, key, step=1024):
            for a in range(0, ncols, step):
                b = min(ncols, a + step)
                for k in range(NCH):
                    p.op("pool", lambda a=a, b=b, k=k: nc.gpsimd.dma_start(
                        out=wt[:, k, a:b], in_=dram[k * 128:(k + 1) * 128, a:b]), writes=[key], dma=True)

        def emit_pair_transposes(apair_t, kap, ptr, tst, row0, scale_ap=None, scale_key=None):
            for qb in range(NB):
                ts_ = tst[qb % 2]
                kts = ("tst", qb % 2)
                for t in range(4):
                    p.op("pe", lambda qb=qb, t=t: nc.tensor.transpose(
                        ptr[:, t * 128:(t + 1) * 128], apair_t[:, qb * 4 + t, :], ident[:]),
                        reads=[kap, "ident"], writes=["ptrb"])
                if scale_ap is None:
                    p.op("dve", lambda ts_=ts_: nc.vector.tensor_copy(out=ts_[:], in_=ptr[:]),
                         reads=["ptrb"], writes=[kts])
                else:
                    p.op("act", lambda ts_=ts_: nc.scalar.activation(out=ts_[:], in_=ptr[:], func=AF.Copy, scale=scale_ap),
                         reads=["ptrb", scale_key], writes=[kts])
                p.op("sp", lambda ts_=ts_, qb=qb: nc.sync.dma_start(
                    out=attnT_d[row0:row0 + 128, qb * 512:(qb + 1) * 512], in_=ts_[:]),
                    reads=[kts], writes=[("attnT", row0, qb)], dma=True)

        def phase_swa_a(L, r, src):
            with ExitStack() as ph:
                w = sbuf(ph, "wswa", [128, NCH, 1280], BF16)
                wsw = sbuf(ph, "wswas", [128, NCH, 1152], BF16)
                load_w(w, swa_w_in[r], 1280, "w")
                load_w(wsw, swa_w_sw[r], 1152, "wsw")
                s1, sh = load_in_vectors(ph, L, 1)
                hT, mk_hT = make_hT_builder(ph, s1, sh, src, 512)
                qst = [sbuf(ph, "qst%d" % i, [128, 512], BF16) for i in range(4)]
                vst = [sbuf(ph, "vst%d" % i, [128, 2, VW], BF16) for i in range(2)]
                for i in range(2):
                    p.op("pool", lambda i=i: nc.gpsimd.memset(vst[i][:, :, 64:VW], 1.0), writes=[("vst", i)])
                cs = [sbuf(ph, "cs%d" % i, [128, 512], F32) for i in range(2)]
                sn = [sbuf(ph, "sn%d" % i, [128, 512], F32) for i in range(2)]
                pq = [psum(ph, "pq%d" % i, [128, 512], F32) for i in range(4)]
                pv = psum(ph, "pv", [128, 128], F32)
                rp = make_rope_proj(ph, w, wsw, hT, pq, "w", "wsw")
                qi = 0
                for tb in range(NB):
                    blk = slice(tb * 512, (tb + 1) * 512)
                    c_, s_ = cs[tb % 2], sn[tb % 2]
                    p.op("sp", lambda c_=c_, blk=blk: nc.sync.dma_start(out=c_[:], in_=cosF_d[:, blk]),
                         writes=[("cs", tb % 2)], dma=True)
                    p.op("sp", lambda s_=s_, blk=blk: nc.sync.dma_start(out=s_[:], in_=sinF_d[:, blk]),
                         writes=[("sn", tb % 2)], dma=True)
                    mk_hT(tb)
                    ck = [("cs", tb % 2), ("sn", tb % 2)]
                    for c in range(NCH + 1):
                        q_ = qst[qi % 4]
                        kq = ("qst", qi % 4)
                        qi += 1
                        rp(c * 128, c * 128, c_[:], s_[:], ck, q_[:], kq)
                        dram, row = (QT_d, c * 128) if c < NCH else (KT_d, 0)
                        p.op("sp", lambda q_=q_, dram=dram, row=row, blk=blk: nc.sync.dma_start(
                            out=dram[row:row + 128, blk], in_=q_[:]), reads=[kq], writes=[("qkd", qi)], dma=True)
                    for t in range(4):
                        tt = tb * 4 + t
                        vs = vst[tt % 2]
                        for k in range(NCH):
                            p.op("pe", lambda k=k, t=t: nc.tensor.matmul(
                                pv[:], hT[:, k, t * 128:(t + 1) * 128], w[:, k, 1152:1280],
                                start=(k == 0), stop=(k == NCH - 1)), reads=["w", ("hT", k)], writes=["pv"])
                        p.op("dve", lambda vs=vs: nc.vector.tensor_copy(
                            out=vs[:, :, 0:64], in_=pv[:].rearrange("p (h d) -> p h d", d=64)),
                            reads=["pv"], writes=[("vst", tt % 2)])
                        p.op("sp", lambda vs=vs, tt=tt: nc.sync.dma_start(
                            out=V_d[tt * 128:(tt + 1) * 128, 0:2 * VW], in_=vs[:].rearrange("p h d -> p (h d)")),
                            reads=[("vst", tt % 2)], writes=[("vd", tt)], dma=True)
                p.flush()

        def phase_swa_b(L, r):
            with ExitStack() as ph:
                Vall = sbuf(ph, "Vall", [128, NT, 2 * VW], BF16)
                Kdup = [sbuf(ph, "Kdup%d" % i, [128, S], BF16) for i in range(2)]
                Qc = [sbuf(ph, "Qc%d" % i, [128, S], BF16) for i in range(2)]
                pt = [sbuf(ph, "pt%d" % i, [128, 256], BF16) for i in range(3)]
                apair = [sbuf(ph, "apair%d" % i, [128, NT, 128], BF16) for i in range(2)]
                den = [sbuf(ph, "den%d" % i, [128, 2], F32) for i in range(4)]
                tst = [sbuf(ph, "tst%d" % i, [128, 512], BF16) for i in range(2)]
                esink = sbuf(ph, "esink", [128, 16], F32)
                ps = [psum(ph, "ps%d" % i, [128, 256], F32) for i in range(3)]
                pacc = [psum(ph, "pacc%d" % i, [128, 65], F32) for i in range(4)]
                ptr = psum(ph, "ptrb", [128, 512], BF16)
                p.op("sp", lambda: nc.sync.dma_start(
                    out=Vall[:], in_=V_d[:, 0:2 * VW].rearrange("(t p) f -> p t f", p=128)), writes=["Vall"], dma=True)
                for g_ in range(2):
                    for half in range(2):
                        p.op("sp", lambda g_=g_, half=half: nc.sync.dma_start(
                            out=Kdup[g_][half * 64:(half + 1) * 64, :], in_=KT_d[g_ * 64:(g_ + 1) * 64, :]),
                            writes=[("Kdup", g_)], dma=True)
                p.op("sp", lambda: nc.sync.dma_start(out=esink[:], in_=swa_sinks[r:r + 1, :].partition_broadcast(128)),
                     writes=["esink"], dma=True)
                p.op("act", lambda: nc.scalar.activation(out=esink[:], in_=esink[:], func=AF.Exp),
                     reads=["esink"], writes=["esink"])
                cnt = 0
                units = []

                def swa_loads(hp):
                    qc = Qc[hp % 2]
                    p.op("sp", lambda: nc.sync.dma_start(out=qc[:], in_=QT_d[hp * 128:(hp + 1) * 128, :]),
                         writes=[("Qc", hp % 2)], dma=True)

                swa_loads(0)
                for hp in range(8):
                    qc = Qc[hp % 2]
                    kqc = ("Qc", hp % 2)
                    ap_ = apair[hp % 2]
                    kap = ("apair", hp % 2)
                    for par in range(2):
                        h = 2 * hp + par
                        g_ = h // 8
                        kx = Kdup[g_][par * 64:(par + 1) * 64, :]
                        qx = qc[par * 64:(par + 1) * 64, :]
                        rq = [kqc, ("Kdup", g_)]
                        for j in range(NT):
                            i = cnt % 3
                            cnt += 1
                            psi, pti = ps[i], pt[i]
                            kps, kpt = ("ps", i), ("pt", i)
                            ks = slice(j * 128, (j + 1) * 128)
                            two = j + 1 < NT
                            wd = 256 if two else 128
                            with p.capture() as front:
                                if par == 0 and j == 0 and hp + 1 < 8:
                                    swa_loads(hp + 1)
                                p.op("pe", lambda psi=psi, kx=kx, qx=qx, ks=ks: nc.tensor.matmul(
                                    psi[:, 0:128], kx[:, ks], qx[:, ks], start=True, stop=False, skip_group_check=True),
                                    reads=rq, writes=[kps])
                                p.op("pe", lambda psi=psi: nc.tensor.matmul(
                                    psi[:, 0:128], ident[:], masks[:, 0, :], start=False, stop=True, skip_group_check=True),
                                    reads=["ident", "masks"], writes=[kps])
                                if two:
                                    ks2 = slice((j + 1) * 128, (j + 2) * 128)
                                    p.op("pe", lambda psi=psi, kx=kx, qx=qx, ks=ks, ks2=ks2: nc.tensor.matmul(
                                        psi[:, 128:256], kx[:, ks], qx[:, ks2], start=False, stop=False, skip_group_check=True),
                                        reads=rq, writes=[kps])
                                    p.op("pe", lambda psi=psi: nc.tensor.matmul(
                                        psi[:, 128:256], ident[:], masks[:, 1, :], start=False, stop=True, skip_group_check=True),
                                        reads=["ident", "masks"], writes=[kps])
                            with p.capture() as back:
                                p.op("act", lambda psi=psi, pti=pti, wd=wd: nc.scalar.activation(
                                    out=pti[:, 0:wd], in_=psi[:, 0:wd], func=AF.Exp, scale=0.125),
                                    reads=[kps], writes=[kpt])
                                a0 = pacc[j % 4]
                                p.op("pe", lambda a0=a0, pti=pti, j=j, g_=g_: nc.tensor.matmul(
                                    a0[:, :], pti[:, 0:128], Vall[:, j, g_ * VW:g_ * VW + 65],
                                    start=(j == 0), stop=True, skip_group_check=True),
                                    reads=[kpt, "Vall"], writes=[("pacc", j % 4)])
                                if two:
                                    a1 = pacc[(j + 1) % 4]
                                    p.op("pe", lambda a1=a1, pti=pti, j=j, g_=g_: nc.tensor.matmul(
                                        a1[:, :], pti[:, 128:256], Vall[:, j, g_ * VW:g_ * VW + 65],
                                        start=True, stop=False, skip_group_check=True),
                                        reads=[kpt, "Vall"], writes=[("pacc", (j + 1) % 4)])
                                dn = den[j % 4]
                                kdn = ("den", j % 4)
                                p.op("dve", lambda dn=dn, a0=a0, h=h: nc.vector.tensor_tensor(
                                    out=dn[:, 0:1], in0=a0[:, 64:65], in1=esink[:, h:h + 1], op=ALU.add),
                                    reads=[("pacc", j % 4), "esink"], writes=[kdn])
                                p.op("dve", lambda dn=dn: nc.vector.reciprocal(out=dn[:, 1:2], in_=dn[:, 0:1]),
                                     reads=[kdn], writes=[kdn])
                                p.op("dve", lambda dn=dn, a0=a0, j=j, par=par, ap_=ap_: nc.vector.tensor_scalar(
                                    out=ap_[:, j, par * 64:(par + 1) * 64], in0=a0[:, 0:64], scalar1=dn[:, 1:2],
                                    scalar2=None, op0=ALU.mult), reads=[("pacc", j % 4), kdn], writes=[kap])
                                if par == 1 and j == NT - 1:
                                    emit_pair_transposes(ap_, kap, ptr, tst, hp * 128)
                            units.append((front, back))
                for cap in p.pipelined(units, 2):
                    p.splice(cap)
                p.flush()

        def mixer_swa(L, r, src, dst, mod_next=None):
            phase_swa_a(L, r, src)
            phase_swa_b(L, r)
            phase_outproj(L, swa_w_out[r], src, dst, mod_next)

        def phase_diff_a(L, r, src):
            with ExitStack() as ph:
                w = sbuf(ph, "wdiff", [128, NCH, 3072], BF16)
                wsw = sbuf(ph, "wdiffs", [128, NCH, 2048], BF16)
                load_w(w, diff_w_in[r], 3072, "w")
                load_w(wsw, diff_w_sw[r], 2048, "wsw")
                s1, sh = load_in_vectors(ph, L, 1)
                hT, mk_hT = make_hT_builder(ph, s1, sh, src, 512)
                qst = [sbuf(ph, "qst%d" % i, [128, 512], BF16) for i in range(4)]
                vst = [sbuf(ph, "vst%d" % i, [128, 8, VW2], BF16) for i in range(2)]
                for i in range(2):
                    p.op("pool", lambda i=i: nc.gpsimd.memset(vst[i][:, :, 128:VW2], 1.0), writes=[("vst", i)])
                cs = [sbuf(ph, "cs%d" % i, [128, 512], F32) for i in range(2)]
                sn = [sbuf(ph, "sn%d" % i, [128, 512], F32) for i in range(2)]
                pq = [psum(ph, "pq%d" % i, [128, 512], F32) for i in range(4)]
                pv = [psum(ph, "pv%d" % i, [128, 512], F32) for i in range(2)]
                rp = make_rope_proj(ph, w, wsw, hT, pq, "w", "wsw")
                qi = 0
                for tb in range(NB):
                    blk = slice(tb * 512, (tb + 1) * 512)
                    c_, s_ = cs[tb % 2], sn[tb % 2]
                    p.op("sp", lambda c_=c_, blk=blk: nc.sync.dma_start(out=c_[:], in_=cosF_d[:, blk]),
                         writes=[("cs", tb % 2)], dma=True)
                    p.op("sp", lambda s_=s_, blk=blk: nc.sync.dma_start(out=s_[:], in_=sinF_d[:, blk]),
                         writes=[("sn", tb % 2)], dma=True)
                    mk_hT(tb)
                    ck = [("cs", tb % 2), ("sn", tb % 2)]
                    for c in range(2 * NCH):
                        q_ = qst[qi % 4]
                        kq = ("qst", qi % 4)
                        qi += 1
                        rp(c * 128, c * 128, c_[:], s_[:], ck, q_[:], kq)
                        dram, row = (QT_d, c * 128) if c < NCH else (KT_d, (c - NCH) * 128)
                        p.op("sp", lambda q_=q_, dram=dram, row=row, blk=blk: nc.sync.dma_start(
                            out=dram[row:row + 128, blk], in_=q_[:]), reads=[kq], writes=[("qkd", qi)], dma=True)
                    for t in range(4):
                        tt = tb * 4 + t
                        vs = vst[tt % 2]
                        for hh in range(2):
                            for k in range(NCH):
                                p.op("pe", lambda k=k, t=t, hh=hh: nc.tensor.matmul(
                                    pv[hh][:], hT[:, k, t * 128:(t + 1) * 128], w[:, k, 2048 + hh * 512:2048 + (hh + 1) * 512],
                                    start=(k == 0), stop=(k == NCH - 1)), reads=["w", ("hT", k)], writes=[("pv", hh)])
                            if hh == 0:
                                p.op("dve", lambda vs=vs, hh=hh: nc.vector.tensor_copy(
                                    out=vs[:, hh * 4:(hh + 1) * 4, 0:128], in_=pv[hh][:].rearrange("p (h d) -> p h d", d=128)),
                                    reads=[("pv", hh)], writes=[("vst", tt % 2)])
                            else:
                                p.op("act", lambda vs=vs, hh=hh: nc.scalar.copy(
                                    out=vs[:, hh * 4:(hh + 1) * 4, 0:128], in_=pv[hh][:].rearrange("p (h d) -> p h d", d=128)),
                                    reads=[("pv", hh)], writes=[("vst", tt % 2)])
                        p.op("sp", lambda vs=vs, tt=tt: nc.sync.dma_start(
                            out=V_d[tt * 128:(tt + 1) * 128, 0:8 * VW2], in_=vs[:].rearrange("p h d -> p (h d)")),
                            reads=[("vst", tt % 2)], writes=[("vd", tt)], dma=True)
                p.flush()

        def phase_diff_b(L, r):
            lam_init = 0.8 - 0.6 * math.exp(-0.3 * L)
            with ExitStack() as ph:
                Vall = sbuf(ph, "Vall", [128, NT, 8 * VW2], BF16)
                Qc = [sbuf(ph, "Qc%d" % i, [128, S], BF16) for i in range(2)]
                Kc = [sbuf(ph, "Kc%d" % i, [128, S], BF16) for i in range(2)]
                pt = [sbuf(ph, "pt%d" % i, [128, 512], BF16) for i in range(3)]
                lt = sbuf(ph, "lamt", [128, 256], F32)
                lsm = sbuf(ph, "lsm", [128, 8], F32)
                sub = sbuf(ph, "subln", [128, 1], F32)
                onesb = sbuf(ph, "onesb", [128, 128], BF16)
                r1 = sbuf(ph, "r1", [128, 512], F32)
                o_ = sbuf(ph, "o_", [128, 512], F32)
                r2 = sbuf(ph, "r2", [128, 512], F32)
                t2 = sbuf(ph, "t2", [128, 512], F32)
                sq = sbuf(ph, "sq", [128, 512], BF16)
                rs = sbuf(ph, "rs", [128, 512], F32)
                ot = [sbuf(ph, "ot%d" % i, [128, 512], BF16) for i in range(2)]
                ps = [psum(ph, "ps%d" % i, [128, 512], F32) for i in range(3)]
                oacc = [psum(ph, "oacc%d" % i, [128, 512], F32) for i in range(2)]
                dacc = [psum(ph, "dacc%d" % i, [128, 512], F32) for i in range(2)]
                pss = psum(ph, "pss", [128, 512], F32)
                p.op("pool", lambda: nc.gpsimd.memset(onesb[:], 1.0), writes=["onesb"])
                epsT = sbuf(ph, "epsT", [128, 1], F32)
                p.op("pool", lambda: nc.gpsimd.memset(epsT[:], RMS_EPS), writes=["epsT"])
                for t0 in range(0, NT, 4):
                    p.op("sp", lambda t0=t0: nc.sync.dma_start(
                        out=Vall[:, t0:t0 + 4, :],
                        in_=V_d[t0 * 128:(t0 + 4) * 128, 0:8 * VW2].rearrange("(t p) f -> p t f", p=128)),
                        writes=["Vall"], dma=True)
                p.op("sp", lambda: nc.sync.dma_start(out=lt[:], in_=diff_lambda[r:r + 1, :].partition_broadcast(128)),
                     writes=["lt"], dma=True)
                p.op("sp", lambda: nc.sync.dma_start(out=sub[:], in_=diff_subln[r, :].rearrange("(p o) -> p o", o=1)),
                     writes=["sub"], dma=True)
                p.op("dve", lambda: nc.vector.tensor_scalar(out=sub[:], in0=sub[:], scalar1=1.0 - lam_init, scalar2=None,
                                                           op0=ALU.mult), reads=["sub"], writes=["sub"])
                p.op("dve", lambda: nc.vector.tensor_tensor(out=lt[:, 0:64], in0=lt[:, 0:64], in1=lt[:, 64:128], op=ALU.mult),
                     reads=["lt"], writes=["lt"])
                p.op("dve", lambda: nc.vector.tensor_tensor(out=lt[:, 128:192], in0=lt[:, 128:192], in1=lt[:, 192:256], op=ALU.mult),
                     reads=["lt"], writes=["lt"])
                p.op("dve", lambda: nc.vector.reduce_sum(out=lsm[:, 0:1], in_=lt[:, 0:64], axis=AX.X), reads=["lt"], writes=["lsm"])
                p.op("dve", lambda: nc.vector.reduce_sum(out=lsm[:, 1:2], in_=lt[:, 128:192], axis=AX.X), reads=["lt"], writes=["lsm"])
                p.op("act", lambda: nc.scalar.activation(out=lsm[:, 2:4], in_=lsm[:, 0:2], func=AF.Exp), reads=["lsm"], writes=["lsm"])
                p.op("dve", lambda: nc.vector.tensor_tensor(out=lsm[:, 4:5], in0=lsm[:, 3:4], in1=lsm[:, 2:3], op=ALU.subtract),
                     reads=["lsm"], writes=["lsm"])
                p.op("dve", lambda: nc.vector.tensor_scalar(out=lsm[:, 5:6], in0=lsm[:, 4:5], scalar1=-lam_init, scalar2=None,
                                                           op0=ALU.add), reads=["lsm"], writes=["lsm"])
                cnt = 0
                fin = 0
                units = []

                def diff_loads(h):
                    qc, kc = Qc[h % 2], Kc[h % 2]
                    p.op("sp", lambda: nc.sync.dma_start(out=qc[:], in_=QT_d[h * 128:(h + 1) * 128, :]),
                         writes=[("Qc", h % 2)], dma=True)
                    p.op("sp", lambda: nc.sync.dma_start(out=kc[:], in_=KT_d[h * 128:(h + 1) * 128, :]),
                         writes=[("Kc", h % 2)], dma=True)

                diff_loads(0)
                for h in range(8):
                    qc, kc = Qc[h % 2], Kc[h % 2]
                    rq = [("Qc", h % 2), ("Kc", h % 2)]
                    for qb in range(NB):
                        for c in range(2):
                            kx = kc[c * 64:(c + 1) * 64, :]
                            qx = qc[c * 64:(c + 1) * 64, :]
                            oa, da = oacc[c], dacc[c]
                            koa, kda = ("oacc", c), ("dacc", c)
                            for j in range(4 * qb + 4):
                                c0 = max(0, j - 4 * qb) * 128
                                i = cnt % 3
                                cnt += 1
                                psi, pti = ps[i], pt[i]
                                kps, kpt = ("ps", i), ("pt", i)
                                ks = slice(j * 128, (j + 1) * 128)
                                q0 = qb * 512
                                with p.capture() as front:
                                    if qb == 0 and c == 0 and j == 0 and h + 1 < 8:
                                        diff_loads(h + 1)
                                    if j >= 4 * qb:
                                        p.op("pe", lambda psi=psi, kx=kx, qx=qx, ks=ks, c0=c0, q0=q0: nc.tensor.matmul(
                                            psi[:, c0:c0 + 128], kx[:, ks], qx[:, q0 + c0:q0 + c0 + 128], start=True, stop=False,
                                            skip_group_check=True), reads=rq, writes=[kps])
                                        p.op("pe", lambda psi=psi, c0=c0: nc.tensor.matmul(
                                            psi[:, c0:c0 + 128], ident[:], masks[:, 0, :], start=False, stop=True,
                                            skip_group_check=True), reads=["ident", "masks"], writes=[kps])
                                        if c0 + 128 < 512:
                                            p.op("pe", lambda psi=psi, kx=kx, qx=qx, ks=ks, c0=c0, q0=q0: nc.tensor.matmul(
                                                psi[:, c0 + 128:512], kx[:, ks], qx[:, q0 + c0 + 128:q0 + 512], start=False, stop=True,
                                                skip_group_check=True), reads=rq, writes=[kps])
                                    else:
                                        p.op("pe", lambda psi=psi, kx=kx, qx=qx, ks=ks, q0=q0: nc.tensor.matmul(
                                            psi[:, :], kx[:, ks], qx[:, q0:q0 + 512], start=True, stop=True), reads=rq, writes=[kps])
                                with p.capture() as back:
                                    p.op("act", lambda psi=psi, pti=pti, c0=c0: nc.scalar.activation(
                                        out=pti[:, c0:512], in_=psi[:, c0:512], func=AF.Exp, scale=0.125),
                                        reads=[kps], writes=[kpt])
                                    last = (j == 4 * qb + 3)
                                    p.op("pe", lambda oa=oa, pti=pti, j=j, h=h, c0=c0, last=last: nc.tensor.matmul(
                                        oa[:, c0:512], Vall[:, j, h * VW2:h * VW2 + 128], pti[:, c0:512],
                                        start=(j == 0), stop=last, skip_group_check=True),
                                        reads=[kpt, "Vall"], writes=[koa])
                                    p.op("pe", lambda da=da, pti=pti, j=j, c0=c0, last=last: nc.tensor.matmul(
                                        da[:, c0:512], onesb[:], pti[:, c0:512],
                                        start=(j == 0), stop=last, skip_group_check=True),
                                        reads=[kpt, "onesb"], writes=[kda])
                                    if c == 0 and last:
                                        p.op("act", lambda: nc.scalar.activation(out=r1[:], in_=dacc[0][:], func=AF.Ln),
                                             reads=[("dacc", 0)], writes=["r1"])
                                        p.op("act", lambda: nc.scalar.activation(out=r1[:], in_=r1[:], func=AF.Exp, scale=-1.0),
                                             reads=["r1"], writes=["r1"])
                                        p.op("dve", lambda: nc.vector.tensor_tensor(out=o_[:], in0=oacc[0][:], in1=r1[:], op=ALU.mult),
                                             reads=[("oacc", 0), "r1"], writes=["o_"])
                                    if c == 1 and last:
                                        ot_ = ot[fin % 2]
                                        kot = ("ot", fin % 2)
                                        fin += 1
                                        p.op("act", lambda: nc.scalar.activation(out=r2[:], in_=dacc[1][:], func=AF.Ln),
                                             reads=[("dacc", 1)], writes=["r2"])
                                        p.op("act", lambda: nc.scalar.activation(out=r2[:], in_=r2[:], func=AF.Exp, scale=-1.0),
                                             reads=["r2"], writes=["r2"])
                                        p.op("dve", lambda: nc.vector.tensor_tensor(out=t2[:], in0=oacc[1][:], in1=r2[:], op=ALU.mult),
                                             reads=[("oacc", 1), "r2"], writes=["t2"])
                                        p.op("dve", lambda: nc.vector.scalar_tensor_tensor(
                                            out=o_[:], in0=t2[:], scalar=lsm[:, 5:6], in1=o_[:], op0=ALU.mult, op1=ALU.add),
                                            reads=["t2", "lsm", "o_"], writes=["o_"])
                                        p.op("pool", lambda: nc.gpsimd.tensor_tensor(out=sq[:], in0=o_[:], in1=o_[:], op=ALU.mult),
                                             reads=["o_"], writes=["sq"])
                                        p.op("pe", lambda: nc.tensor.matmul(pss[:], onesb[:], sq[:], start=True, stop=True),
                                             reads=["sq", "onesb"], writes=["pss"])
                                        p.op("act", lambda: nc.scalar.activation(out=rs[:], in_=pss[:], func=AF.Ln,
                                                                                 bias=epsT[:, 0:1], scale=1.0 / 128),
                                             reads=["pss", "epsT"], writes=["rs"])
                                        p.op("act", lambda: nc.scalar.activation(out=rs[:], in_=rs[:], func=AF.Exp, scale=-0.5),
                                             reads=["rs"], writes=["rs"])
                                        p.op("dve", lambda ot_=ot_: nc.vector.scalar_tensor_tensor(
                                            out=ot_[:], in0=o_[:], scalar=sub[:, 0:1], in1=rs[:], op0=ALU.mult, op1=ALU.mult),
                                            reads=["o_", "sub", "rs"], writes=[kot])
                                        p.op("sp", lambda ot_=ot_, h=h, qb=qb: nc.sync.dma_start(
                                            out=attnT_d[h * 128:(h + 1) * 128, qb * 512:(qb + 1) * 512], in_=ot_[:]),
                                            reads=[kot], writes=[("attnT", h, qb)], dma=True)
                                units.append((front, back))
                for cap in p.pipelined(units, 2):
                    p.splice(cap)
                p.flush()

        def mixer_diff(L, r, src, dst, mod_next=None):
            phase_diff_a(L, r, src)
            phase_diff_b(L, r)
            phase_outproj(L, diff_w_out[r], src, dst, mod_next)


        def phase_dsa_a(L, r, src):
            with ExitStack() as ph:
                w = sbuf(ph, "wdsa", [128, NCH, 1448], BF16)
                wsw = sbuf(ph, "wdsas", [128, NCH, 1312], BF16)
                load_w(w, dsa_w_in[r], 1448, "w")
                load_w(wsw, dsa_w_sw[r], 1312, "wsw")
                wkv = sbuf(ph, "wkv", [128, 128], BF16)
                wkvs = sbuf(ph, "wkvs", [128, 64], BF16)
                kvn = sbuf(ph, "kvn", [128, 1], F32)
                p.op("pool", lambda: nc.gpsimd.dma_start(out=wkv[:], in_=dsa_w_kv_up[r]), writes=["wkv"], dma=True)
                p.op("pool", lambda: nc.gpsimd.dma_start(out=wkvs[:], in_=dsa_w_kv_sw[r]), writes=["wkvs"], dma=True)
                p.op("sp", lambda: nc.sync.dma_start(out=kvn[:], in_=dsa_kv_norm[r, :].rearrange("(p o) -> p o", o=1)),
                     writes=["kvn"], dma=True)
                s1, sh = load_in_vectors(ph, L, 1)
                hT, mk_hT = make_hT_builder(ph, s1, sh, src, 512)
                qst = [sbuf(ph, "qst%d" % i, [128, 512], BF16) for i in range(4)]
                vst = [sbuf(ph, "vst%d" % i, [128, VW], BF16) for i in range(2)]
                for i in range(2):
                    p.op("pool", lambda i=i: nc.gpsimd.memset(vst[i][:, 64:VW], 1.0), writes=[("vst", i)])
                cs = [sbuf(ph, "cs%d" % i, [128, 512], F32) for i in range(2)]
                sn = [sbuf(ph, "sn%d" % i, [128, 512], F32) for i in range(2)]
                csi = [sbuf(ph, "csi%d" % i, [128, 512], F32) for i in range(2)]
                sni = [sbuf(ph, "sni%d" % i, [128, 512], F32) for i in range(2)]
                ckn = [sbuf(ph, "ckn%d" % i, [128, 128], BF16) for i in range(2)]
                ckvT = sbuf(ph, "ckvT", [128, 512], BF16)
                wist = sbuf(ph, "wist", [128, NT, 8], F32)
                fs = [sbuf(ph, "fs%d" % i, [128, 4], F32) for i in range(2)]
                jk = sbuf(ph, "junk", [128, 128], F32)
                nh = sbuf(ph, "neghalf3", [128, 1], F32)
                kt1 = sbuf(ph, "kt1", [64, 512], F32)
                kt2 = sbuf(ph, "kt2", [64, 512], F32)
                p.op("pool", lambda: nc.gpsimd.memset(nh[:], -0.5), writes=["nh"])
                pq = [psum(ph, "pq%d" % i, [128, 512], F32) for i in range(4)]
                pv = psum(ph, "pv", [128, 128], F32)
                pT2 = psum(ph, "pT2", [128, 512], BF16)
                rp = make_rope_proj(ph, w, wsw, hT, pq, "w", "wsw")
                qi = 0
                for tb in range(NB):
                    blk = slice(tb * 512, (tb + 1) * 512)
                    b2 = tb % 2
                    for tl, dr, nm_ in ((cs, cosF_d, "cs"), (sn, sinF_d, "sn"), (csi, cosI_d, "csi"), (sni, sinI_d, "sni")):
                        p.op("sp", lambda tl=tl, dr=dr, blk=blk, b2=b2: nc.sync.dma_start(out=tl[b2][:], in_=dr[:, blk]),
                             writes=[(nm_, b2)], dma=True)
                    mk_hT(tb)
                    ck = [("cs", b2), ("sn", b2)]
                    cki = [("csi", b2), ("sni", b2)]
                    for c in range(NCH + 3):
                        q_ = qst[qi % 4]
                        kq = ("qst", qi % 4)
                        qi += 1
                        if c < NCH:
                            rp(c * 128, c * 128, cs[b2][:], sn[b2][:], ck, q_[:], kq)
                            p.op("sp", lambda q_=q_, c=c, blk=blk: nc.sync.dma_start(
                                out=QT_d[c * 128:(c + 1) * 128, blk], in_=q_[:]), reads=[kq], writes=[("qkd", qi)], dma=True)
                        elif c < NCH + 2:
                            ci = c - NCH
                            rp(1152 + ci * 128, 1024 + ci * 128, csi[b2][:], sni[b2][:], cki, q_[:], kq)
                            p.op("sp", lambda q_=q_, ci=ci, blk=blk: nc.sync.dma_start(
                                out=QI_d[ci * 128:(ci + 1) * 128, blk], in_=q_[:]), reads=[kq], writes=[("qkd", qi)], dma=True)
                        else:
                            rp(1408, 1280, csi[b2][0:32, :], sni[b2][0:32, :], cki, q_[0:32, :], kq, M=32)
                            p.op("sp", lambda q_=q_, blk=blk: nc.sync.dma_start(
                                out=KI_d[:, blk], in_=q_[0:32, :]), reads=[kq], writes=[("qkd", qi)], dma=True)
                    for t in range(4):
                        tt = tb * 4 + t
                        f_ = fs[tt % 2]
                        kf = ("fs", tt % 2)
                        cn = ckn[tt % 2]
                        kcn = ("ckn", tt % 2)
                        for k in range(NCH):
                            p.op("pe", lambda k=k, t=t: nc.tensor.matmul(
                                pv[:], hT[:, k, t * 128:(t + 1) * 128], w[:, k, 1024:1152],
                                start=(k == 0), stop=(k == NCH - 1)), reads=["w", ("hT", k)], writes=["pv"])
                        p.op("act", lambda f_=f_: nc.scalar.activation(out=jk[:], in_=pv[:], func=AF.Square, accum_out=f_[:, 0:1]),
                             reads=["pv"], writes=[kf, "junk"])
                        p.op("pool", lambda f_=f_: nc.gpsimd.tensor_scalar(out=f_[:, 1:2], in0=f_[:, 0:1], scalar1=1.0 / 128,
                                                                          scalar2=RMS_EPS, op0=ALU.mult, op1=ALU.add),
                             reads=[kf], writes=[kf])
                        p.op("pool", lambda f_=f_: nc.gpsimd.tensor_tensor(out=f_[:, 2:3], in0=f_[:, 1:2], in1=nh[:], op=ALU.pow),
                             reads=[kf, "nh"], writes=[kf])
                        p.op("dve", lambda f_=f_, cn=cn: nc.vector.tensor_scalar(out=cn[:], in0=pv[:], scalar1=f_[:, 2:3],
                                                                               scalar2=None, op0=ALU.mult),
                             reads=["pv", kf], writes=[kcn])
                        p.op("pe", lambda cn=cn, t=t: nc.tensor.transpose(pT2[:, t * 128:(t + 1) * 128], cn[:], ident[:]),
                             reads=[kcn, "ident"], writes=["pT2"])
                        for k in range(NCH):
                            p.op("pe", lambda k=k, t=t: nc.tensor.matmul(
                                pq[3][:, 0:8], hT[:, k, t * 128:(t + 1) * 128], w[:, k, 1440:1448],
                                start=(k == 0), stop=(k == NCH - 1)), reads=["w", ("hT", k)], writes=[("pq", 3)])
                        p.op("dve", lambda tt=tt: nc.vector.tensor_scalar(out=wist[:, tt, :], in0=pq[3][:, 0:8], scalar1=1.0 / 16,
                                                                        scalar2=None, op0=ALU.mult),
                             reads=[("pq", 3)], writes=["wist"])
                    p.op("act", lambda: nc.scalar.activation(out=ckvT[:], in_=pT2[:], func=AF.Copy, scale=kvn[:, 0:1]),
                         reads=["pT2", "kvn"], writes=["ckvT"])
                    p.op("pe", lambda: nc.tensor.matmul(pq[0][0:64, :], wkv[:, 0:64], ckvT[:], start=True, stop=True),
                         reads=["wkv", "ckvT"], writes=[("pq", 0)])
                    p.op("pe", lambda: nc.tensor.matmul(pq[1][0:64, :], wkvs[:, 0:64], ckvT[:], start=True, stop=True),
                         reads=["wkvs", "ckvT"], writes=[("pq", 1)])
                    p.op("dve", lambda b2=b2: nc.vector.tensor_tensor(out=kt1[:], in0=pq[0][0:64, :], in1=cs[b2][0:64, :], op=ALU.mult),
                         reads=[("pq", 0)] + ck, writes=["kt1"])
                    p.op("dve", lambda b2=b2: nc.vector.tensor_tensor(out=kt2[:], in0=pq[1][0:64, :], in1=sn[b2][0:64, :], op=ALU.mult),
                         reads=[("pq", 1)] + ck, writes=["kt2"])
                    q_ = qst[qi % 4]
                    kq = ("qst", qi % 4)
                    qi += 1
                    p.op("pool", lambda q_=q_: nc.gpsimd.tensor_tensor(out=q_[0:64, :], in0=kt1[:], in1=kt2[:], op=ALU.add),
                         reads=["kt1", "kt2"], writes=[kq])
                    p.op("sp", lambda q_=q_, blk=blk: nc.sync.dma_start(out=KT_d[0:64, blk], in_=q_[0:64, :]),
                         reads=[kq], writes=[("qkd", qi)], dma=True)
                    for t in range(4):
                        tt = tb * 4 + t
                        vs = vst[tt % 2]
                        p.op("pe", lambda t=t: nc.tensor.matmul(pv[:, 0:64], ckvT[:, t * 128:(t + 1) * 128], wkv[:, 64:128],
                                                                start=True, stop=True),
                             reads=["wkv", "ckvT"], writes=["pv"])
                        p.op("dve", lambda vs=vs: nc.vector.tensor_copy(out=vs[:, 0:64], in_=pv[:, 0:64]),
                             reads=["pv"], writes=[("vst", tt % 2)])
                        p.op("sp", lambda vs=vs, tt=tt: nc.sync.dma_start(out=V_d[tt * 128:(tt + 1) * 128, 0:VW], in_=vs[:]),
                             reads=[("vst", tt % 2)], writes=[("vd", tt)], dma=True)
                p.op("sp", lambda: nc.sync.dma_start(out=WI_d.rearrange("(t p) h -> p t h", p=128), in_=wist[:]),
                     reads=["wist"], writes=["wid"], dma=True)
                p.flush()

        def phase_dsa_b(L, r):
            NIT = 20
            KSEL = min(256, S // 4)
            with ExitStack() as ph:
                KT2 = sbuf(ph, "KT2", [128, S], BF16)
                Vall = sbuf(ph, "Vall", [128, NT, VW], BF16)
                KI = sbuf(ph, "KI", [32, S], BF16)
                WI = sbuf(ph, "WI", [128, NT, 8], F32)
                maskT = [sbuf(ph, "maskT%d" % i, [128, NT, 512], BF16) for i in range(2)]
                ablk = sbuf(ph, "ablk", [128, 4, D], BF16)
                QTb = [sbuf(ph, "QTb%d" % i, [128, NCH, 512], BF16) for i in range(2)]
                QIb = [sbuf(ph, "QIb%d" % i, [32, 8, 512], BF16) for i in range(2)]
                sc = sbuf(ph, "sc", [128, S], F32)
                junk = sbuf(ph, "junkb", [128, S], BF16)
                nm = sbuf(ph, "nm", [128, S], BF16)
                tmp = [sbuf(ph, "tmp%d" % i, [128, 512], F32) for i in range(2)]
                pt = [sbuf(ph, "pt%d" % i, [128, 512], BF16) for i in range(3)]
                st = sbuf(ph, "st", [128, 8], F32)
                rec = [sbuf(ph, "rec%d" % i, [128, 4], F32) for i in range(2)]
                tst = [sbuf(ph, "tst%d" % i, [128, 512], BF16) for i in range(2)]
                mge = sbuf(ph, "mge", [128, 128], F32)
                psc = [psum(ph, "psc%d" % i, [128, 512], F32) for i in range(2)]
                ps = [psum(ph, "ps%d" % i, [128, 512], F32) for i in range(2)]
                pacc = [psum(ph, "pacc%d" % i, [128, 4, 65], F32) for i in range(2)]
                ptrm = psum(ph, "ptrm", [128, 512], BF16)
                ptra = psum(ph, "ptra", [128, 512], BF16)
                for half in range(2):
                    p.op("sp", lambda half=half: nc.sync.dma_start(out=KT2[half * 64:(half + 1) * 64, :], in_=KT_d[0:64, :]),
                         writes=["KT2"], dma=True)
                p.op("sp", lambda: nc.sync.dma_start(out=Vall[:], in_=V_d[:, 0:VW].rearrange("(t p) f -> p t f", p=128)),
                     writes=["Vall"], dma=True)
                p.op("sp", lambda: nc.sync.dma_start(out=KI[:], in_=KI_d), writes=["KI"], dma=True)
                p.op("sp", lambda: nc.sync.dma_start(out=WI[:], in_=WI_d.rearrange("(t p) h -> p t h", p=128)),
                     writes=["WI"], dma=True)
                p.op("dve", lambda: nc.vector.tensor_copy(out=mge[:], in_=masks[:, 2, :]), reads=["masks"], writes=["mge"])
                neg1 = sbuf(ph, "neg1", [128, 4], F32)
                p.op("pool", lambda: nc.gpsimd.memset(neg1[:], -1.0), writes=["neg1"])
                cA = [0]

                def step_a(qb):
                    caps = []

                    def unit(cost=1.0):
                        c = p.capture()
                        c.cost = cost
                        caps.append(c)
                        return c
                    mt = maskT[qb % 2]
                    kmt = ("maskT", qb % 2)
                    qib = QIb[qb % 2]
                    kqi = ("QIb", qb % 2)
                    with unit():
                        p.op("sp", lambda: nc.sync.dma_start(
                            out=qib[:], in_=QI_d[:, qb * 512:(qb + 1) * 512].rearrange("(h d) s -> d h s", d=32)),
                            writes=[kqi], dma=True)
                    for t in range(4):
                        tt = 4 * qb + t
                        nk = (tt + 1) * 128
                        dg = slice(tt * 128, (tt + 1) * 128)
                        if (tt + 1) * 128 <= KSEL:
                            with unit():
                                if tt > 0:
                                    p.op("pool", lambda tt=tt: nc.gpsimd.memset(nm[:, 0:tt * 128], 0.0), writes=["nm"])
                                p.op("pool", lambda dg=dg: nc.gpsimd.tensor_copy(out=nm[:, dg], in_=masks[:, 2, :]),
                                     reads=["masks"], writes=["nm"])
                        else:
                            nkb = (nk + 511) // 512
                            sck = [("sc", kb) for kb in range(nkb)]
                            for kb in range(nkb):
                                wd = min(512, nk - kb * 512)
                                cols = slice(kb * 512, kb * 512 + wd)
                                for h in range(8):
                                    i = cA[0] % 2
                                    cA[0] += 1
                                    pc = psc[i]
                                    kpc = ("psc", i)
                                    with unit(0.9):
                                        p.op("pe", lambda pc=pc, h=h, t=t, cols=cols, wd=wd: nc.tensor.matmul(
                                            pc[:, 0:wd], qib[:, h, t * 128:(t + 1) * 128], KI[:, cols], start=True, stop=True),
                                            reads=[kqi, "KI"], writes=[kpc])
                                        if h == 0:
                                            p.op("dve", lambda pc=pc, cols=cols, wd=wd, tt=tt: nc.vector.tensor_scalar(
                                                out=sc[:, cols], in0=pc[:, 0:wd], scalar1=0.0, scalar2=WI[:, tt, 0:1],
                                                op0=ALU.max, op1=ALU.mult), reads=[kpc, "WI", "nmdone"], writes=[("sc", kb)])
                                        else:
                                            tm = tmp[i]
                                            ktm = ("tmp", i)
                                            p.op("dve", lambda pc=pc, tm=tm, wd=wd, tt=tt, h=h: nc.vector.tensor_scalar(
                                                out=tm[:, 0:wd], in0=pc[:, 0:wd], scalar1=0.0, scalar2=WI[:, tt, h:h + 1],
                                                op0=ALU.max, op1=ALU.mult), reads=[kpc, "WI"], writes=[ktm])
                                            if h <= 5:
                                                p.op("pool", lambda tm=tm, cols=cols, wd=wd: nc.gpsimd.tensor_tensor(
                                                    out=sc[:, cols], in0=sc[:, cols], in1=tm[:, 0:wd], op=ALU.add),
                                                    reads=[ktm, ("sc", kb)], writes=[("sc", kb)])
                                            else:
                                                p.op("dve", lambda tm=tm, cols=cols, wd=wd: nc.vector.tensor_tensor(
                                                    out=sc[:, cols], in0=sc[:, cols], in1=tm[:, 0:wd], op=ALU.add),
                                                    reads=[ktm, ("sc", kb)], writes=[("sc", kb)])
                            with unit(0.5 + nk / 960.0):
                                p.op("dve", lambda nk=nk: nc.vector.reduce_max(out=st[:, 0:1], in_=sc[:, 0:nk], axis=AX.X,
                                                                              apply_absolute_value=True),
                                     reads=sck, writes=["st"])
                                p.op("pool", lambda dg=dg: nc.gpsimd.tensor_tensor(out=sc[:, dg], in0=sc[:, dg], in1=mge[:], op=ALU.add),
                                     reads=sck + ["mge", "st"], writes=sck)
                                p.op("dve", lambda: nc.vector.tensor_scalar(out=st[:, 1:2], in0=st[:, 0:1], scalar1=-1.0, scalar2=None,
                                                                           op0=ALU.mult), reads=["st"], writes=["st"])
                                p.op("dve", lambda: nc.vector.tensor_scalar(out=st[:, 2:3], in0=st[:, 0:1], scalar1=2.0002, scalar2=1e-6,
                                                                           op0=ALU.mult, op1=ALU.add), reads=["st"], writes=["st"])
                            for n_ in range(1, NIT + 1):
                                f = 2.0 ** (-n_)
                                with unit(0.8 + nk / 960.0):
                                    p.op("dve", lambda f=f: nc.vector.tensor_scalar(out=st[:, 3:4], in0=st[:, 2:3], scalar1=f,
                                                                                  scalar2=st[:, 1:2], op0=ALU.mult, op1=ALU.add),
                                         reads=["st"], writes=["st"])
                                    p.op("dve", lambda nk=nk: nc.vector.tensor_scalar(
                                        out=junk[:, 0:nk], in0=sc[:, 0:nk], scalar1=st[:, 3:4], scalar2=0.0,
                                        op0=ALU.is_ge, op1=ALU.add, accum_out=st[:, 4:5]),
                                        reads=sck + ["st"], writes=["junk", "st"])
                                    p.op("dve", lambda: nc.vector.tensor_scalar(out=st[:, 5:6], in0=st[:, 4:5], scalar1=KSEL - 0.5,
                                                                               scalar2=st[:, 2:3], op0=ALU.is_ge, op1=ALU.mult),
                                         reads=["st"], writes=["st"])
                                    p.op("dve", lambda f=f: nc.vector.scalar_tensor_tensor(out=st[:, 1:2], in0=st[:, 5:6], scalar=f,
                                                                                         in1=st[:, 1:2], op0=ALU.mult, op1=ALU.add),
                                         reads=["st"], writes=["st"])
                            with unit(0.3 + nk / 1900.0):
                                p.op("dve", lambda nk=nk: nc.vector.tensor_scalar(
                                    out=nm[:, 0:nk], in0=sc[:, 0:nk], scalar1=st[:, 1:2], scalar2=NEG, op0=ALU.is_lt, op1=ALU.mult),
                                    reads=sck + ["st"], writes=["nm", "nmdone"])
                        for j0 in range(0, tt + 1, 4):
                            n4 = min(4, tt + 1 - j0)
                            with unit():
                                for i4 in range(n4):
                                    p.op("pe", lambda j0=j0, i4=i4: nc.tensor.transpose(
                                        ptrm[:, i4 * 128:(i4 + 1) * 128], nm[:, (j0 + i4) * 128:(j0 + i4 + 1) * 128], ident[:]),
                                        reads=["nm", "ident"], writes=["ptrm"])
                                p.op("dve", lambda j0=j0, n4=n4, t=t: nc.vector.tensor_copy(
                                    out=mt[:, j0:j0 + n4, t * 128:(t + 1) * 128],
                                    in_=ptrm[:, 0:n4 * 128].rearrange("p (a b) -> p a b", b=128)),
                                    reads=["ptrm"], writes=[kmt])
                    return caps

                cB = [0, 0]

                def step_b(qb):
                    units = []
                    mt = maskT[qb % 2]
                    kmt = ("maskT", qb % 2)
                    qtb = QTb[qb % 2]
                    kqt = ("QTb", qb % 2)
                    first = True
                    for h in range(16):
                        par = h % 2
                        kx = KT2[par * 64:(par + 1) * 64, :]
                        qx = qtb[par * 64:(par + 1) * 64, h // 2, :]
                        pa = pacc[cB[1] % 2]
                        kpa = ("pacc", cB[1] % 2)
                        rc = rec[cB[1] % 2]
                        krc = ("rec", cB[1] % 2)
                        cB[1] += 1
                        for j in range(4 * qb + 4):
                            c0 = max(0, j - 4 * qb) * 128
                            i = cB[0] % 2
                            ip = cB[0] % 3
                            cB[0] += 1
                            psi, pti = ps[i], pt[ip]
                            kps, kpt = ("ps", i), ("pt", ip)
                            ks = slice(j * 128, (j + 1) * 128)
                            with p.capture() as front:
                                if first:
                                    first = False
                                    p.op("sp", lambda: nc.sync.dma_start(
                                        out=qtb[:], in_=QT_d[:, qb * 512:(qb + 1) * 512].rearrange("(c p) s -> p c s", p=128)),
                                        writes=[kqt], dma=True)
                                p.op("pe", lambda psi=psi, kx=kx, qx=qx, ks=ks, c0=c0: nc.tensor.matmul(
                                    psi[:, c0:512], kx[:, ks], qx[:, c0:512], start=True, stop=False, skip_group_check=True),
                                    reads=["KT2", kqt], writes=[kps])
                                p.op("pe", lambda psi=psi, j=j, c0=c0: nc.tensor.matmul(
                                    psi[:, c0:512], ident[:], mt[:, j, c0:512], start=False, stop=True, skip_group_check=True),
                                    reads=["ident", kmt], writes=[kps])
                            with p.capture() as back:
                                p.op("act", lambda psi=psi, pti=pti, c0=c0: nc.scalar.activation(
                                    out=pti[:, c0:512], in_=psi[:, c0:512], func=AF.Exp, scale=0.125),
                                    reads=[kps], writes=[kpt])
                                for t in range(c0 // 128, 4):
                                    p.op("pe", lambda pa=pa, pti=pti, t=t, j=j: nc.tensor.matmul(
                                        pa[:, t, :], pti[:, t * 128:(t + 1) * 128], Vall[:, j, 0:65],
                                        start=(j == 0 and t == 0), stop=(j == 4 * qb + t), skip_group_check=True),
                                        reads=[kpt, "Vall"], writes=[kpa])
                                if j == 4 * qb + 3:
                                    p.op("act", lambda rc=rc, pa=pa: nc.scalar.copy(out=rc[:], in_=pa[:, :, 64]),
                                         reads=[kpa], writes=[krc])
                                    p.op("pool", lambda rc=rc: nc.gpsimd.tensor_tensor(out=rc[:], in0=rc[:], in1=neg1[:], op=ALU.pow),
                                         reads=[krc, "neg1"], writes=[krc])
                                    for t in range(4):
                                        p.op("act", lambda pa=pa, rc=rc, t=t, h=h: nc.scalar.activation(
                                            out=ablk[:, t, h * 64:(h + 1) * 64], in_=pa[:, t, 0:64], func=AF.Copy,
                                            scale=rc[:, t:t + 1]),
                                            reads=[kpa, krc], writes=[("ablk", h // 2)])
                                    if h == 15:
                                        for c in range(NCH):
                                            ts_ = tst[c % 2]
                                            kts = ("tst", c % 2)
                                            for t in range(4):
                                                p.op("pe", lambda c=c, t=t: nc.tensor.transpose(
                                                    ptra[:, t * 128:(t + 1) * 128], ablk[:, t, c * 128:(c + 1) * 128], ident[:]),
                                                    reads=[("ablk", c), "ident"], writes=["ptra"])
                                            p.op("act", lambda ts_=ts_: nc.scalar.copy(out=ts_[:], in_=ptra[:]), reads=["ptra"], writes=[kts])
                                            p.op("sp", lambda ts_=ts_, c=c: nc.sync.dma_start(
                                                out=attnT_d[c * 128:(c + 1) * 128, qb * 512:(qb + 1) * 512], in_=ts_[:]),
                                                reads=[kts], writes=[("attnT", c, qb)], dma=True)
                            units.append((front, back))
                    lst = p.pipelined(units, 1)
                    for c_ in lst:
                        c_.cost = 0.4
                    return lst

                for cap in step_a(0):
                    p.splice(cap)
                for qb in range(NB):
                    lb = step_b(qb)
                    la = step_a(qb + 1) if qb + 1 < NB else []
                    p.merge(la, lb)
                p.flush()

        def mixer_dsa(L, r, src, dst, mod_next=None):
            phase_dsa_a(L, r, src)
            phase_dsa_b(L, r)
            phase_outproj(L, dsa_w_out[r], src, dst, mod_next)

        layers = cfg["layers"]
        phase_mod(layers[:1])
        cur = x_ext
        plan = []
        for L in layers:
            for j in range(3):
                plan.append((L, j))
                if cfg.get("stop_after") == (L, j):
                    break
            else:
                continue
            break
        for idx, (L, j) in enumerate(plan):
            dst = out_ext if idx == len(plan) - 1 else xs
            if j in (0, 2):
                phase_ffn(L, j, cur, dst)
            else:
                mname = cfg.get("mixer_of", {0: "dsa", 1: "fox", 2: "swa", 3: "diff"})[L]
                {"fox": mixer_fox, "swa": mixer_swa, "diff": mixer_diff, "dsa": mixer_dsa}[mname](L, 0, cur, dst, (L + 1) if (L + 1) in layers else None)
            cur = dst
        g.stats = dict(n_ops=p.n_ops, n_wait=p.n_wait)
    return nc, g


def make_consts(S):
    pp = np.arange(128)[:, None]
    ff = np.arange(128)[None, :]
    m = np.zeros((128, 3, 128), np.float32)
    m[:, 0, :] = np.where(pp <= ff, 0.0, NEG)
    m[:, 1, :] = np.where(pp > ff, 0.0, NEG)
    m[:, 2, :] = np.where(pp >= ff, 0.0, NEG)
    inv = (np.float32(ROPE_THETA) ** (-np.arange(0, 16, 2, dtype=np.float32) / np.float32(16))).astype(np.float32)
    ang = (np.arange(S, dtype=np.float32)[:, None] * inv[None, :]).astype(np.float32)
    cosF = np.ones((128, S), np.float32)
    sinF = np.zeros((128, S), np.float32)
    for hh in range(2):
        cosF[hh * 64:hh * 64 + 8] = np.cos(ang).T
        cosF[hh * 64 + 8:hh * 64 + 16] = np.cos(ang).T
        sinF[hh * 64:hh * 64 + 8] = -np.sin(ang).T
        sinF[hh * 64 + 8:hh * 64 + 16] = np.sin(ang).T
    invi = (np.float32(ROPE_THETA) ** (-np.arange(0, 8, 2, dtype=np.float32) / np.float32(8))).astype(np.float32)
    angi = (np.arange(S, dtype=np.float32)[:, None] * invi[None, :]).astype(np.float32)
    cosI = np.ones((128, S), np.float32)
    sinI = np.zeros((128, S), np.float32)
    for hh in range(4):
        cosI[hh * 32:hh * 32 + 4] = np.cos(angi).T
        cosI[hh * 32 + 4:hh * 32 + 8] = np.cos(angi).T
        sinI[hh * 32:hh * 32 + 4] = -np.sin(angi).T
        sinI[hh * 32 + 4:hh * 32 + 8] = np.sin(angi).T
    return dict(ident=_bf(np.eye(128, dtype=np.float32)), identf=np.eye(128, dtype=np.float32), masks=_bf(m),
                cosF=cosF, sinF=sinF, cosI=cosI, sinI=sinI)


def _swap_cols(w, head_dim, half):
    n = w.shape[-1]
    idx = np.arange(n)
    d = idx % head_dim
    src = np.where(d < half, idx + half, np.where(d < 2 * half, idx - half, idx))
    return np.ascontiguousarray(w[..., src])


SHARED_KEYS = ("ln_g", "ln_b", "w_ada", "b_ada", "w_ffn_in", "w_ffn_out",
               "fox_w_in", "fox_f_bias", "fox_w_out", "swa_w_in", "swa_sinks", "swa_w_out",
               "diff_w_in", "diff_subln", "diff_w_out", "dsa_w_in", "dsa_kv_norm", "dsa_w_kv_up", "dsa_w_out")


def make_shared(inputs, S):
    shared = {k: np.ascontiguousarray(inputs[k]) for k in SHARED_KEYS}
    shared.update(make_consts(S))
    shared["swa_w_sw"] = _swap_cols(inputs["swa_w_in"][:, :, :1152], 64, 8)
    shared["diff_w_sw"] = _swap_cols(inputs["diff_w_in"][:, :, :2048], 64, 8)
    dw = inputs["dsa_w_in"]
    shared["dsa_w_sw"] = np.ascontiguousarray(np.concatenate(
        [_swap_cols(dw[:, :, 0:1024], 64, 8), _swap_cols(dw[:, :, 1152:1408], 32, 4), _swap_cols(dw[:, :, 1408:1440], 32, 4)],
        axis=-1))
    shared["dsa_w_kv_sw"] = _swap_cols(inputs["dsa_w_kv_up"][:, :, 0:64], 64, 8)
    shared["diff_lambda"] = np.ascontiguousarray(inputs["diff_lambda"].reshape(1, 256))
    return shared


def make_in_map(inputs, b, S, shared=None):
    m = dict(shared if shared is not None else make_shared(inputs, S))
    m["x"] = np.ascontiguousarray(inputs["x"][b, :S])
    m["cT"] = np.ascontiguousarray(inputs["c"][b].reshape(NCH, 128).T)
    return m


def kernel(**inputs):
    S = inputs["x"].shape[1]
    B = inputs["x"].shape[0]
    cfg = dict(layers=[0, 1, 2, 3], stop_after=None)
    nc, g = build(S, cfg)
    shared = make_shared(inputs, S)
    in_maps = [make_in_map(inputs, b, S, shared) for b in range(B)]
    res = run_bass_kernel_spmd(nc, in_maps, core_ids=list(range(B)))
    return np.stack([r["out"] for r in res.results], axis=0)
```

```python
import math
from contextlib import ExitStack

import numpy as np
import ml_dtypes
import concourse.bass as bass
import concourse.mybir as mybir
from concourse.bass_utils import run_bass_kernel_spmd

F32 = mybir.dt.float32
BF16 = mybir.dt.bfloat16
ALU = mybir.AluOpType
AF = mybir.ActivationFunctionType
AX = mybir.AxisListType

D = 1024
DFF = 2816
NCH = 8
NHC = 22
DEPTH = 4
ALPHA = (2 * DEPTH) ** 0.25
LN_EPS = 1e-5 / (ALPHA * ALPHA)
RMS_EPS = 1e-6
NEG = -30000.0
NSEM_DMA = 8
ROPE_THETA = 500000.0


class Prog:
    def __init__(self, nc, stack):
        self.nc = nc
        self.eng = {"pe": nc.tensor, "act": nc.scalar, "dve": nc.vector,
                    "pool": nc.gpsimd, "sp": nc.sync}
        self.ops = []
        self.csem = {e: stack.enter_context(nc.semaphore("c_" + e)) for e in self.eng}
        self.dsem = {e: [stack.enter_context(nc.semaphore("d_%s%d" % (e, k))) for k in range(NSEM_DMA)]
                     for e in ("sp", "act", "pool")}
        self.ccount = {e: 0 for e in self.eng}
        self.dcount = {e: 0 for e in self.dsem}
        self.known = {e: {} for e in self.eng}
        self.n_ops = 0
        self.n_wait = 0

    def op(self, eng, fn, reads=(), writes=(), dma=False):
        self.ops.append((eng, fn, tuple(reads), tuple(writes), dma))

    class _Cap:
        def __init__(self, prog):
            self.prog = prog
            self.ops = []

        def __enter__(self):
            self.saved = self.prog.ops
            self.prog.ops = self.ops
            return self

        def __exit__(self, *a):
            self.prog.ops = self.saved

    def capture(self):
        return Prog._Cap(self)

    def splice(self, cap):
        self.ops.extend(cap.ops)

    def pipelined(self, units, depth):
        out = []
        n = len(units)
        for i in range(min(depth, n)):
            out.append(units[i][0])
        for i in range(n):
            if i + depth < n:
                out.append(units[i + depth][0])
            out.append(units[i][1])
        return out

    def merge(self, la, lb):
        ta = sum(getattr(c, "cost", 1.0) for c in la) or 1.0
        tb = sum(getattr(c, "cost", 1.0) for c in lb) or 1.0
        na, nb = len(la), len(lb)
        ia = ib = 0
        ca = cb = 0.0
        while ia < na or ib < nb:
            if ib < nb and (ia >= na or cb / tb <= ca / ta):
                self.splice(lb[ib])
                cb += getattr(lb[ib], "cost", 1.0)
                ib += 1
            else:
                self.splice(la[ia])
                ca += getattr(la[ia], "cost", 1.0)
                ia += 1

    def _wait(self, e, s, v):
        key = id(s)
        if self.known[e].get(key, 0) >= v:
            return
        self.eng[e].wait_ge(s, v)
        self.known[e][key] = v
        self.n_wait += 1

    def flush(self):
        ops = self.ops
        self.ops = []
        n = len(ops)
        if n == 0:
            return
        last_w = {}
        readers = {}
        need = [None] * n
        signal = [False] * n
        for j, (eng, fn, reads, writes, dma) in enumerate(ops):
            d = set()
            for b in reads:
                if b in last_w:
                    d.add(last_w[b])
            for b in writes:
                if b in last_w:
                    d.add(last_w[b])
                for r in readers.get(b, ()):
                    d.add(r)
            d.discard(j)
            lst = []
            for i in d:
                ei, dmai = ops[i][0], ops[i][4]
                if ei == "pe" and eng == "pe" and not dmai and not dma:
                    continue
                lst.append(i)
                signal[i] = True
            need[j] = lst
            for b in reads:
                readers.setdefault(b, []).append(j)
            for b in writes:
                last_w[b] = j
                readers[b] = []
        last_c = {}
        last_d = {}
        for j, (eng, fn, reads, writes, dma) in enumerate(ops):
            if fn is None:
                continue
            if dma:
                last_d.setdefault(eng, []).append(j)
            else:
                last_c[eng] = j
        for j in range(n):
            if ops[j][4] and ops[j][1] is not None:
                signal[j] = True
        bar = []
        for e, j in last_c.items():
            signal[j] = True
            bar.append(j)
        for e, lst in last_d.items():
            for j in lst[-NSEM_DMA:]:
                signal[j] = True
                bar.append(j)
        sig = [None] * n
        for j in range(n):
            if not signal[j]:
                continue
            e, dma = ops[j][0], ops[j][4]
            if dma:
                k = self.dcount[e]
                self.dcount[e] += 1
                sig[j] = (self.dsem[e][k % NSEM_DMA], 16 * (k // NSEM_DMA + 1), 16)
            else:
                self.ccount[e] += 1
                sig[j] = (self.csem[e], self.ccount[e], 1)
        for j in range(n):
            e, fn, reads, writes, dma = ops[j]
            best = {}
            for i in need[j]:
                s, v, _ = sig[i]
                if id(s) not in best or best[id(s)][1] < v:
                    best[id(s)] = (s, v)
            for s, v in best.values():
                self._wait(e, s, v)
            if fn is None:
                continue
            ins = fn()
            if sig[j] is not None:
                ins.then_inc(sig[j][0], sig[j][2])
        for e in self.eng:
            for j in bar:
                self._wait(e, sig[j][0], sig[j][1])
        self.n_ops += n


def _bf(a):
    return np.ascontiguousarray(a).astype(ml_dtypes.bfloat16)


class Ctx:
    pass


def build(S, cfg):
    NT = S // 128
    NB = S // 512
    nc = bass.Bass("TRN2", target_bir_lowering=False)
    g = Ctx()
    g.nc = nc
    g.S, g.NT, g.NB = S, NT, NB

    def din(name, shape, dt=F32):
        return nc.dram_tensor(name, list(shape), dt, kind="ExternalInput").ap()

    x_ext = din("x", [S, D])
    cT = din("cT", [128, NCH])
    ln_g = din("ln_g", [DEPTH, 3, D])
    ln_b = din("ln_b", [DEPTH, 3, D])
    w_ada = din("w_ada", [DEPTH, D, 9 * D])
    b_ada = din("b_ada", [DEPTH, 9 * D])
    w_ffn_in = din("w_ffn_in", [DEPTH, 2, D, 2 * DFF])
    w_ffn_out = din("w_ffn_out", [DEPTH, 2, DFF, D])
    ident_d = din("ident", [128, 128], BF16)
    identf_d = din("identf", [128, 128], F32)
    masks_d = din("masks", [128, 3, 128], BF16)
    fox_w_in = din("fox_w_in", [1, D, 3088])
    fox_f_bias = din("fox_f_bias", [1, 16])
    fox_w_out = din("fox_w_out", [1, D, D])
    swa_w_in = din("swa_w_in", [1, D, 1280])
    swa_w_sw = din("swa_w_sw", [1, D, 1152])
    swa_sinks = din("swa_sinks", [1, 16])
    swa_w_out = din("swa_w_out", [1, D, D])
    diff_w_in = din("diff_w_in", [1, D, 3072])
    diff_w_sw = din("diff_w_sw", [1, D, 2048])
    diff_lambda = din("diff_lambda", [1, 256])
    diff_subln = din("diff_subln", [1, 128])
    diff_w_out = din("diff_w_out", [1, D, D])
    dsa_w_in = din("dsa_w_in", [1, D, 1448])
    dsa_w_sw = din("dsa_w_sw", [1, D, 1312])
    dsa_kv_norm = din("dsa_kv_norm", [1, 128])
    dsa_w_kv_up = din("dsa_w_kv_up", [1, 128, 128])
    dsa_w_kv_sw = din("dsa_w_kv_sw", [1, 128, 64])
    dsa_w_out = din("dsa_w_out", [1, D, D])
    cosI_d = din("cosI", [128, S])
    sinI_d = din("sinI", [128, S])
    QI_d = nc.dram_tensor("QI_d", [256, S], BF16).ap()
    KI_d = nc.dram_tensor("KI_d", [32, S], BF16).ap()
    WI_d = nc.dram_tensor("WI_d", [S, 8], F32).ap()
    cosF_d = din("cosF", [128, S])
    sinF_d = din("sinF", [128, S])
    VW = 66
    VW2 = 130
    skind = "ExternalOutput" if cfg.get("debug") else "Internal"
    QT_d = nc.dram_tensor("QT_d", [D, S], BF16, kind=skind).ap()
    KT_d = nc.dram_tensor("KT_d", [D, S], BF16, kind=skind).ap()
    V_d = nc.dram_tensor("V_d", [S, 16 * VW], BF16, kind=skind).ap()
    attnT_d = nc.dram_tensor("attnT_d", [D, S], BF16, kind=skind).ap()
    cq3_d = nc.dram_tensor("cq3_d", [16, 3, S], BF16, kind=skind).ap()
    ck_d = nc.dram_tensor("ck_d", [S, 16], F32, kind=skind).ap()
    out_ext = nc.dram_tensor("out", [S, D], F32, kind="ExternalOutput").ap()
    xs = nc.dram_tensor("xs", [S, D], F32).ap()
    mod_d = nc.dram_tensor("mod_d", [DEPTH, 9 * D], F32).ap()

    with ExitStack() as top:
        p = Prog(nc, top)
        g.p = p

        uid = [0]

        def sbuf(st, name, shape, dt):
            uid[0] += 1
            return st.enter_context(nc.sbuf_tensor("%s_%d" % (name, uid[0]), list(shape), dt))

        class _View:
            def __init__(self, t, shape):
                self.t, self.shape = t, shape
                n = 1
                for d in shape[1:]:
                    n *= d
                self.n = n

            def _base(self):
                a = self.t[0:self.shape[0], 0:self.n]
                if len(self.shape) == 3:
                    a = a.rearrange("p (a b) -> p a b", b=self.shape[2])
                return a

            def __getitem__(self, key):
                return self._base()[key]

        def psum(st, name, shape, dt):
            uid[0] += 1
            per = 512 if dt == F32 else 1024
            t = st.enter_context(nc.psum_tensor("%s_%d" % (name, uid[0]), [128, per], dt))
            return _View(t, list(shape))

        ident = sbuf(top, "ident_sb", [128, 128], BF16)
        p.op("sp", lambda: nc.sync.dma_start(out=ident[:], in_=ident_d), writes=["ident"], dma=True)
        identf = sbuf(top, "identf_sb", [128, 128], F32)
        p.op("sp", lambda: nc.sync.dma_start(out=identf[:], in_=identf_d), writes=["identf"], dma=True)
        masks = sbuf(top, "masks_sb", [128, 3, 128], BF16)
        p.op("sp", lambda: nc.sync.dma_start(out=masks[:], in_=masks_d), writes=["masks"], dma=True)
        p.flush()

        def make_mod_chunks(ph, L):
            cond = sbuf(ph, "cond", [128, NCH], F32)
            condb = sbuf(ph, "condb", [128, NCH], BF16)
            CB = 1152
            NCB = 9 * D // CB
            wblk = [sbuf(ph, "wada%d" % i, [128, NCH, CB], BF16) for i in range(2)]
            brow = [sbuf(ph, "brow%d" % i, [1, CB], F32) for i in range(2)]
            mrow = [sbuf(ph, "mrow%d" % i, [1, CB], F32) for i in range(2)]
            pm = psum(ph, "pm", [1, 384], F32)
            p.op("sp", lambda: nc.sync.dma_start(out=cond[:], in_=cT), writes=["cond"], dma=True)
            p.op("act", lambda: nc.scalar.activation(out=condb[:], in_=cond[:], func=AF.Silu),
                 reads=["cond"], writes=["condb"])

            def loads(cb):
                wb, br = wblk[cb % 2], brow[cb % 2]
                src_ = w_ada[L, :, cb * CB:(cb + 1) * CB].rearrange("(k p) n -> p k n", p=128)
                p.op("pool", lambda: nc.gpsimd.dma_start(out=wb[:], in_=src_), writes=[("wblk", cb % 2)], dma=True)
                p.op("sp", lambda: nc.sync.dma_start(out=br[:], in_=b_ada[L:L + 1, cb * CB:(cb + 1) * CB]),
                     writes=[("brow", cb % 2)], dma=True)

            def chunk(cb):
                if cb == 0:
                    loads(0)
                    loads(1)
                    return
                cb -= 1
                wb, br, mr = wblk[cb % 2], brow[cb % 2], mrow[cb % 2]
                wk, bk, mk = ("wblk", cb % 2), ("brow", cb % 2), ("mrow", cb % 2)
                for sbi in range(CB // 384):
                    for k in range(NCH):
                        p.op("pe", lambda k=k, sbi=sbi: nc.tensor.matmul(
                            pm[:], condb[:, k:k + 1], wb[:, k, sbi * 384:(sbi + 1) * 384],
                            start=(k == 0), stop=(k == NCH - 1)),
                            reads=["condb", wk], writes=["pm"])
                    p.op("dve", lambda sbi=sbi: nc.vector.tensor_tensor(
                        out=mr[:, sbi * 384:(sbi + 1) * 384], in0=pm[:], in1=br[:, sbi * 384:(sbi + 1) * 384], op=ALU.add),
                        reads=["pm", bk], writes=[mk])
                p.op("sp", lambda: nc.sync.dma_start(out=mod_d[L:L + 1, cb * CB:(cb + 1) * CB], in_=mr[:]),
                     reads=[mk], writes=[("mod_d", L, cb)], dma=True)
                if cb + 2 < NCB:
                    loads(cb + 2)
            return [lambda cb=cb: chunk(cb) for cb in range(NCB + 1)]

        def phase_mod(layers):
            for L in layers:
                with ExitStack() as ph:
                    for ch in make_mod_chunks(ph, L):
                        ch()
                    p.flush()

        def load_in_vectors(ph, L, j):
            s1 = sbuf(ph, "s1", [128, NCH], F32)
            sh = sbuf(ph, "sh", [128, NCH], F32)
            o = 3 * j * D
            p.op("sp", lambda: nc.sync.dma_start(
                out=sh[:], in_=mod_d[L, o:o + D].rearrange("(c p) -> p c", p=128),
                allow_slow_non_contiguous=True), writes=["sh"], dma=True)
            p.op("sp", lambda: nc.sync.dma_start(
                out=s1[:], in_=mod_d[L, o + D:o + 2 * D].rearrange("(c p) -> p c", p=128),
                allow_slow_non_contiguous=True), writes=["s1"], dma=True)
            p.op("dve", lambda: nc.vector.tensor_scalar(out=s1[:], in0=s1[:], scalar1=1.0, scalar2=None,
                                                       op0=ALU.add), reads=["s1"], writes=["s1"])
            return s1, sh

        def load_out_vectors(ph, L, j, rw):
            G = sbuf(ph, "G", [128, D], F32)
            gb = sbuf(ph, "gb", [128, D], F32)
            bb = sbuf(ph, "bb", [128, D], F32)
            o = 3 * j * D
            p.op("sp", lambda: nc.sync.dma_start(
                out=G[:], in_=mod_d[L:L + 1, o + 2 * D:o + 3 * D].partition_broadcast(128)),
                writes=["G"], dma=True)
            p.op("sp", lambda: nc.sync.dma_start(
                out=gb[:], in_=ln_g[L, j:j + 1, :].partition_broadcast(128)), writes=["gb"], dma=True)
            p.op("sp", lambda: nc.sync.dma_start(
                out=bb[:], in_=ln_b[L, j:j + 1, :].partition_broadcast(128)), writes=["bb"], dma=True)
            p.op("dve", lambda: nc.vector.tensor_scalar(out=G[:], in0=G[:], scalar1=1.0, scalar2=rw / ALPHA,
                                                       op0=ALU.add, op1=ALU.mult), reads=["G"], writes=["G"])
            return G, gb, bb

        def load_mod_vectors(ph, L, j, rw):
            s1 = sbuf(ph, "s1", [128, NCH], F32)
            sh = sbuf(ph, "sh", [128, NCH], F32)
            G = sbuf(ph, "G", [128, D], F32)
            gb = sbuf(ph, "gb", [128, D], F32)
            bb = sbuf(ph, "bb", [128, D], F32)
            o = 3 * j * D
            p.op("sp", lambda: nc.sync.dma_start(
                out=sh[:], in_=mod_d[L, o:o + D].rearrange("(c p) -> p c", p=128),
                allow_slow_non_contiguous=True), writes=["sh"], dma=True)
            p.op("sp", lambda: nc.sync.dma_start(
                out=s1[:], in_=mod_d[L, o + D:o + 2 * D].rearrange("(c p) -> p c", p=128),
                allow_slow_non_contiguous=True), writes=["s1"], dma=True)
            p.op("sp", lambda: nc.sync.dma_start(
                out=G[:], in_=mod_d[L:L + 1, o + 2 * D:o + 3 * D].partition_broadcast(128)),
                writes=["G"], dma=True)
            p.op("sp", lambda: nc.sync.dma_start(
                out=gb[:], in_=ln_g[L, j:j + 1, :].partition_broadcast(128)), writes=["gb"], dma=True)
            p.op("sp", lambda: nc.sync.dma_start(
                out=bb[:], in_=ln_b[L, j:j + 1, :].partition_broadcast(128)), writes=["bb"], dma=True)
            p.op("dve", lambda: nc.vector.tensor_scalar(out=s1[:], in0=s1[:], scalar1=1.0, scalar2=None,
                                                       op0=ALU.add), reads=["s1"], writes=["s1"])
            p.op("dve", lambda: nc.vector.tensor_scalar(out=G[:], in0=G[:], scalar1=1.0, scalar2=rw / ALPHA,
                                                       op0=ALU.add, op1=ALU.mult), reads=["G"], writes=["G"])
            return s1, sh, G, gb, bb

        def make_epilogue(ph, G, gb, bb, src, dst):
            xres = [sbuf(ph, "xres%d" % i, [128, D], F32) for i in range(2)]
            tmpA = [sbuf(ph, "tmpA%d" % i, [128, D], F32) for i in range(2)]
            st6 = [sbuf(ph, "st6_%d" % i, [128, 2, 6], F32) for i in range(2)]
            mv = [sbuf(ph, "mv%d" % i, [128, 2], F32) for i in range(2)]
            sm = [sbuf(ph, "sm%d" % i, [128, 4], F32) for i in range(2)]
            nh = sbuf(ph, "neghalf", [128, 1], F32)
            p.op("pool", lambda: nc.gpsimd.memset(nh[:], -0.5), writes=["neghalf"])
            cnt = [0]
            pending = [None]

            def stage_a(t, po_list, par):
                xr, ta, s6, m, s = xres[par], tmpA[par], st6[par], mv[par], sm[par]
                kx, kt, ks = ("xres", par), ("tmpA", par), ("stat", par)
                p.op("sp", lambda: nc.sync.dma_start(out=xr[:], in_=src[t * 128:(t + 1) * 128, :]),
                     reads=[("x", t)], writes=[kx], dma=True)
                for (pa, pk, c0, w) in po_list:
                    p.op("dve", lambda pa=pa, c0=c0, w=w: nc.vector.tensor_tensor(
                        out=ta[:, c0:c0 + w], in0=pa, in1=G[:, c0:c0 + w], op=ALU.mult),
                        reads=[pk, "G"], writes=[kt])
                p.op("dve", lambda: nc.vector.tensor_tensor(out=ta[:], in0=ta[:], in1=xr[:], op=ALU.add),
                     reads=[kt, kx], writes=[kt])
                for hh in range(2):
                    p.op("dve", lambda hh=hh: nc.vector.bn_stats(out=s6[:, hh, :], in_=ta[:, hh * 512:(hh + 1) * 512]),
                         reads=[kt], writes=[ks])
                p.op("dve", lambda: nc.vector.bn_aggr(out=m[:], in_=s6[:].rearrange("p a b -> p (a b)")),
                     reads=[ks], writes=[ks])
                p.op("pool", lambda: nc.gpsimd.tensor_scalar(out=s[:, 0:1], in0=m[:, 1:2], scalar1=LN_EPS,
                                                            scalar2=None, op0=ALU.add),
                     reads=[ks], writes=[ks])
                p.op("pool", lambda: nc.gpsimd.tensor_tensor(out=s[:, 1:2], in0=s[:, 0:1], in1=nh[:], op=ALU.pow),
                     reads=[ks, "neghalf"], writes=[ks])
                p.op("pool", lambda: nc.gpsimd.tensor_scalar(out=s[:, 2:3], in0=m[:, 0:1], scalar1=-1.0,
                                                            scalar2=s[:, 1:2], op0=ALU.mult, op1=ALU.mult),
                     reads=[ks], writes=[ks])

            def stage_b(t, par):
                xr, ta, s = xres[par], tmpA[par], sm[par]
                kx, kt, ks = ("xres", par), ("tmpA", par), ("stat", par)
                p.op("act", lambda: nc.scalar.activation(out=xr[:], in_=ta[:], func=AF.Identity,
                                                         bias=s[:, 2:3], scale=s[:, 1:2]),
                     reads=[kt, ks, kx], writes=[kx])
                p.op("dve", lambda: nc.vector.tensor_tensor(out=xr[:], in0=xr[:], in1=gb[:], op=ALU.mult),
                     reads=[kx, "gb"], writes=[kx])
                p.op("pool", lambda: nc.gpsimd.tensor_tensor(out=xr[:], in0=xr[:], in1=bb[:], op=ALU.add),
                     reads=[kx, "bb"], writes=[kx])
                p.op("sp", lambda: nc.sync.dma_start(out=dst[t * 128:(t + 1) * 128, :], in_=xr[:]),
                     reads=[kx], writes=[("x", t), ("xo", t)], dma=True)

            def epi(t, po_list):
                par = cnt[0] % 2
                cnt[0] += 1
                stage_a(t, po_list, par)
                if pending[0] is not None:
                    stage_b(*pending[0])
                pending[0] = (t, par)

            def finish():
                if pending[0] is not None:
                    stage_b(*pending[0])
                    pending[0] = None
            epi.finish = finish
            return epi

        def make_hT_builder(ph, s1, sh, src, width):
            nt = width // 128
            xb = sbuf(ph, "xb", [128, nt, D], BF16)
            hT = sbuf(ph, "hT", [128, NCH, width], BF16)
            pT = [psum(ph, "pT%d" % i, [128, width], BF16) for i in range(2)]

            pre = set()

            def loads(tb):
                for t in range(nt):
                    tt = tb * nt + t
                    p.op("pool", lambda t=t, tt=tt: nc.gpsimd.dma_start(out=xb[:, t, :], in_=src[tt * 128:(tt + 1) * 128, :]),
                         reads=[("x", tt)], writes=[("xb", t)], dma=True)

            def preload(tb):
                pre.add(tb)
                loads(tb)

            def mk(tb):
                if tb in pre:
                    pre.discard(tb)
                else:
                    loads(tb)
                for c in range(NCH):
                    pt = pT[c % 2]
                    pk = ("pT", c % 2)
                    for t in range(nt):
                        p.op("pe", lambda c=c, t=t, pt=pt: nc.tensor.transpose(
                            pt[:, t * 128:(t + 1) * 128], xb[:, t, c * 128:(c + 1) * 128], ident[:]),
                            reads=[("xb", t), "ident"], writes=[pk])
                    p.op("act", lambda c=c, pt=pt: nc.scalar.activation(
                        out=hT[:, c, :], in_=pt[:], func=AF.Identity, bias=sh[:, c:c + 1], scale=s1[:, c:c + 1]),
                        reads=[pk, "s1", "sh"], writes=[("hT", c)])
            mk.preload = preload
            return hT, mk

        def phase_ffn(L, j, src, dst):
            fi = 0 if j == 0 else 1
            with ExitStack() as ph:
                win = sbuf(ph, "win", [128, NCH, 2 * DFF], BF16)
                wout = sbuf(ph, "wout", [128, NHC, D], BF16)
                act = sbuf(ph, "act", [128, NHC, 512], BF16)
                sg = [sbuf(ph, "sg%d" % i, [128, 512], BF16) for i in range(2)]
                s1, sh, G, gb, bb = load_mod_vectors(ph, L, j, 0.5)
                hT, mk_hT = make_hT_builder(ph, s1, sh, src, 512)
                mk_hT.preload(0)
                epi = make_epilogue(ph, G, gb, bb, src, dst)
                pg = [psum(ph, "pg%d" % i, [128, 512], F32) for i in range(2)]
                pu = [psum(ph, "pu%d" % i, [128, 512], F32) for i in range(2)]
                po = [psum(ph, "po%d" % i, [128, 512], F32) for i in range(2)]
                GRP = [(0, 2), (2, 6), (6, 14), (14, 22)]
                grp_of = {}
                for gi, (ma, mb) in enumerate(GRP):
                    for m in range(ma, mb):
                        grp_of[m] = gi
                    for gu in range(2):
                        c0 = gu * DFF + ma * 128
                        wd_ = (mb - ma) * 128
                        for k in range(NCH):
                            p.op("pool", lambda c0=c0, k=k, wd_=wd_: nc.gpsimd.dma_start(
                                out=win[:, k, c0:c0 + wd_], in_=w_ffn_in[L, fi, k * 128:(k + 1) * 128, c0:c0 + wd_]),
                                writes=[("win", gi)], dma=True)
                for m0 in range(0, NHC, 2):
                    p.op("pool", lambda m0=m0: nc.gpsimd.dma_start(
                        out=wout[:, m0:m0 + 2, :],
                        in_=w_ffn_out[L, fi, m0 * 128:(m0 + 2) * 128, :].rearrange("(m p) n -> p m n", p=128)),
                        writes=[("wout", m0 // 2)], dma=True)
                mk_hT(0)
                for tb in range(NB):
                    for m in range(NHC):
                        grp = grp_of[m]
                        a, b = pg[m % 2], pu[m % 2]
                        ka, kb = ("pg", m % 2), ("pu", m % 2)
                        for k in range(NCH):
                            p.op("pe", lambda a=a, k=k, m=m: nc.tensor.matmul(
                                a[:], win[:, k, m * 128:(m + 1) * 128], hT[:, k, :],
                                start=(k == 0), stop=(k == NCH - 1)),
                                reads=[("win", grp), ("hT", k)], writes=[ka])
                        for k in range(NCH):
                            p.op("pe", lambda b=b, k=k, m=m: nc.tensor.matmul(
                                b[:], win[:, k, DFF + m * 128:DFF + (m + 1) * 128], hT[:, k, :],
                                start=(k == 0), stop=(k == NCH - 1)),
                                reads=[("win", grp), ("hT", k)], writes=[kb])
                        s = sg[m % 2]
                        p.op("act", lambda a=a, s=s: nc.scalar.activation(out=s[:], in_=a[:], func=AF.Silu),
                             reads=[ka], writes=[("sg", m % 2)])
                        p.op("dve", lambda b=b, s=s, m=m: nc.vector.tensor_tensor(
                            out=act[:, m, :], in0=b[:], in1=s[:], op=ALU.mult),
                            reads=[kb, ("sg", m % 2)], writes=[("act", m)])
                    if tb + 1 < NB:
                        mk_hT(tb + 1)
                    for t in range(4):
                        tt = tb * 4 + t
                        for hh in range(2):
                            for m in range(NHC):
                                p.op("pe", lambda hh=hh, m=m, t=t: nc.tensor.matmul(
                                    po[hh][:], act[:, m, t * 128:(t + 1) * 128], wout[:, m, hh * 512:(hh + 1) * 512],
                                    start=(m == 0), stop=(m == NHC - 1)),
                                    reads=[("act", m), ("wout", m // 2)], writes=[("po", hh)])
                        epi(tt, [(po[0][:], ("po", 0), 0, 512), (po[1][:], ("po", 1), 512, 512)])
                epi.finish()
                p.flush()


        def phase_outproj(L, w_out_d, src, dst, mod_next=None):
            with ExitStack() as ph:
                mod_ch = make_mod_chunks(ph, mod_next) if mod_next is not None else []
                per_blk = (len(mod_ch) + NB - 1) // NB
                wo = sbuf(ph, "wo", [128, NCH, D], BF16)
                for c0 in range(0, NCH, 2):
                    p.op("pool", lambda c0=c0: nc.gpsimd.dma_start(
                        out=wo[:, c0:c0 + 2, :], in_=w_out_d[c0 * 128:(c0 + 2) * 128, :].rearrange("(c p) n -> p c n", p=128)),
                        writes=[("wo", c0 // 2)], dma=True)
                G, gb, bb = load_out_vectors(ph, L, 1, 1.0)
                epi = make_epilogue(ph, G, gb, bb, src, dst)
                aT = [sbuf(ph, "aT%d" % i, [128, NCH, 512], BF16) for i in range(2)]
                po = [psum(ph, "po%d" % i, [128, 512], F32) for i in range(4)]
                for tb in range(NB):
                    a = aT[tb % 2]
                    ka = ("aT", tb % 2)
                    p.op("sp", lambda a=a, tb=tb: nc.sync.dma_start(
                        out=a[:], in_=attnT_d[:, tb * 512:(tb + 1) * 512].rearrange("(c p) s -> p c s", p=128)),
                        writes=[ka], dma=True)
                    for t in range(4):
                        tt = tb * 4 + t
                        pl = []
                        for hh in range(2):
                            pi = (tt % 2) * 2 + hh
                            for c in range(NCH):
                                p.op("pe", lambda pi=pi, c=c, t=t, a=a, hh=hh: nc.tensor.matmul(
                                    po[pi][:], a[:, c, t * 128:(t + 1) * 128], wo[:, c, hh * 512:(hh + 1) * 512],
                                    start=(c == 0), stop=(c == NCH - 1)),
                                    reads=[ka, ("wo", c // 2)], writes=[("po", pi)])
                            pl.append((po[pi][:], ("po", pi), hh * 512, 512))
                        epi(tt, pl)
                    for _ in range(per_blk):
                        if mod_ch:
                            mod_ch.pop(0)()
                while mod_ch:
                    mod_ch.pop(0)()
                epi.finish()
                p.flush()

        def phase_fox_a(L, r, src):
            with ExitStack() as ph:
                w = sbuf(ph, "wfox", [128, NCH, 3088], BF16)
                for grp, (a, b) in enumerate([(0, 1024), (1024, 2048), (2048, 3072), (3072, 3088)]):
                    for k in range(NCH):
                        p.op("pool", lambda a=a, b=b, k=k: nc.gpsimd.dma_start(
                            out=w[:, k, a:b], in_=fox_w_in[r, k * 128:(k + 1) * 128, a:b]),
                            writes=[("w", grp)], dma=True)
                s1, sh = load_in_vectors(ph, L, 1)
                hT, mk_hT = make_hT_builder(ph, s1, sh, src, 512)
                qst = [sbuf(ph, "qst%d" % i, [128, 512], BF16) for i in range(4)]
                vst = [sbuf(ph, "vst%d" % i, [128, 16, VW], BF16) for i in range(2)]
                for i in range(2):
                    p.op("pool", lambda i=i: nc.gpsimd.memset(vst[i][:, :, 64:VW], 1.0), writes=[("vst", i)])
                fb = sbuf(ph, "fb", [16, 1], F32)
                p.op("sp", lambda: nc.sync.dma_start(out=fb[:], in_=fox_f_bias[r, :].rearrange("(h o) -> h o", o=1)),
                     writes=["fb"], dma=True)
                p.op("dve", lambda: nc.vector.tensor_scalar(out=fb[:], in0=fb[:], scalar1=-1.0, scalar2=None,
                                                           op0=ALU.mult), reads=["fb"], writes=["fb"])
                ones16 = sbuf(ph, "ones16", [16, 512], F32)
                p.op("pool", lambda: nc.gpsimd.memset(ones16[:], 1.0), writes=["ones16"])
                cumneg = sbuf(ph, "cumneg", [16, S], F32)
                ef = sbuf(ph, "ef", [16, 512], F32)
                r1 = sbuf(ph, "r1", [16, 512], F32)
                r2 = sbuf(ph, "r2", [16, 512], F32)
                c3 = sbuf(ph, "c3", [16, 3, 512], BF16)
                ckst = sbuf(ph, "ckst", [128, NT, 16], F32)
                pq = [psum(ph, "pq%d" % i, [128, 512], F32) for i in range(4)]
                pf = psum(ph, "pf", [16, 512], F32)
                ptr = psum(ph, "ptr", [128, 16], F32)
                cnt = [0]

                def proj_fm(col0, grp, c):
                    i = cnt[0] % 4
                    cnt[0] += 1
                    for k in range(NCH):
                        p.op("pe", lambda i=i, k=k: nc.tensor.matmul(
                            pq[i][:], w[:, k, col0:col0 + 128], hT[:, k, :], start=(k == 0), stop=(k == NCH - 1)),
                            reads=[("w", grp), ("hT", k)], writes=[("pq", i)])
                    return i

                for tb in range(NB):
                    mk_hT(tb)
                    for which, (base, dram) in enumerate([(0, QT_d), (1024, KT_d)]):
                        for c in range(NCH):
                            i = proj_fm(base + c * 128, which, c)
                            if which == 0:
                                p.op("act", lambda i=i: nc.scalar.copy(out=qst[i][:], in_=pq[i][:]),
                                     reads=[("pq", i)], writes=[("qst", i)])
                            else:
                                p.op("dve", lambda i=i: nc.vector.tensor_copy(out=qst[i][:], in_=pq[i][:]),
                                     reads=[("pq", i)], writes=[("qst", i)])
                            p.op("sp", lambda i=i, c=c, dram=dram, tb=tb: nc.sync.dma_start(
                                out=dram[c * 128:(c + 1) * 128, tb * 512:(tb + 1) * 512], in_=qst[i][:]),
                                reads=[("qst", i)], writes=[("qkd", which, c, tb)], dma=True)
                    for t in range(4):
                        tt = tb * 4 + t
                        vs = vst[tt % 2]
                        for hh in range(2):
                            i = cnt[0] % 4
                            cnt[0] += 1
                            for k in range(NCH):
                                p.op("pe", lambda i=i, k=k, t=t, hh=hh: nc.tensor.matmul(
                                    pq[i][:], hT[:, k, t * 128:(t + 1) * 128], w[:, k, 2048 + hh * 512:2048 + (hh + 1) * 512],
                                    start=(k == 0), stop=(k == NCH - 1)),
                                    reads=[("w", 2), ("hT", k)], writes=[("pq", i)])
                            eng = "dve" if hh == 0 else "act"
                            if hh == 0:
                                p.op("dve", lambda i=i, vs=vs, hh=hh: nc.vector.tensor_copy(
                                    out=vs[:, hh * 8:(hh + 1) * 8, 0:64], in_=pq[i][:].rearrange("p (h d) -> p h d", d=64)),
                                    reads=[("pq", i)], writes=[("vst", tt % 2)])
                            else:
                                p.op("act", lambda i=i, vs=vs, hh=hh: nc.scalar.copy(
                                    out=vs[:, hh * 8:(hh + 1) * 8, 0:64], in_=pq[i][:].rearrange("p (h d) -> p h d", d=64)),
                                    reads=[("pq", i)], writes=[("vst", tt % 2)])
                        p.op("sp", lambda vs=vs, tt=tt: nc.sync.dma_start(
                            out=V_d[tt * 128:(tt + 1) * 128, :], in_=vs[:].rearrange("p h d -> p (h d)")),
                            reads=[("vst", tt % 2)], writes=[("vd", tt)], dma=True)
                    for k in range(NCH):
                        p.op("pe", lambda k=k: nc.tensor.matmul(
                            pf[:], w[:, k, 3072:3088], hT[:, k, :], start=(k == 0), stop=(k == NCH - 1)),
                            reads=[("w", 3), ("hT", k)], writes=["pf"])
                    p.op("act", lambda: nc.scalar.activation(out=ef[:], in_=pf[:], func=AF.Exp, bias=fb[:], scale=-1.0),
                         reads=["pf", "fb"], writes=["ef"])
                    p.op("act", lambda: nc.scalar.activation(out=ef[:], in_=ef[:], func=AF.Ln, bias=1.0, scale=1.0),
                         reads=["ef"], writes=["ef"])
                    blk = slice(tb * 512, (tb + 1) * 512)
                    init = 0.0 if tb == 0 else cumneg[:, tb * 512 - 1:tb * 512]
                    p.op("dve", lambda blk=blk, init=init: nc.vector.tensor_tensor_scan(
                        out=cumneg[:, blk], data0=ones16[:], data1=ef[:], initial=init, op0=ALU.mult, op1=ALU.add),
                        reads=["ef", "ones16", "cumneg"], writes=["cumneg"])
                    p.op("dve", lambda blk=blk: nc.vector.tensor_scalar(out=r1[:], in0=cumneg[:, blk], scalar1=-8.0,
                                                                      scalar2=None, op0=ALU.mult),
                         reads=["cumneg"], writes=["r1"])
                    p.op("dve", lambda: nc.vector.tensor_copy(out=c3[:, 0, :], in_=r1[:]), reads=["r1"], writes=["c3"])
                    p.op("dve", lambda: nc.vector.tensor_tensor(out=r2[:], in0=r1[:], in1=c3[:, 0, :], op=ALU.subtract),
                         reads=["r1", "c3"], writes=["r2"])
                    p.op("dve", lambda: nc.vector.tensor_copy(out=c3[:, 1, :], in_=r2[:]), reads=["r2"], writes=["c3"])
                    p.op("dve", lambda: nc.vector.tensor_tensor(out=r1[:], in0=r2[:], in1=c3[:, 1, :], op=ALU.subtract),
                         reads=["r2", "c3"], writes=["r1"])
                    p.op("dve", lambda: nc.vector.tensor_copy(out=c3[:, 2, :], in_=r1[:]), reads=["r1"], writes=["c3"])
                    p.op("sp", lambda blk=blk: nc.sync.dma_start(out=cq3_d[:, :, blk], in_=c3[:]),
                         reads=["c3"], writes=[("cq3d", tb)], dma=True)
                    for t in range(4):
                        tt = tb * 4 + t
                        p.op("pe", lambda tt=tt: nc.tensor.transpose(
                            ptr[:], cumneg[:, tt * 128:(tt + 1) * 128], identf[0:16, 0:16]),
                            reads=["cumneg", "identf"], writes=["ptr"])
                        p.op("act", lambda tt=tt: nc.scalar.copy(out=ckst[:, tt, :], in_=ptr[:]),
                             reads=["ptr"], writes=["ckst"])
                p.op("sp", lambda: nc.sync.dma_start(out=ck_d.rearrange("(t p) h -> p t h", p=128), in_=ckst[:]),
                     reads=["ckst"], writes=["ckd"], dma=True)
                p.flush()

        def phase_fox_b(L, r):
            with ExitStack() as ph:
                Vall = sbuf(ph, "Vall", [128, NT, 16 * VW], BF16)
                ckT = sbuf(ph, "ckT", [128, NT, 16], F32)
                Qx = [sbuf(ph, "Qx%d" % i, [67, S], BF16) for i in range(2)]
                Kx = [sbuf(ph, "Kx%d" % i, [67, S], BF16) for i in range(2)]
                pt = [sbuf(ph, "pt%d" % i, [128, 512], BF16) for i in range(3)]
                apair = [sbuf(ph, "apair%d" % i, [128, NT, 128], BF16) for i in range(2)]
                rec = [sbuf(ph, "rec%d" % i, [128, 4], F32) for i in range(2)]
                tst = [sbuf(ph, "tst%d" % i, [128, 512], BF16) for i in range(2)]
                ps = [psum(ph, "ps%d" % i, [128, 512], F32) for i in range(3)]
                pacc = [psum(ph, "pacc%d" % i, [128, 4, 65], F32) for i in range(2)]
                ptr = psum(ph, "ptrb", [128, 512], BF16)
                for t0 in range(0, NT, 4):
                    p.op("sp", lambda t0=t0: nc.sync.dma_start(
                        out=Vall[:, t0:t0 + 4, :], in_=V_d[t0 * 128:(t0 + 4) * 128, :].rearrange("(t p) f -> p t f", p=128)),
                        writes=["Vall"], dma=True)
                p.op("sp", lambda: nc.sync.dma_start(out=ckT[:], in_=ck_d.rearrange("(t p) h -> p t h", p=128)),
                     writes=["ckT"], dma=True)
                for i in range(2):
                    p.op("pool", lambda i=i: nc.gpsimd.memset(Kx[i][64:67, :], 1.0), writes=[("Kx1", i)])
                cnt = 0
                fin = 0
                units = []

                def fox_loads(h):
                    par = h % 2
                    q_, k_ = Qx[par], Kx[par]
                    p.op("sp", lambda: nc.sync.dma_start(out=q_[0:64, :], in_=QT_d[h * 64:(h + 1) * 64, :]),
                         writes=[("Qx", par)], dma=True)
                    p.op("sp", lambda: nc.sync.dma_start(out=q_[64:67, :], in_=cq3_d[h]),
                         writes=[("Qx", par)], dma=True)
                    p.op("sp", lambda: nc.sync.dma_start(out=k_[0:64, :], in_=KT_d[h * 64:(h + 1) * 64, :]),
                         writes=[("Kx", par)], dma=True)

                fox_loads(0)
                for h in range(16):
                    par = h % 2
                    hp = h // 2
                    q_, k_ = Qx[par], Kx[par]
                    rq = [("Qx", par), ("Kx", par), ("Kx1", par)]
                    for qb in range(NB):
                        pa = pacc[fin % 2]
                        kpa = ("pacc", fin % 2)
                        rc = rec[fin % 2]
                        krc = ("rec", fin % 2)
                        fin += 1
                        for j in range(4 * qb + 4):
                            c0 = max(0, j - 4 * qb) * 128
                            i = cnt % 3
                            cnt += 1
                            psi, pti = ps[i], pt[i]
                            kps, kpt = ("ps", i), ("pt", i)
                            ks = slice(j * 128, (j + 1) * 128)
                            q0 = qb * 512
                            with p.capture() as front:
                                if qb == 0 and j == 0 and h + 1 < 16:
                                    fox_loads(h + 1)
                                if j >= 4 * qb:
                                    p.op("pe", lambda psi=psi, k_=k_, q_=q_, ks=ks, c0=c0, q0=q0: nc.tensor.matmul(
                                        psi[:, c0:c0 + 128], k_[:, ks], q_[:, q0 + c0:q0 + c0 + 128], start=True, stop=False,
                                        skip_group_check=True), reads=rq, writes=[kps])
                                    p.op("pe", lambda psi=psi, c0=c0: nc.tensor.matmul(
                                        psi[:, c0:c0 + 128], ident[:], masks[:, 0, :], start=False, stop=True,
                                        skip_group_check=True), reads=["ident", "masks"], writes=[kps])
                                    if c0 + 128 < 512:
                                        p.op("pe", lambda psi=psi, k_=k_, q_=q_, ks=ks, c0=c0, q0=q0: nc.tensor.matmul(
                                            psi[:, c0 + 128:512], k_[:, ks], q_[:, q0 + c0 + 128:q0 + 512], start=False, stop=True,
                                            skip_group_check=True), reads=rq, writes=[kps])
                                else:
                                    p.op("pe", lambda psi=psi, k_=k_, q_=q_, ks=ks, q0=q0: nc.tensor.matmul(
                                        psi[:, :], k_[:, ks], q_[:, q0:q0 + 512], start=True, stop=True),
                                        reads=rq, writes=[kps])
                            with p.capture() as back:
                                p.op("act", lambda psi=psi, pti=pti, c0=c0, j=j, h=h: nc.scalar.activation(
                                    out=pti[:, c0:512], in_=psi[:, c0:512], func=AF.Exp, bias=ckT[:, j, h:h + 1], scale=0.125),
                                    reads=[kps, "ckT"], writes=[kpt])
                                for t in range(c0 // 128, 4):
                                    p.op("pe", lambda pa=pa, pti=pti, t=t, j=j, h=h, qb=qb: nc.tensor.matmul(
                                        pa[:, t, :], pti[:, t * 128:(t + 1) * 128], Vall[:, j, h * VW:h * VW + 65],
                                        start=(j == 0 and t == 0), stop=(j == 4 * qb + t), skip_group_check=True),
                                        reads=[kpt, "Vall"], writes=[kpa])
                                if j == 4 * qb + 3:
                                    ap_ = apair[hp % 2]
                                    kap = ("apair", hp % 2)
                                    p.op("dve", lambda rc=rc, pa=pa: nc.vector.reciprocal(out=rc[:], in_=pa[:, :, 64]),
                                         reads=[kpa], writes=[krc])
                                    for t in range(4):
                                        p.op("dve", lambda ap_=ap_, pa=pa, rc=rc, t=t, qb=qb, par=par: nc.vector.tensor_scalar(
                                            out=ap_[:, qb * 4 + t, par * 64:(par + 1) * 64], in0=pa[:, t, 0:64],
                                            scalar1=rc[:, t:t + 1], scalar2=None, op0=ALU.mult),
                                            reads=[kpa, krc], writes=[kap])
                                    if par == 1 and qb == NB - 1:
                                        emit_pair_transposes(ap_, kap, ptr, tst, hp * 128)
                            units.append((front, back))
                for cap in p.pipelined(units, 2):
                    p.splice(cap)
                p.flush()

        def mixer_fox(L, r, src, dst, mod_next=None):
            phase_fox_a(L, r, src)
            phase_fox_b(L, r)
            phase_outproj(L, fox_w_out[r], src, dst, mod_next)


        def make_rope_proj(ph, w, wsw, hT, pq, wkey, wskey):
            t1 = [sbuf(ph, "rt1_%d" % i, [128, 512], F32) for i in range(2)]
            t2 = [sbuf(ph, "rt2_%d" % i, [128, 512], F32) for i in range(2)]
            cnt = [0]

            def fn(col0, col0s, cs, sn, cskeys, out_ap, out_key, M=128):
                i = cnt[0] % 2
                cnt[0] += 1
                pa, pb = pq[2 * i], pq[2 * i + 1]
                ka, kb = ("pq", 2 * i), ("pq", 2 * i + 1)
                for k in range(NCH):
                    p.op("pe", lambda k=k: nc.tensor.matmul(pa[0:M, :], w[:, k, col0:col0 + M], hT[:, k, :],
                                                             start=(k == 0), stop=(k == NCH - 1)),
                         reads=[wkey, ("hT", k)], writes=[ka])
                for k in range(NCH):
                    p.op("pe", lambda k=k: nc.tensor.matmul(pb[0:M, :], wsw[:, k, col0s:col0s + M], hT[:, k, :],
                                                             start=(k == 0), stop=(k == NCH - 1)),
                         reads=[wskey, ("hT", k)], writes=[kb])
                p.op("dve", lambda: nc.vector.tensor_tensor(out=t1[i][0:M, :], in0=pa[0:M, :], in1=cs, op=ALU.mult),
                     reads=[ka] + cskeys, writes=[("rt1", i)])
                p.op("dve", lambda: nc.vector.tensor_tensor(out=t2[i][0:M, :], in0=pb[0:M, :], in1=sn, op=ALU.mult),
                     reads=[kb] + cskeys, writes=[("rt2", i)])
                p.op("pool", lambda: nc.gpsimd.tensor_tensor(out=out_ap, in0=t1[i][0:M, :], in1=t2[i][0:M, :], op=ALU.add),
                     reads=[("rt1", i), ("rt2", i)], writes=[out_key])
            return fn

        def load_w(wt, dram, ncols, key, step=1024):
            for a in range(0, ncols, step):
                b = min(ncols, a + step)
                for k in range(NCH):
                    p.op("pool", lambda a=a, b=b, k=k: nc.gpsimd.dma_start(
                        out=wt[:, k, a:b], in_=dram[k * 128:(k + 1) * 128, a:b]), writes=[key], dma=True)

        def emit_pair_transposes(apair_t, kap, ptr, tst, row0, scale_ap=None, scale_key=None):
            for qb in range(NB):
                ts_ = tst[qb % 2]
                kts = ("tst", qb % 2)
                for t in range(4):
                    p.op("pe", lambda qb=qb, t=t: nc.tensor.transpose(
                        ptr[:, t * 128:(t + 1) * 128], apair_t[:, qb * 4 + t, :], ident[:]),
                        reads=[kap, "ident"], writes=["ptrb"])
                if scale_ap is None:
                    p.op("dve", lambda ts_=ts_: nc.vector.tensor_copy(out=ts_[:], in_=ptr[:]),
                         reads=["ptrb"], writes=[kts])
                else:
                    p.op("act", lambda ts_=ts_: nc.scalar.activation(out=ts_[:], in_=ptr[:], func=AF.Copy, scale=scale_ap),
                         reads=["ptrb", scale_key], writes=[kts])
                p.op("sp", lambda ts_=ts_, qb=qb: nc.sync.dma_start(
                    out=attnT_d[row0:row0 + 128, qb * 512:(qb + 1) * 512], in_=ts_[:]),
                    reads=[kts], writes=[("attnT", row0, qb)], dma=True)

        def phase_swa_a(L, r, src):
            with ExitStack() as ph:
                w = sbuf(ph, "wswa", [128, NCH, 1280], BF16)
                wsw = sbuf(ph, "wswas", [128, NCH, 1152], BF16)
                load_w(w, swa_w_in[r], 1280, "w")
                load_w(wsw, swa_w_sw[r], 1152, "wsw")
                s1, sh = load_in_vectors(ph, L, 1)
                hT, mk_hT = make_hT_builder(ph, s1, sh, src, 512)
                qst = [sbuf(ph, "qst%d" % i, [128, 512], BF16) for i in range(4)]
                vst = [sbuf(ph, "vst%d" % i, [128, 2, VW], BF16) for i in range(2)]
                for i in range(2):
                    p.op("pool", lambda i=i: nc.gpsimd.memset(vst[i][:, :, 64:VW], 1.0), writes=[("vst", i)])
                cs = [sbuf(ph, "cs%d" % i, [128, 512], F32) for i in range(2)]
                sn = [sbuf(ph, "sn%d" % i, [128, 512], F32) for i in range(2)]
                pq = [psum(ph, "pq%d" % i, [128, 512], F32) for i in range(4)]
                pv = psum(ph, "pv", [128, 128], F32)
                rp = make_rope_proj(ph, w, wsw, hT, pq, "w", "wsw")
                qi = 0
                for tb in range(NB):
                    blk = slice(tb * 512, (tb + 1) * 512)
                    c_, s_ = cs[tb % 2], sn[tb % 2]
                    p.op("sp", lambda c_=c_, blk=blk: nc.sync.dma_start(out=c_[:], in_=cosF_d[:, blk]),
                         writes=[("cs", tb % 2)], dma=True)
                    p.op("sp", lambda s_=s_, blk=blk: nc.sync.dma_start(out=s_[:], in_=sinF_d[:, blk]),
                         writes=[("sn", tb % 2)], dma=True)
                    mk_hT(tb)
                    ck = [("cs", tb % 2), ("sn", tb % 2)]
                    for c in range(NCH + 1):
                        q_ = qst[qi % 4]
                        kq = ("qst", qi % 4)
                        qi += 1
                        rp(c * 128, c * 128, c_[:], s_[:], ck, q_[:], kq)
                        dram, row = (QT_d, c * 128) if c < NCH else (KT_d, 0)
                        p.op("sp", lambda q_=q_, dram=dram, row=row, blk=blk: nc.sync.dma_start(
                            out=dram[row:row + 128, blk], in_=q_[:]), reads=[kq], writes=[("qkd", qi)], dma=True)
                    for t in range(4):
                        tt = tb * 4 + t
                        vs = vst[tt % 2]
                        for k in range(NCH):
                            p.op("pe", lambda k=k, t=t: nc.tensor.matmul(
                                pv[:], hT[:, k, t * 128:(t + 1) * 128], w[:, k, 1152:1280],
                                start=(k == 0), stop=(k == NCH - 1)), reads=["w", ("hT", k)], writes=["pv"])
                        p.op("dve", lambda vs=vs: nc.vector.tensor_copy(
                            out=vs[:, :, 0:64], in_=pv[:].rearrange("p (h d) -> p h d", d=64)),
                            reads=["pv"], writes=[("vst", tt % 2)])
                        p.op("sp", lambda vs=vs, tt=tt: nc.sync.dma_start(
                            out=V_d[tt * 128:(tt + 1) * 128, 0:2 * VW], in_=vs[:].rearrange("p h d -> p (h d)")),
                            reads=[("vst", tt % 2)], writes=[("vd", tt)], dma=True)
                p.flush()

        def phase_swa_b(L, r):
            with ExitStack() as ph:
                Vall = sbuf(ph, "Vall", [128, NT, 2 * VW], BF16)
                Kdup = [sbuf(ph, "Kdup%d" % i, [128, S], BF16) for i in range(2)]
                Qc = [sbuf(ph, "Qc%d" % i, [128, S], BF16) for i in range(2)]
                pt = [sbuf(ph, "pt%d" % i, [128, 256], BF16) for i in range(3)]
                apair = [sbuf(ph, "apair%d" % i, [128, NT, 128], BF16) for i in range(2)]
                den = [sbuf(ph, "den%d" % i, [128, 2], F32) for i in range(4)]
                tst = [sbuf(ph, "tst%d" % i, [128, 512], BF16) for i in range(2)]
                esink = sbuf(ph, "esink", [128, 16], F32)
                ps = [psum(ph, "ps%d" % i, [128, 256], F32) for i in range(3)]
                pacc = [psum(ph, "pacc%d" % i, [128, 65], F32) for i in range(4)]
                ptr = psum(ph, "ptrb", [128, 512], BF16)
                p.op("sp", lambda: nc.sync.dma_start(
                    out=Vall[:], in_=V_d[:, 0:2 * VW].rearrange("(t p) f -> p t f", p=128)), writes=["Vall"], dma=True)
                for g_ in range(2):
                    for half in range(2):
                        p.op("sp", lambda g_=g_, half=half: nc.sync.dma_start(
                            out=Kdup[g_][half * 64:(half + 1) * 64, :], in_=KT_d[g_ * 64:(g_ + 1) * 64, :]),
                            writes=[("Kdup", g_)], dma=True)
                p.op("sp", lambda: nc.sync.dma_start(out=esink[:], in_=swa_sinks[r:r + 1, :].partition_broadcast(128)),
                     writes=["esink"], dma=True)
                p.op("act", lambda: nc.scalar.activation(out=esink[:], in_=esink[:], func=AF.Exp),
                     reads=["esink"], writes=["esink"])
                cnt = 0
                units = []

                def swa_loads(hp):
                    qc = Qc[hp % 2]
                    p.op("sp", lambda: nc.sync.dma_start(out=qc[:], in_=QT_d[hp * 128:(hp + 1) * 128, :]),
                         writes=[("Qc", hp % 2)], dma=True)

                swa_loads(0)
                for hp in range(8):
                    qc = Qc[hp % 2]
                    kqc = ("Qc", hp % 2)
                    ap_ = apair[hp % 2]
                    kap = ("apair", hp % 2)
                    for par in range(2):
                        h = 2 * hp + par
                        g_ = h // 8
                        kx = Kdup[g_][par * 64:(par + 1) * 64, :]
                        qx = qc[par * 64:(par + 1) * 64, :]
                        rq = [kqc, ("Kdup", g_)]
                        for j in range(NT):
                            i = cnt % 3
                            cnt += 1
                            psi, pti = ps[i], pt[i]
                            kps, kpt = ("ps", i), ("pt", i)
                            ks = slice(j * 128, (j + 1) * 128)
                            two = j + 1 < NT
                            wd = 256 if two else 128
                            with p.capture() as front:
                                if par == 0 and j == 0 and hp + 1 < 8:
                                    swa_loads(hp + 1)
                                p.op("pe", lambda psi=psi, kx=kx, qx=qx, ks=ks: nc.tensor.matmul(
                                    psi[:, 0:128], kx[:, ks], qx[:, ks], start=True, stop=False, skip_group_check=True),
                                    reads=rq, writes=[kps])
                                p.op("pe", lambda psi=psi: nc.tensor.matmul(
                                    psi[:, 0:128], ident[:], masks[:, 0, :], start=False, stop=True, skip_group_check=True),
                                    reads=["ident", "masks"], writes=[kps])
                                if two:
                                    ks2 = slice((j + 1) * 128, (j + 2) * 128)
                                    p.op("pe", lambda psi=psi, kx=kx, qx=qx, ks=ks, ks2=ks2: nc.tensor.matmul(
                                        psi[:, 128:256], kx[:, ks], qx[:, ks2], start=False, stop=False, skip_group_check=True),
                                        reads=rq, writes=[kps])
                                    p.op("pe", lambda psi=psi: nc.tensor.matmul(
                                        psi[:, 128:256], ident[:], masks[:, 1, :], start=False, stop=True, skip_group_check=True),
                                        reads=["ident", "masks"], writes=[kps])
                            with p.capture() as back:
                                p.op("act", lambda psi=psi, pti=pti, wd=wd: nc.scalar.activation(
                                    out=pti[:, 0:wd], in_=psi[:, 0:wd], func=AF.Exp, scale=0.125),
                                    reads=[kps], writes=[kpt])
                                a0 = pacc[j % 4]
                                p.op("pe", lambda a0=a0, pti=pti, j=j, g_=g_: nc.tensor.matmul(
                                    a0[:, :], pti[:, 0:128], Vall[:, j, g_ * VW:g_ * VW + 65],
                                    start=(j == 0), stop=True, skip_group_check=True),
                                    reads=[kpt, "Vall"], writes=[("pacc", j % 4)])
                                if two:
                                    a1 = pacc[(j + 1) % 4]
                                    p.op("pe", lambda a1=a1, pti=pti, j=j, g_=g_: nc.tensor.matmul(
                                        a1[:, :], pti[:, 128:256], Vall[:, j, g_ * VW:g_ * VW + 65],
                                        start=True, stop=False, skip_group_check=True),
                                        reads=[kpt, "Vall"], writes=[("pacc", (j + 1) % 4)])
                                dn = den[j % 4]
                                kdn = ("den", j % 4)
                                p.op("dve", lambda dn=dn, a0=a0, h=h: nc.vector.tensor_tensor(
                                    out=dn[:, 0:1], in0=a0[:, 64:65], in1=esink[:, h:h + 1], op=ALU.add),
                                    reads=[("pacc", j % 4), "esink"], writes=[kdn])
                                p.op("dve", lambda dn=dn: nc.vector.reciprocal(out=dn[:, 1:2], in_=dn[:, 0:1]),
                                     reads=[kdn], writes=[kdn])
                                p.op("dve", lambda dn=dn, a0=a0, j=j, par=par, ap_=ap_: nc.vector.tensor_scalar(
                                    out=ap_[:, j, par * 64:(par + 1) * 64], in0=a0[:, 0:64], scalar1=dn[:, 1:2],
                                    scalar2=None, op0=ALU.mult), reads=[("pacc", j % 4), kdn], writes=[kap])
                                if par == 1 and j == NT - 1:
                                    emit_pair_transposes(ap_, kap, ptr, tst, hp * 128)
                            units.append((front, back))
                for cap in p.pipelined(units, 2):
                    p.splice(cap)
                p.flush()

        def mixer_swa(L, r, src, dst, mod_next=None):
            phase_swa_a(L, r, src)
            phase_swa_b(L, r)
            phase_outproj(L, swa_w_out[r], src, dst, mod_next)

        def phase_diff_a(L, r, src):
            with ExitStack() as ph:
                w = sbuf(ph, "wdiff", [128, NCH, 3072], BF16)
                wsw = sbuf(ph, "wdiffs", [128, NCH, 2048], BF16)
                load_w(w, diff_w_in[r], 3072, "w")
                load_w(wsw, diff_w_sw[r], 2048, "wsw")
                s1, sh = load_in_vectors(ph, L, 1)
                hT, mk_hT = make_hT_builder(ph, s1, sh, src, 512)
                qst = [sbuf(ph, "qst%d" % i, [128, 512], BF16) for i in range(4)]
                vst = [sbuf(ph, "vst%d" % i, [128, 8, VW2], BF16) for i in range(2)]
                for i in range(2):
                    p.op("pool", lambda i=i: nc.gpsimd.memset(vst[i][:, :, 128:VW2], 1.0), writes=[("vst", i)])
                cs = [sbuf(ph, "cs%d" % i, [128, 512], F32) for i in range(2)]
                sn = [sbuf(ph, "sn%d" % i, [128, 512], F32) for i in range(2)]
                pq = [psum(ph, "pq%d" % i, [128, 512], F32) for i in range(4)]
                pv = [psum(ph, "pv%d" % i, [128, 512], F32) for i in range(2)]
                rp = make_rope_proj(ph, w, wsw, hT, pq, "w", "wsw")
                qi = 0
                for tb in range(NB):
                    blk = slice(tb * 512, (tb + 1) * 512)
                    c_, s_ = cs[tb % 2], sn[tb % 2]
                    p.op("sp", lambda c_=c_, blk=blk: nc.sync.dma_start(out=c_[:], in_=cosF_d[:, blk]),
                         writes=[("cs", tb % 2)], dma=True)
                    p.op("sp", lambda s_=s_, blk=blk: nc.sync.dma_start(out=s_[:], in_=sinF_d[:, blk]),
                         writes=[("sn", tb % 2)], dma=True)
                    mk_hT(tb)
                    ck = [("cs", tb % 2), ("sn", tb % 2)]
                    for c in range(2 * NCH):
                        q_ = qst[qi % 4]
                        kq = ("qst", qi % 4)
                        qi += 1
                        rp(c * 128, c * 128, c_[:], s_[:], ck, q_[:], kq)
                        dram, row = (QT_d, c * 128) if c < NCH else (KT_d, (c - NCH) * 128)
                        p.op("sp", lambda q_=q_, dram=dram, row=row, blk=blk: nc.sync.dma_start(
                            out=dram[row:row + 128, blk], in_=q_[:]), reads=[kq], writes=[("qkd", qi)], dma=True)
                    for t in range(4):
                        tt = tb * 4 + t
                        vs = vst[tt % 2]
                        for hh in range(2):
                            for k in range(NCH):
                                p.op("pe", lambda k=k, t=t, hh=hh: nc.tensor.matmul(
                                    pv[hh][:], hT[:, k, t * 128:(t + 1) * 128], w[:, k, 2048 + hh * 512:2048 + (hh + 1) * 512],
                                    start=(k == 0), stop=(k == NCH - 1)), reads=["w", ("hT", k)], writes=[("pv", hh)])
                            if hh == 0:
                                p.op("dve", lambda vs=vs, hh=hh: nc.vector.tensor_copy(
                                    out=vs[:, hh * 4:(hh + 1) * 4, 0:128], in_=pv[hh][:].rearrange("p (h d) -> p h d", d=128)),
                                    reads=[("pv", hh)], writes=[("vst", tt % 2)])
                            else:
                                p.op("act", lambda vs=vs, hh=hh: nc.scalar.copy(
                                    out=vs[:, hh * 4:(hh + 1) * 4, 0:128], in_=pv[hh][:].rearrange("p (h d) -> p h d", d=128)),
                                    reads=[("pv", hh)], writes=[("vst", tt % 2)])
                        p.op("sp", lambda vs=vs, tt=tt: nc.sync.dma_start(
                            out=V_d[tt * 128:(tt + 1) * 128, 0:8 * VW2], in_=vs[:].rearrange("p h d -> p (h d)")),
                            reads=[("vst", tt % 2)], writes=[("vd", tt)], dma=True)
                p.flush()

        def phase_diff_b(L, r):
            lam_init = 0.8 - 0.6 * math.exp(-0.3 * L)
            with ExitStack() as ph:
                Vall = sbuf(ph, "Vall", [128, NT, 8 * VW2], BF16)
                Qc = [sbuf(ph, "Qc%d" % i, [128, S], BF16) for i in range(2)]
                Kc = [sbuf(ph, "Kc%d" % i, [128, S], BF16) for i in range(2)]
                pt = [sbuf(ph, "pt%d" % i, [128, 512], BF16) for i in range(3)]
                lt = sbuf(ph, "lamt", [128, 256], F32)
                lsm = sbuf(ph, "lsm", [128, 8], F32)
                sub = sbuf(ph, "subln", [128, 1], F32)
                onesb = sbuf(ph, "onesb", [128, 128], BF16)
                r1 = sbuf(ph, "r1", [128, 512], F32)
                o_ = sbuf(ph, "o_", [128, 512], F32)
                r2 = sbuf(ph, "r2", [128, 512], F32)
                t2 = sbuf(ph, "t2", [128, 512], F32)
                sq = sbuf(ph, "sq", [128, 512], BF16)
                rs = sbuf(ph, "rs", [128, 512], F32)
                ot = [sbuf(ph, "ot%d" % i, [128, 512], BF16) for i in range(2)]
                ps = [psum(ph, "ps%d" % i, [128, 512], F32) for i in range(3)]
                oacc = [psum(ph, "oacc%d" % i, [128, 512], F32) for i in range(2)]
                dacc = [psum(ph, "dacc%d" % i, [128, 512], F32) for i in range(2)]
                pss = psum(ph, "pss", [128, 512], F32)
                p.op("pool", lambda: nc.gpsimd.memset(onesb[:], 1.0), writes=["onesb"])
                onesf = sbuf(ph, "onesf", [128, 128], F32)
                p.op("pool", lambda: nc.gpsimd.memset(onesf[:], 1.0), writes=["onesf"])
                dsum = [sbuf(ph, "dsum%d" % i, [128, 512], F32) for i in range(2)]
                epsT = sbuf(ph, "epsT", [128, 1], F32)
                p.op("pool", lambda: nc.gpsimd.memset(epsT[:], RMS_EPS), writes=["epsT"])
                for t0 in range(0, NT, 4):
                    p.op("sp", lambda t0=t0: nc.sync.dma_start(
                        out=Vall[:, t0:t0 + 4, :],
                        in_=V_d[t0 * 128:(t0 + 4) * 128, 0:8 * VW2].rearrange("(t p) f -> p t f", p=128)),
                        writes=["Vall"], dma=True)
                p.op("sp", lambda: nc.sync.dma_start(out=lt[:], in_=diff_lambda[r:r + 1, :].partition_broadcast(128)),
                     writes=["lt"], dma=True)
                p.op("sp", lambda: nc.sync.dma_start(out=sub[:], in_=diff_subln[r, :].rearrange("(p o) -> p o", o=1)),
                     writes=["sub"], dma=True)
                p.op("dve", lambda: nc.vector.tensor_scalar(out=sub[:], in0=sub[:], scalar1=1.0 - lam_init, scalar2=None,
                                                           op0=ALU.mult), reads=["sub"], writes=["sub"])
                p.op("dve", lambda: nc.vector.tensor_tensor(out=lt[:, 0:64], in0=lt[:, 0:64], in1=lt[:, 64:128], op=ALU.mult),
                     reads=["lt"], writes=["lt"])
                p.op("dve", lambda: nc.vector.tensor_tensor(out=lt[:, 128:192], in0=lt[:, 128:192], in1=lt[:, 192:256], op=ALU.mult),
                     reads=["lt"], writes=["lt"])
                p.op("dve", lambda: nc.vector.reduce_sum(out=lsm[:, 0:1], in_=lt[:, 0:64], axis=AX.X), reads=["lt"], writes=["lsm"])
                p.op("dve", lambda: nc.vector.reduce_sum(out=lsm[:, 1:2], in_=lt[:, 128:192], axis=AX.X), reads=["lt"], writes=["lsm"])
                p.op("act", lambda: nc.scalar.activation(out=lsm[:, 2:4], in_=lsm[:, 0:2], func=AF.Exp), reads=["lsm"], writes=["lsm"])
                p.op("dve", lambda: nc.vector.tensor_tensor(out=lsm[:, 4:5], in0=lsm[:, 3:4], in1=lsm[:, 2:3], op=ALU.subtract),
                     reads=["lsm"], writes=["lsm"])
                p.op("dve", lambda: nc.vector.tensor_scalar(out=lsm[:, 5:6], in0=lsm[:, 4:5], scalar1=-lam_init, scalar2=None,
                                                           op0=ALU.add), reads=["lsm"], writes=["lsm"])
                cnt = 0
                fin = 0
                units = []

                def diff_loads(h):
                    qc, kc = Qc[h % 2], Kc[h % 2]
                    p.op("sp", lambda: nc.sync.dma_start(out=qc[:], in_=QT_d[h * 128:(h + 1) * 128, :]),
                         writes=[("Qc", h % 2)], dma=True)
                    p.op("sp", lambda: nc.sync.dma_start(out=kc[:], in_=KT_d[h * 128:(h + 1) * 128, :]),
                         writes=[("Kc", h % 2)], dma=True)

                diff_loads(0)
                for h in range(8):
                    qc, kc = Qc[h % 2], Kc[h % 2]
                    rq = [("Qc", h % 2), ("Kc", h % 2)]
                    for qb in range(NB):
                        for c in range(2):
                            kx = kc[c * 64:(c + 1) * 64, :]
                            qx = qc[c * 64:(c + 1) * 64, :]
                            oa, da = oacc[c], dacc[c]
                            koa, kda = ("oacc", c), ("dacc", c)
                            for j in range(4 * qb + 4):
                                c0 = max(0, j - 4 * qb) * 128
                                i = cnt % 3
                                cnt += 1
                                psi, pti = ps[i], pt[i]
                                kps, kpt = ("ps", i), ("pt", i)
                                ks = slice(j * 128, (j + 1) * 128)
                                q0 = qb * 512
                                with p.capture() as front:
                                    if qb == 0 and c == 0 and j == 0 and h + 1 < 8:
                                        diff_loads(h + 1)
                                    if j >= 4 * qb:
                                        p.op("pe", lambda psi=psi, kx=kx, qx=qx, ks=ks, c0=c0, q0=q0: nc.tensor.matmul(
                                            psi[:, c0:c0 + 128], kx[:, ks], qx[:, q0 + c0:q0 + c0 + 128], start=True, stop=False,
                                            skip_group_check=True), reads=rq, writes=[kps])
                                        p.op("pe", lambda psi=psi, c0=c0: nc.tensor.matmul(
                                            psi[:, c0:c0 + 128], ident[:], masks[:, 0, :], start=False, stop=True,
                                            skip_group_check=True), reads=["ident", "masks"], writes=[kps])
                                        if c0 + 128 < 512:
                                            p.op("pe", lambda psi=psi, kx=kx, qx=qx, ks=ks, c0=c0, q0=q0: nc.tensor.matmul(
                                                psi[:, c0 + 128:512], kx[:, ks], qx[:, q0 + c0 + 128:q0 + 512], start=False, stop=True,
                                                skip_group_check=True), reads=rq, writes=[kps])
                                    else:
                                        p.op("pe", lambda psi=psi, kx=kx, qx=qx, ks=ks, q0=q0: nc.tensor.matmul(
                                            psi[:, :], kx[:, ks], qx[:, q0:q0 + 512], start=True, stop=True), reads=rq, writes=[kps])
                                with p.capture() as back:
                                    p.op("act", lambda psi=psi, pti=pti, c0=c0: nc.scalar.activation(
                                        out=pti[:, c0:512], in_=psi[:, c0:512], func=AF.Exp, scale=0.125),
                                        reads=[kps], writes=[kpt])
                                    last = (j == 4 * qb + 3)
                                    p.op("pe", lambda oa=oa, pti=pti, j=j, h=h, c0=c0, last=last: nc.tensor.matmul(
                                        oa[:, c0:512], Vall[:, j, h * VW2:h * VW2 + 128], pti[:, c0:512],
                                        start=(j == 0), stop=last, skip_group_check=True),
                                        reads=[kpt, "Vall"], writes=[koa])
                                    ds = dsum[c]
                                    kds = ("dsum", c)
                                    if j == 0:
                                        p.op("dve", lambda ds=ds, pti=pti: nc.vector.tensor_copy(out=ds[:], in_=pti[:]),
                                             reads=[kpt], writes=[kds])
                                    else:
                                        p.op("dve", lambda ds=ds, pti=pti, c0=c0: nc.vector.tensor_tensor(
                                            out=ds[:, c0:512], in0=ds[:, c0:512], in1=pti[:, c0:512], op=ALU.add),
                                            reads=[kpt, kds], writes=[kds])
                                    if last:
                                        p.op("pe", lambda da=da, ds=ds: nc.tensor.matmul(da[:], onesf[:], ds[:], start=True, stop=True),
                                             reads=[kds, "onesf"], writes=[kda])
                                    if c == 0 and last:
                                        p.op("act", lambda: nc.scalar.activation(out=r1[:], in_=dacc[0][:], func=AF.Ln),
                                             reads=[("dacc", 0)], writes=["r1"])
                                        p.op("act", lambda: nc.scalar.activation(out=r1[:], in_=r1[:], func=AF.Exp, scale=-1.0),
                                             reads=["r1"], writes=["r1"])
                                        p.op("dve", lambda: nc.vector.tensor_tensor(out=o_[:], in0=oacc[0][:], in1=r1[:], op=ALU.mult),
                                             reads=[("oacc", 0), "r1"], writes=["o_"])
                                    if c == 1 and last:
                                        ot_ = ot[fin % 2]
                                        kot = ("ot", fin % 2)
                                        fin += 1
                                        p.op("act", lambda: nc.scalar.activation(out=r2[:], in_=dacc[1][:], func=AF.Ln),
                                             reads=[("dacc", 1)], writes=["r2"])
                                        p.op("act", lambda: nc.scalar.activation(out=r2[:], in_=r2[:], func=AF.Exp, scale=-1.0),
                                             reads=["r2"], writes=["r2"])
                                        p.op("dve", lambda: nc.vector.tensor_tensor(out=t2[:], in0=oacc[1][:], in1=r2[:], op=ALU.mult),
                                             reads=[("oacc", 1), "r2"], writes=["t2"])
                                        p.op("dve", lambda: nc.vector.scalar_tensor_tensor(
                                            out=o_[:], in0=t2[:], scalar=lsm[:, 5:6], in1=o_[:], op0=ALU.mult, op1=ALU.add),
                                            reads=["t2", "lsm", "o_"], writes=["o_"])
                                        p.op("pool", lambda: nc.gpsimd.tensor_tensor(out=sq[:], in0=o_[:], in1=o_[:], op=ALU.mult),
                                             reads=["o_"], writes=["sq"])
                                        p.op("pe", lambda: nc.tensor.matmul(pss[:], onesb[:], sq[:], start=True, stop=True),
                                             reads=["sq", "onesb"], writes=["pss"])
                                        p.op("act", lambda: nc.scalar.activation(out=rs[:], in_=pss[:], func=AF.Ln,
                                                                                 bias=epsT[:, 0:1], scale=1.0 / 128),
                                             reads=["pss", "epsT"], writes=["rs"])
                                        p.op("act", lambda: nc.scalar.activation(out=rs[:], in_=rs[:], func=AF.Exp, scale=-0.5),
                                             reads=["rs"], writes=["rs"])
                                        p.op("dve", lambda ot_=ot_: nc.vector.scalar_tensor_tensor(
                                            out=ot_[:], in0=o_[:], scalar=sub[:, 0:1], in1=rs[:], op0=ALU.mult, op1=ALU.mult),
                                            reads=["o_", "sub", "rs"], writes=[kot])
                                        p.op("sp", lambda ot_=ot_, h=h, qb=qb: nc.sync.dma_start(
                                            out=attnT_d[h * 128:(h + 1) * 128, qb * 512:(qb + 1) * 512], in_=ot_[:]),
                                            reads=[kot], writes=[("attnT", h, qb)], dma=True)
                                units.append((front, back))
                for cap in p.pipelined(units, 2):
                    p.splice(cap)
                p.flush()

        def mixer_diff(L, r, src, dst, mod_next=None):
            phase_diff_a(L, r, src)
            phase_diff_b(L, r)
            phase_outproj(L, diff_w_out[r], src, dst, mod_next)


        def phase_dsa_a(L, r, src):
            with ExitStack() as ph:
                w = sbuf(ph, "wdsa", [128, NCH, 1448], BF16)
                wsw = sbuf(ph, "wdsas", [128, NCH, 1312], BF16)
                load_w(w, dsa_w_in[r], 1448, "w")
                load_w(wsw, dsa_w_sw[r], 1312, "wsw")
                wkv = sbuf(ph, "wkv", [128, 128], BF16)
                wkvs = sbuf(ph, "wkvs", [128, 64], BF16)
                kvn = sbuf(ph, "kvn", [128, 1], F32)
                p.op("pool", lambda: nc.gpsimd.dma_start(out=wkv[:], in_=dsa_w_kv_up[r]), writes=["wkv"], dma=True)
                p.op("pool", lambda: nc.gpsimd.dma_start(out=wkvs[:], in_=dsa_w_kv_sw[r]), writes=["wkvs"], dma=True)
                p.op("sp", lambda: nc.sync.dma_start(out=kvn[:], in_=dsa_kv_norm[r, :].rearrange("(p o) -> p o", o=1)),
                     writes=["kvn"], dma=True)
                s1, sh = load_in_vectors(ph, L, 1)
                hT, mk_hT = make_hT_builder(ph, s1, sh, src, 512)
                qst = [sbuf(ph, "qst%d" % i, [128, 512], BF16) for i in range(4)]
                vst = [sbuf(ph, "vst%d" % i, [128, VW], BF16) for i in range(2)]
                for i in range(2):
                    p.op("pool", lambda i=i: nc.gpsimd.memset(vst[i][:, 64:VW], 1.0), writes=[("vst", i)])
                cs = [sbuf(ph, "cs%d" % i, [128, 512], F32) for i in range(2)]
                sn = [sbuf(ph, "sn%d" % i, [128, 512], F32) for i in range(2)]
                csi = [sbuf(ph, "csi%d" % i, [128, 512], F32) for i in range(2)]
                sni = [sbuf(ph, "sni%d" % i, [128, 512], F32) for i in range(2)]
                ckn = [sbuf(ph, "ckn%d" % i, [128, 128], BF16) for i in range(2)]
                ckvT = sbuf(ph, "ckvT", [128, 512], BF16)
                wist = sbuf(ph, "wist", [128, NT, 8], F32)
                fs = [sbuf(ph, "fs%d" % i, [128, 4], F32) for i in range(2)]
                jk = sbuf(ph, "junk", [128, 128], F32)
                nh = sbuf(ph, "neghalf3", [128, 1], F32)
                kt1 = sbuf(ph, "kt1", [64, 512], F32)
                kt2 = sbuf(ph, "kt2", [64, 512], F32)
                p.op("pool", lambda: nc.gpsimd.memset(nh[:], -0.5), writes=["nh"])
                pq = [psum(ph, "pq%d" % i, [128, 512], F32) for i in range(4)]
                pv = psum(ph, "pv", [128, 128], F32)
                pT2 = psum(ph, "pT2", [128, 512], BF16)
                rp = make_rope_proj(ph, w, wsw, hT, pq, "w", "wsw")
                qi = 0
                for tb in range(NB):
                    blk = slice(tb * 512, (tb + 1) * 512)
                    b2 = tb % 2
                    for tl, dr, nm_ in ((cs, cosF_d, "cs"), (sn, sinF_d, "sn"), (csi, cosI_d, "csi"), (sni, sinI_d, "sni")):
                        p.op("sp", lambda tl=tl, dr=dr, blk=blk, b2=b2: nc.sync.dma_start(out=tl[b2][:], in_=dr[:, blk]),
                             writes=[(nm_, b2)], dma=True)
                    mk_hT(tb)
                    ck = [("cs", b2), ("sn", b2)]
                    cki = [("csi", b2), ("sni", b2)]
                    for c in range(NCH + 3):
                        q_ = qst[qi % 4]
                        kq = ("qst", qi % 4)
                        qi += 1
                        if c < NCH:
                            rp(c * 128, c * 128, cs[b2][:], sn[b2][:], ck, q_[:], kq)
                            p.op("sp", lambda q_=q_, c=c, blk=blk: nc.sync.dma_start(
                                out=QT_d[c * 128:(c + 1) * 128, blk], in_=q_[:]), reads=[kq], writes=[("qkd", qi)], dma=True)
                        elif c < NCH + 2:
                            ci = c - NCH
                            rp(1152 + ci * 128, 1024 + ci * 128, csi[b2][:], sni[b2][:], cki, q_[:], kq)
                            p.op("sp", lambda q_=q_, ci=ci, blk=blk: nc.sync.dma_start(
                                out=QI_d[ci * 128:(ci + 1) * 128, blk], in_=q_[:]), reads=[kq], writes=[("qkd", qi)], dma=True)
                        else:
                            rp(1408, 1280, csi[b2][0:32, :], sni[b2][0:32, :], cki, q_[0:32, :], kq, M=32)
                            p.op("sp", lambda q_=q_, blk=blk: nc.sync.dma_start(
                                out=KI_d[:, blk], in_=q_[0:32, :]), reads=[kq], writes=[("qkd", qi)], dma=True)
                    for t in range(4):
                        tt = tb * 4 + t
                        f_ = fs[tt % 2]
                        kf = ("fs", tt % 2)
                        cn = ckn[tt % 2]
                        kcn = ("ckn", tt % 2)
                        for k in range(NCH):
                            p.op("pe", lambda k=k, t=t: nc.tensor.matmul(
                                pv[:], hT[:, k, t * 128:(t + 1) * 128], w[:, k, 1024:1152],
                                start=(k == 0), stop=(k == NCH - 1)), reads=["w", ("hT", k)], writes=["pv"])
                        p.op("act", lambda f_=f_: nc.scalar.activation(out=jk[:], in_=pv[:], func=AF.Square, accum_out=f_[:, 0:1]),
                             reads=["pv"], writes=[kf, "junk"])
                        p.op("pool", lambda f_=f_: nc.gpsimd.tensor_scalar(out=f_[:, 1:2], in0=f_[:, 0:1], scalar1=1.0 / 128,
                                                                          scalar2=RMS_EPS, op0=ALU.mult, op1=ALU.add),
                             reads=[kf], writes=[kf])
                        p.op("pool", lambda f_=f_: nc.gpsimd.tensor_tensor(out=f_[:, 2:3], in0=f_[:, 1:2], in1=nh[:], op=ALU.pow),
                             reads=[kf, "nh"], writes=[kf])
                        p.op("dve", lambda f_=f_, cn=cn: nc.vector.tensor_scalar(out=cn[:], in0=pv[:], scalar1=f_[:, 2:3],
                                                                               scalar2=None, op0=ALU.mult),
                             reads=["pv", kf], writes=[kcn])
                        p.op("pe", lambda cn=cn, t=t: nc.tensor.transpose(pT2[:, t * 128:(t + 1) * 128], cn[:], ident[:]),
                             reads=[kcn, "ident"], writes=["pT2"])
                        for k in range(NCH):
                            p.op("pe", lambda k=k, t=t: nc.tensor.matmul(
                                pq[3][:, 0:8], hT[:, k, t * 128:(t + 1) * 128], w[:, k, 1440:1448],
                                start=(k == 0), stop=(k == NCH - 1)), reads=["w", ("hT", k)], writes=[("pq", 3)])
                        p.op("dve", lambda tt=tt: nc.vector.tensor_scalar(out=wist[:, tt, :], in0=pq[3][:, 0:8], scalar1=1.0 / 16,
                                                                        scalar2=None, op0=ALU.mult),
                             reads=[("pq", 3)], writes=["wist"])
                    p.op("act", lambda: nc.scalar.activation(out=ckvT[:], in_=pT2[:], func=AF.Copy, scale=kvn[:, 0:1]),
                         reads=["pT2", "kvn"], writes=["ckvT"])
                    p.op("pe", lambda: nc.tensor.matmul(pq[0][0:64, :], wkv[:, 0:64], ckvT[:], start=True, stop=True),
                         reads=["wkv", "ckvT"], writes=[("pq", 0)])
                    p.op("pe", lambda: nc.tensor.matmul(pq[1][0:64, :], wkvs[:, 0:64], ckvT[:], start=True, stop=True),
                         reads=["wkvs", "ckvT"], writes=[("pq", 1)])
                    p.op("dve", lambda b2=b2: nc.vector.tensor_tensor(out=kt1[:], in0=pq[0][0:64, :], in1=cs[b2][0:64, :], op=ALU.mult),
                         reads=[("pq", 0)] + ck, writes=["kt1"])
                    p.op("dve", lambda b2=b2: nc.vector.tensor_tensor(out=kt2[:], in0=pq[1][0:64, :], in1=sn[b2][0:64, :], op=ALU.mult),
                         reads=[("pq", 1)] + ck, writes=["kt2"])
                    q_ = qst[qi % 4]
                    kq = ("qst", qi % 4)
                    qi += 1
                    p.op("pool", lambda q_=q_: nc.gpsimd.tensor_tensor(out=q_[0:64, :], in0=kt1[:], in1=kt2[:], op=ALU.add),
                         reads=["kt1", "kt2"], writes=[kq])
                    p.op("sp", lambda q_=q_, blk=blk: nc.sync.dma_start(out=KT_d[0:64, blk], in_=q_[0:64, :]),
                         reads=[kq], writes=[("qkd", qi)], dma=True)
                    for t in range(4):
                        tt = tb * 4 + t
                        vs = vst[tt % 2]
                        p.op("pe", lambda t=t: nc.tensor.matmul(pv[:, 0:64], ckvT[:, t * 128:(t + 1) * 128], wkv[:, 64:128],
                                                                start=True, stop=True),
                             reads=["wkv", "ckvT"], writes=["pv"])
                        p.op("dve", lambda vs=vs: nc.vector.tensor_copy(out=vs[:, 0:64], in_=pv[:, 0:64]),
                             reads=["pv"], writes=[("vst", tt % 2)])
                        p.op("sp", lambda vs=vs, tt=tt: nc.sync.dma_start(out=V_d[tt * 128:(tt + 1) * 128, 0:VW], in_=vs[:]),
                             reads=[("vst", tt % 2)], writes=[("vd", tt)], dma=True)
                p.op("sp", lambda: nc.sync.dma_start(out=WI_d.rearrange("(t p) h -> p t h", p=128), in_=wist[:]),
                     reads=["wist"], writes=["wid"], dma=True)
                p.flush()

        def phase_dsa_b(L, r):
            NIT = 20
            KSEL = min(256, S // 4)
            with ExitStack() as ph:
                KT2 = sbuf(ph, "KT2", [128, S], BF16)
                Vall = sbuf(ph, "Vall", [128, NT, VW], BF16)
                KI = sbuf(ph, "KI", [32, S], BF16)
                WI = sbuf(ph, "WI", [128, NT, 8], F32)
                maskT = [sbuf(ph, "maskT%d" % i, [128, NT, 512], BF16) for i in range(2)]
                ablk = sbuf(ph, "ablk", [128, 4, D], BF16)
                QTb = [sbuf(ph, "QTb%d" % i, [128, NCH, 512], BF16) for i in range(2)]
                QIb = [sbuf(ph, "QIb%d" % i, [32, 8, 512], BF16) for i in range(2)]
                sc = sbuf(ph, "sc", [128, S], F32)
                junk = sbuf(ph, "junkb", [128, S], BF16)
                nm = sbuf(ph, "nm", [128, S], BF16)
                tmp = [sbuf(ph, "tmp%d" % i, [128, 512], F32) for i in range(2)]
                pt = [sbuf(ph, "pt%d" % i, [128, 512], BF16) for i in range(3)]
                st = sbuf(ph, "st", [128, 8], F32)
                rec = [sbuf(ph, "rec%d" % i, [128, 4], F32) for i in range(2)]
                tst = [sbuf(ph, "tst%d" % i, [128, 512], BF16) for i in range(2)]
                mge = sbuf(ph, "mge", [128, 128], F32)
                psc = [psum(ph, "psc%d" % i, [128, 512], F32) for i in range(2)]
                ps = [psum(ph, "ps%d" % i, [128, 512], F32) for i in range(2)]
                pacc = [psum(ph, "pacc%d" % i, [128, 4, 65], F32) for i in range(2)]
                ptrm = psum(ph, "ptrm", [128, 512], BF16)
                ptra = psum(ph, "ptra", [128, 512], BF16)
                for half in range(2):
                    p.op("sp", lambda half=half: nc.sync.dma_start(out=KT2[half * 64:(half + 1) * 64, :], in_=KT_d[0:64, :]),
                         writes=["KT2"], dma=True)
                p.op("sp", lambda: nc.sync.dma_start(out=Vall[:], in_=V_d[:, 0:VW].rearrange("(t p) f -> p t f", p=128)),
                     writes=["Vall"], dma=True)
                p.op("sp", lambda: nc.sync.dma_start(out=KI[:], in_=KI_d), writes=["KI"], dma=True)
                p.op("sp", lambda: nc.sync.dma_start(out=WI[:], in_=WI_d.rearrange("(t p) h -> p t h", p=128)),
                     writes=["WI"], dma=True)
                p.op("dve", lambda: nc.vector.tensor_copy(out=mge[:], in_=masks[:, 2, :]), reads=["masks"], writes=["mge"])
                neg1 = sbuf(ph, "neg1", [128, 4], F32)
                p.op("pool", lambda: nc.gpsimd.memset(neg1[:], -1.0), writes=["neg1"])
                cA = [0]

                def step_a(qb):
                    caps = []

                    def unit(cost=1.0):
                        c = p.capture()
                        c.cost = cost
                        caps.append(c)
                        return c
                    mt = maskT[qb % 2]
                    kmt = ("maskT", qb % 2)
                    qib = QIb[qb % 2]
                    kqi = ("QIb", qb % 2)
                    with unit():
                        p.op("sp", lambda: nc.sync.dma_start(
                            out=qib[:], in_=QI_d[:, qb * 512:(qb + 1) * 512].rearrange("(h d) s -> d h s", d=32)),
                            writes=[kqi], dma=True)
                    for t in range(4):
                        tt = 4 * qb + t
                        nk = (tt + 1) * 128
                        dg = slice(tt * 128, (tt + 1) * 128)
                        if (tt + 1) * 128 <= KSEL:
                            with unit():
                                if tt > 0:
                                    p.op("pool", lambda tt=tt: nc.gpsimd.memset(nm[:, 0:tt * 128], 0.0), writes=["nm"])
                                p.op("pool", lambda dg=dg: nc.gpsimd.tensor_copy(out=nm[:, dg], in_=masks[:, 2, :]),
                                     reads=["masks"], writes=["nm"])
                        else:
                            nkb = (nk + 511) // 512
                            sck = [("sc", kb) for kb in range(nkb)]
                            for kb in range(nkb):
                                wd = min(512, nk - kb * 512)
                                cols = slice(kb * 512, kb * 512 + wd)
                                for h in range(8):
                                    i = cA[0] % 2
                                    cA[0] += 1
                                    pc = psc[i]
                                    kpc = ("psc", i)
                                    with unit(0.9):
                                        p.op("pe", lambda pc=pc, h=h, t=t, cols=cols, wd=wd: nc.tensor.matmul(
                                            pc[:, 0:wd], qib[:, h, t * 128:(t + 1) * 128], KI[:, cols], start=True, stop=True),
                                            reads=[kqi, "KI"], writes=[kpc])
                                        if h == 0:
                                            p.op("dve", lambda pc=pc, cols=cols, wd=wd, tt=tt: nc.vector.tensor_scalar(
                                                out=sc[:, cols], in0=pc[:, 0:wd], scalar1=0.0, scalar2=WI[:, tt, 0:1],
                                                op0=ALU.max, op1=ALU.mult), reads=[kpc, "WI", "nmdone"], writes=[("sc", kb)])
                                        else:
                                            tm = tmp[i]
                                            ktm = ("tmp", i)
                                            p.op("dve", lambda pc=pc, tm=tm, wd=wd, tt=tt, h=h: nc.vector.tensor_scalar(
                                                out=tm[:, 0:wd], in0=pc[:, 0:wd], scalar1=0.0, scalar2=WI[:, tt, h:h + 1],
                                                op0=ALU.max, op1=ALU.mult), reads=[kpc, "WI"], writes=[ktm])
                                            if h <= 5:
                                                p.op("pool", lambda tm=tm, cols=cols, wd=wd: nc.gpsimd.tensor_tensor(
                                                    out=sc[:, cols], in0=sc[:, cols], in1=tm[:, 0:wd], op=ALU.add),
                                                    reads=[ktm, ("sc", kb)], writes=[("sc", kb)])
                                            else:
                                                p.op("dve", lambda tm=tm, cols=cols, wd=wd: nc.vector.tensor_tensor(
                                                    out=sc[:, cols], in0=sc[:, cols], in1=tm[:, 0:wd], op=ALU.add),
                                                    reads=[ktm, ("sc", kb)], writes=[("sc", kb)])
                            with unit(0.5 + nk / 960.0):
                                p.op("dve", lambda nk=nk: nc.vector.reduce_max(out=st[:, 0:1], in_=sc[:, 0:nk], axis=AX.X,
                                                                              apply_absolute_value=True),
                                     reads=sck, writes=["st"])
                                p.op("pool", lambda dg=dg: nc.gpsimd.tensor_tensor(out=sc[:, dg], in0=sc[:, dg], in1=mge[:], op=ALU.add),
                                     reads=sck + ["mge", "st"], writes=sck)
                                p.op("dve", lambda: nc.vector.tensor_scalar(out=st[:, 1:2], in0=st[:, 0:1], scalar1=-1.0, scalar2=None,
                                                                           op0=ALU.mult), reads=["st"], writes=["st"])
                                p.op("dve", lambda: nc.vector.tensor_scalar(out=st[:, 2:3], in0=st[:, 0:1], scalar1=2.0002, scalar2=1e-6,
                                                                           op0=ALU.mult, op1=ALU.add), reads=["st"], writes=["st"])
                            for n_ in range(1, NIT + 1):
                                f = 2.0 ** (-n_)
                                with unit(0.8 + nk / 960.0):
                                    p.op("dve", lambda f=f: nc.vector.tensor_scalar(out=st[:, 3:4], in0=st[:, 2:3], scalar1=f,
                                                                                  scalar2=st[:, 1:2], op0=ALU.mult, op1=ALU.add),
                                         reads=["st"], writes=["st"])
                                    p.op("dve", lambda nk=nk: nc.vector.tensor_scalar(
                                        out=junk[:, 0:nk], in0=sc[:, 0:nk], scalar1=st[:, 3:4], scalar2=0.0,
                                        op0=ALU.is_ge, op1=ALU.add, accum_out=st[:, 4:5]),
                                        reads=sck + ["st"], writes=["junk", "st"])
                                    p.op("dve", lambda: nc.vector.tensor_scalar(out=st[:, 5:6], in0=st[:, 4:5], scalar1=KSEL - 0.5,
                                                                               scalar2=st[:, 2:3], op0=ALU.is_ge, op1=ALU.mult),
                                         reads=["st"], writes=["st"])
                                    p.op("dve", lambda f=f: nc.vector.scalar_tensor_tensor(out=st[:, 1:2], in0=st[:, 5:6], scalar=f,
                                                                                         in1=st[:, 1:2], op0=ALU.mult, op1=ALU.add),
                                         reads=["st"], writes=["st"])
                            with unit(0.3 + nk / 1900.0):
                                p.op("dve", lambda nk=nk: nc.vector.tensor_scalar(
                                    out=nm[:, 0:nk], in0=sc[:, 0:nk], scalar1=st[:, 1:2], scalar2=NEG, op0=ALU.is_lt, op1=ALU.mult),
                                    reads=sck + ["st"], writes=["nm", "nmdone"])
                        for j0 in range(0, tt + 1, 4):
                            n4 = min(4, tt + 1 - j0)
                            with unit():
                                for i4 in range(n4):
                                    p.op("pe", lambda j0=j0, i4=i4: nc.tensor.transpose(
                                        ptrm[:, i4 * 128:(i4 + 1) * 128], nm[:, (j0 + i4) * 128:(j0 + i4 + 1) * 128], ident[:]),
                                        reads=["nm", "ident"], writes=["ptrm"])
                                p.op("dve", lambda j0=j0, n4=n4, t=t: nc.vector.tensor_copy(
                                    out=mt[:, j0:j0 + n4, t * 128:(t + 1) * 128],
                                    in_=ptrm[:, 0:n4 * 128].rearrange("p (a b) -> p a b", b=128)),
                                    reads=["ptrm"], writes=[kmt])
                    return caps

                cB = [0, 0]

                def step_b(qb):
                    units = []
                    mt = maskT[qb % 2]
                    kmt = ("maskT", qb % 2)
                    qtb = QTb[qb % 2]
                    kqt = ("QTb", qb % 2)
                    first = True
                    for h in range(16):
                        par = h % 2
                        kx = KT2[par * 64:(par + 1) * 64, :]
                        qx = qtb[par * 64:(par + 1) * 64, h // 2, :]
                        pa = pacc[cB[1] % 2]
                        kpa = ("pacc", cB[1] % 2)
                        rc = rec[cB[1] % 2]
                        krc = ("rec", cB[1] % 2)
                        cB[1] += 1
                        for j in range(4 * qb + 4):
                            c0 = max(0, j - 4 * qb) * 128
                            i = cB[0] % 2
                            ip = cB[0] % 3
                            cB[0] += 1
                            psi, pti = ps[i], pt[ip]
                            kps, kpt = ("ps", i), ("pt", ip)
                            ks = slice(j * 128, (j + 1) * 128)
                            with p.capture() as front:
                                if first:
                                    first = False
                                    p.op("sp", lambda: nc.sync.dma_start(
                                        out=qtb[:], in_=QT_d[:, qb * 512:(qb + 1) * 512].rearrange("(c p) s -> p c s", p=128)),
                                        writes=[kqt], dma=True)
                                p.op("pe", lambda psi=psi, kx=kx, qx=qx, ks=ks, c0=c0: nc.tensor.matmul(
                                    psi[:, c0:512], kx[:, ks], qx[:, c0:512], start=True, stop=False, skip_group_check=True),
                                    reads=["KT2", kqt], writes=[kps])
                                p.op("pe", lambda psi=psi, j=j, c0=c0: nc.tensor.matmul(
                                    psi[:, c0:512], ident[:], mt[:, j, c0:512], start=False, stop=True, skip_group_check=True),
                                    reads=["ident", kmt], writes=[kps])
                            with p.capture() as back:
                                p.op("act", lambda psi=psi, pti=pti, c0=c0: nc.scalar.activation(
                                    out=pti[:, c0:512], in_=psi[:, c0:512], func=AF.Exp, scale=0.125),
                                    reads=[kps], writes=[kpt])
                                for t in range(c0 // 128, 4):
                                    p.op("pe", lambda pa=pa, pti=pti, t=t, j=j: nc.tensor.matmul(
                                        pa[:, t, :], pti[:, t * 128:(t + 1) * 128], Vall[:, j, 0:65],
                                        start=(j == 0 and t == 0), stop=(j == 4 * qb + t), skip_group_check=True),
                                        reads=[kpt, "Vall"], writes=[kpa])
                                if j == 4 * qb + 3:
                                    p.op("act", lambda rc=rc, pa=pa: nc.scalar.copy(out=rc[:], in_=pa[:, :, 64]),
                                         reads=[kpa], writes=[krc])
                                    p.op("pool", lambda rc=rc: nc.gpsimd.tensor_tensor(out=rc[:], in0=rc[:], in1=neg1[:], op=ALU.pow),
                                         reads=[krc, "neg1"], writes=[krc])
                                    for t in range(4):
                                        p.op("act", lambda pa=pa, rc=rc, t=t, h=h: nc.scalar.activation(
                                            out=ablk[:, t, h * 64:(h + 1) * 64], in_=pa[:, t, 0:64], func=AF.Copy,
                                            scale=rc[:, t:t + 1]),
                                            reads=[kpa, krc], writes=[("ablk", h // 2)])
                                    if h == 15:
                                        for c in range(NCH):
                                            ts_ = tst[c % 2]
                                            kts = ("tst", c % 2)
                                            for t in range(4):
                                                p.op("pe", lambda c=c, t=t: nc.tensor.transpose(
                                                    ptra[:, t * 128:(t + 1) * 128], ablk[:, t, c * 128:(c + 1) * 128], ident[:]),
                                                    reads=[("ablk", c), "ident"], writes=["ptra"])
                                            p.op("act", lambda ts_=ts_: nc.scalar.copy(out=ts_[:], in_=ptra[:]), reads=["ptra"], writes=[kts])
                                            p.op("sp", lambda ts_=ts_, c=c: nc.sync.dma_start(
                                                out=attnT_d[c * 128:(c + 1) * 128, qb * 512:(qb + 1) * 512], in_=ts_[:]),
                                                reads=[kts], writes=[("attnT", c, qb)], dma=True)
                            units.append((front, back))
                    lst = p.pipelined(units, 1)
                    for c_ in lst:
                        c_.cost = 0.4
                    return lst

                for cap in step_a(0):
                    p.splice(cap)
                for qb in range(NB):
                    lb = step_b(qb)
                    la = step_a(qb + 1) if qb + 1 < NB else []
                    p.merge(la, lb)
                p.flush()

        def mixer_dsa(L, r, src, dst, mod_next=None):
            phase_dsa_a(L, r, src)
            phase_dsa_b(L, r)
            phase_outproj(L, dsa_w_out[r], src, dst, mod_next)

        layers = cfg["layers"]
        phase_mod(layers[:1])
        cur = x_ext
        plan = []
        for L in layers:
            for j in range(3):
                plan.append((L, j))
                if cfg.get("stop_after") == (L, j):
                    break
            else:
                continue
            break
        for idx, (L, j) in enumerate(plan):
            dst = out_ext if idx == len(plan) - 1 else xs
            if j in (0, 2):
                phase_ffn(L, j, cur, dst)
            else:
                mname = cfg.get("mixer_of", {0: "dsa", 1: "fox", 2: "swa", 3: "diff"})[L]
                {"fox": mixer_fox, "swa": mixer_swa, "diff": mixer_diff, "dsa": mixer_dsa}[mname](L, 0, cur, dst, (L + 1) if (L + 1) in layers else None)
            cur = dst
        g.stats = dict(n_ops=p.n_ops, n_wait=p.n_wait)
    return nc, g


def make_consts(S):
    pp = np.arange(128)[:, None]
    ff = np.arange(128)[None, :]
    m = np.zeros((128, 3, 128), np.float32)
    m[:, 0, :] = np.where(pp <= ff, 0.0, NEG)
    m[:, 1, :] = np.where(pp > ff, 0.0, NEG)
    m[:, 2, :] = np.where(pp >= ff, 0.0, NEG)
    inv = (np.float32(ROPE_THETA) ** (-np.arange(0, 16, 2, dtype=np.float32) / np.float32(16))).astype(np.float32)
    ang = (np.arange(S, dtype=np.float32)[:, None] * inv[None, :]).astype(np.float32)
    cosF = np.ones((128, S), np.float32)
    sinF = np.zeros((128, S), np.float32)
    for hh in range(2):
        cosF[hh * 64:hh * 64 + 8] = np.cos(ang).T
        cosF[hh * 64 + 8:hh * 64 + 16] = np.cos(ang).T
        sinF[hh * 64:hh * 64 + 8] = -np.sin(ang).T
        sinF[hh * 64 + 8:hh * 64 + 16] = np.sin(ang).T
    invi = (np.float32(ROPE_THETA) ** (-np.arange(0, 8, 2, dtype=np.float32) / np.float32(8))).astype(np.float32)
    angi = (np.arange(S, dtype=np.float32)[:, None] * invi[None, :]).astype(np.float32)
    cosI = np.ones((128, S), np.float32)
    sinI = np.zeros((128, S), np.float32)
    for hh in range(4):
        cosI[hh * 32:hh * 32 + 4] = np.cos(angi).T
        cosI[hh * 32 + 4:hh * 32 + 8] = np.cos(angi).T
        sinI[hh * 32:hh * 32 + 4] = -np.sin(angi).T
        sinI[hh * 32 + 4:hh * 32 + 8] = np.sin(angi).T
    return dict(ident=_bf(np.eye(128, dtype=np.float32)), identf=np.eye(128, dtype=np.float32), masks=_bf(m),
                cosF=cosF, sinF=sinF, cosI=cosI, sinI=sinI)


def _swap_cols(w, head_dim, half):
    n = w.shape[-1]
    idx = np.arange(n)
    d = idx % head_dim
    src = np.where(d < half, idx + half, np.where(d < 2 * half, idx - half, idx))
    return np.ascontiguousarray(w[..., src])


SHARED_KEYS = ("ln_g", "ln_b", "w_ada", "b_ada", "w_ffn_in", "w_ffn_out",
               "fox_w_in", "fox_f_bias", "fox_w_out", "swa_w_in", "swa_sinks", "swa_w_out",
               "diff_w_in", "diff_subln", "diff_w_out", "dsa_w_in", "dsa_kv_norm", "dsa_w_kv_up", "dsa_w_out")


def make_shared(inputs, S):
    shared = {k: np.ascontiguousarray(inputs[k]) for k in SHARED_KEYS}
    shared.update(make_consts(S))
    shared["swa_w_sw"] = _swap_cols(inputs["swa_w_in"][:, :, :1152], 64, 8)
    shared["diff_w_sw"] = _swap_cols(inputs["diff_w_in"][:, :, :2048], 64, 8)
    dw = inputs["dsa_w_in"]
    shared["dsa_w_sw"] = np.ascontiguousarray(np.concatenate(
        [_swap_cols(dw[:, :, 0:1024], 64, 8), _swap_cols(dw[:, :, 1152:1408], 32, 4), _swap_cols(dw[:, :, 1408:1440], 32, 4)],
        axis=-1))
    shared["dsa_w_kv_sw"] = _swap_cols(inputs["dsa_w_kv_up"][:, :, 0:64], 64, 8)
    shared["diff_lambda"] = np.ascontiguousarray(inputs["diff_lambda"].reshape(1, 256))
    return shared


def make_in_map(inputs, b, S, shared=None):
    m = dict(shared if shared is not None else make_shared(inputs, S))
    m["x"] = np.ascontiguousarray(inputs["x"][b, :S])
    m["cT"] = np.ascontiguousarray(inputs["c"][b].reshape(NCH, 128).T)
    return m


def kernel(**inputs):
    S = inputs["x"].shape[1]
    B = inputs["x"].shape[0]
    cfg = dict(layers=[0, 1, 2, 3], stop_after=None)
    nc, g = build(S, cfg)
    shared = make_shared(inputs, S)
    in_maps = [make_in_map(inputs, b, S, shared) for b in range(B)]
    res = run_bass_kernel_spmd(nc, in_maps, core_ids=list(range(B)))
    return np.stack([r["out"] for r in res.results], axis=0)
```

```python
import math
from contextlib import ExitStack

import numpy as np
import ml_dtypes
import concourse.bass as bass
import concourse.mybir as mybir
from concourse.bass_utils import run_bass_kernel_spmd

F32 = mybir.dt.float32
BF16 = mybir.dt.bfloat16
ALU = mybir.AluOpType
AF = mybir.ActivationFunctionType
AX = mybir.AxisListType

D = 1024
DFF = 2816
NCH = 8
NHC = 22
DEPTH = 4
ALPHA = (2 * DEPTH) ** 0.25
LN_EPS = 1e-5 / (ALPHA * ALPHA)
RMS_EPS = 1e-6
NEG = -30000.0
NSEM_DMA = 8
ROPE_THETA = 500000.0


class Prog:
    def __init__(self, nc, stack):
        self.nc = nc
        self.eng = {"pe": nc.tensor, "act": nc.scalar, "dve": nc.vector,
                    "pool": nc.gpsimd, "sp": nc.sync}
        self.ops = []
        self.csem = {e: stack.enter_context(nc.semaphore("c_" + e)) for e in self.eng}
        self.dsem = {e: [stack.enter_context(nc.semaphore("d_%s%d" % (e, k))) for k in range(NSEM_DMA)]
                     for e in ("sp", "act", "pool")}
        self.ccount = {e: 0 for e in self.eng}
        self.dcount = {e: 0 for e in self.dsem}
        self.known = {e: {} for e in self.eng}
        self.n_ops = 0
        self.n_wait = 0

    def op(self, eng, fn, reads=(), writes=(), dma=False):
        self.ops.append((eng, fn, tuple(reads), tuple(writes), dma))

    class _Cap:
        def __init__(self, prog):
            self.prog = prog
            self.ops = []

        def __enter__(self):
            self.saved = self.prog.ops
            self.prog.ops = self.ops
            return self

        def __exit__(self, *a):
            self.prog.ops = self.saved

    def capture(self):
        return Prog._Cap(self)

    def splice(self, cap):
        self.ops.extend(cap.ops)

    def pipelined(self, units, depth):
        out = []
        n = len(units)
        for i in range(min(depth, n)):
            out.append(units[i][0])
        for i in range(n):
            if i + depth < n:
                out.append(units[i + depth][0])
            out.append(units[i][1])
        return out

    def merge(self, la, lb):
        ta = sum(getattr(c, "cost", 1.0) for c in la) or 1.0
        tb = sum(getattr(c, "cost", 1.0) for c in lb) or 1.0
        na, nb = len(la), len(lb)
        ia = ib = 0
        ca = cb = 0.0
        while ia < na or ib < nb:
            if ib < nb and (ia >= na or cb / tb <= ca / ta):
                self.splice(lb[ib])
                cb += getattr(lb[ib], "cost", 1.0)
                ib += 1
            else:
                self.splice(la[ia])
                ca += getattr(la[ia], "cost", 1.0)
                ia += 1

    def _wait(self, e, s, v):
        key = id(s)
        if self.known[e].get(key, 0) >= v:
            return
        self.eng[e].wait_ge(s, v)
        self.known[e][key] = v
        self.n_wait += 1

    def flush(self):
        ops = self.ops
        self.ops = []
        n = len(ops)
        if n == 0:
            return
        last_w = {}
        readers = {}
        need = [None] * n
        signal = [False] * n
        for j, (eng, fn, reads, writes, dma) in enumerate(ops):
            d = set()
            for b in reads:
                if b in last_w:
                    d.add(last_w[b])
            for b in writes:
                if b in last_w:
                    d.add(last_w[b])
                for r in readers.get(b, ()):
                    d.add(r)
            d.discard(j)
            lst = []
            for i in d:
                ei, dmai = ops[i][0], ops[i][4]
                if ei == "pe" and eng == "pe" and not dmai and not dma:
                    continue
                lst.append(i)
                signal[i] = True
            need[j] = lst
            for b in reads:
                readers.setdefault(b, []).append(j)
            for b in writes:
                last_w[b] = j
                readers[b] = []
        last_c = {}
        last_d = {}
        for j, (eng, fn, reads, writes, dma) in enumerate(ops):
            if fn is None:
                continue
            if dma:
                last_d.setdefault(eng, []).append(j)
            else:
                last_c[eng] = j
        for j in range(n):
            if ops[j][4] and ops[j][1] is not None:
                signal[j] = True
        bar = []
        for e, j in last_c.items():
            signal[j] = True
            bar.append(j)
        for e, lst in last_d.items():
            for j in lst[-NSEM_DMA:]:
                signal[j] = True
                bar.append(j)
        sig = [None] * n
        for j in range(n):
            if not signal[j]:
                continue
            e, dma = ops[j][0], ops[j][4]
            if dma:
                k = self.dcount[e]
                self.dcount[e] += 1
                sig[j] = (self.dsem[e][k % NSEM_DMA], 16 * (k // NSEM_DMA + 1), 16)
            else:
                self.ccount[e] += 1
                sig[j] = (self.csem[e], self.ccount[e], 1)
        for j in range(n):
            e, fn, reads, writes, dma = ops[j]
            best = {}
            for i in need[j]:
                s, v, _ = sig[i]
                if id(s) not in best or best[id(s)][1] < v:
                    best[id(s)] = (s, v)
            for s, v in best.values():
                self._wait(e, s, v)
            if fn is None:
                continue
            if dma and sig[j] is not None and sig[j][1] > 16:
                self._wait(e, sig[j][0], sig[j][1] - 16)
            ins = fn()
            if sig[j] is not None:
                ins.then_inc(sig[j][0], sig[j][2])
        for e in self.eng:
            for j in bar:
                self._wait(e, sig[j][0], sig[j][1])
        self.n_ops += n


def _bf(a):
    return np.ascontiguousarray(a).astype(ml_dtypes.bfloat16)


class Ctx:
    pass


def build(S, cfg):
    NT = S // 128
    NB = S // 512
    nc = bass.Bass("TRN2", target_bir_lowering=False)
    g = Ctx()
    g.nc = nc
    g.S, g.NT, g.NB = S, NT, NB

    def din(name, shape, dt=F32):
        return nc.dram_tensor(name, list(shape), dt, kind="ExternalInput").ap()

    x_ext = din("x", [S, D])
    cT = din("cT", [128, NCH])
    ln_g = din("ln_g", [DEPTH, 3, D])
    ln_b = din("ln_b", [DEPTH, 3, D])
    w_ada = din("w_ada", [DEPTH, D, 9 * D])
    b_ada = din("b_ada", [DEPTH, 9 * D])
    w_ffn_in = din("w_ffn_in", [DEPTH, 2, D, 2 * DFF])
    w_ffn_out = din("w_ffn_out", [DEPTH, 2, DFF, D])
    ident_d = din("ident", [128, 128], BF16)
    identf_d = din("identf", [128, 128], F32)
    masks_d = din("masks", [128, 3, 128], BF16)
    fox_w_in = din("fox_w_in", [1, D, 3088])
    fox_f_bias = din("fox_f_bias", [1, 16])
    fox_w_out = din("fox_w_out", [1, D, D])
    swa_w_in = din("swa_w_in", [1, D, 1280])
    swa_w_sw = din("swa_w_sw", [1, D, 1152])
    swa_sinks = din("swa_sinks", [1, 16])
    swa_w_out = din("swa_w_out", [1, D, D])
    diff_w_in = din("diff_w_in", [1, D, 3072])
    diff_w_sw = din("diff_w_sw", [1, D, 2048])
    diff_lambda = din("diff_lambda", [1, 256])
    diff_subln = din("diff_subln", [1, 128])
    diff_w_out = din("diff_w_out", [1, D, D])
    dsa_w_in = din("dsa_w_in", [1, D, 1448])
    dsa_w_sw = din("dsa_w_sw", [1, D, 1312])
    dsa_kv_norm = din("dsa_kv_norm", [1, 128])
    dsa_w_kv_up = din("dsa_w_kv_up", [1, 128, 128])
    dsa_w_kv_sw = din("dsa_w_kv_sw", [1, 128, 64])
    dsa_w_out = din("dsa_w_out", [1, D, D])
    cosI_d = din("cosI", [128, S])
    sinI_d = din("sinI", [128, S])
    QI_d = nc.dram_tensor("QI_d", [256, S], BF16).ap()
    KI_d = nc.dram_tensor("KI_d", [32, S], BF16).ap()
    WI_d = nc.dram_tensor("WI_d", [S, 8], F32).ap()
    cosF_d = din("cosF", [128, S])
    sinF_d = din("sinF", [128, S])
    VW = 66
    VW2 = 130
    skind = "ExternalOutput" if cfg.get("debug") else "Internal"
    QT_d = nc.dram_tensor("QT_d", [D, S], BF16, kind=skind).ap()
    KT_d = nc.dram_tensor("KT_d", [D, S], BF16, kind=skind).ap()
    V_d = nc.dram_tensor("V_d", [S, 16 * VW], BF16, kind=skind).ap()
    attnT_d = nc.dram_tensor("attnT_d", [D, S], BF16, kind=skind).ap()
    cq3_d = nc.dram_tensor("cq3_d", [16, 3, S], BF16, kind=skind).ap()
    ck_d = nc.dram_tensor("ck_d", [S, 16], F32, kind=skind).ap()
    out_ext = nc.dram_tensor("out", [S, D], F32, kind="ExternalOutput").ap()
    xs = nc.dram_tensor("xs", [S, D], F32).ap()
    mod_d = nc.dram_tensor("mod_d", [DEPTH, 9 * D], F32).ap()

    with ExitStack() as top:
        p = Prog(nc, top)
        g.p = p

        uid = [0]

        def sbuf(st, name, shape, dt):
            uid[0] += 1
            return st.enter_context(nc.sbuf_tensor("%s_%d" % (name, uid[0]), list(shape), dt))

        class _View:
            def __init__(self, t, shape):
                self.t, self.shape = t, shape
                n = 1
                for d in shape[1:]:
                    n *= d
                self.n = n

            def _base(self):
                a = self.t[0:self.shape[0], 0:self.n]
                if len(self.shape) == 3:
                    a = a.rearrange("p (a b) -> p a b", b=self.shape[2])
                return a

            def __getitem__(self, key):
                return self._base()[key]

        def psum(st, name, shape, dt):
            uid[0] += 1
            per = 512 if dt == F32 else 1024
            t = st.enter_context(nc.psum_tensor("%s_%d" % (name, uid[0]), [128, per], dt))
            return _View(t, list(shape))

        ident = sbuf(top, "ident_sb", [128, 128], BF16)
        p.op("sp", lambda: nc.sync.dma_start(out=ident[:], in_=ident_d), writes=["ident"], dma=True)
        identf = sbuf(top, "identf_sb", [128, 128], F32)
        p.op("sp", lambda: nc.sync.dma_start(out=identf[:], in_=identf_d), writes=["identf"], dma=True)
        masks = sbuf(top, "masks_sb", [128, 3, 128], BF16)
        p.op("sp", lambda: nc.sync.dma_start(out=masks[:], in_=masks_d), writes=["masks"], dma=True)
        p.flush()

        def make_mod_chunks(ph, L):
            cond = sbuf(ph, "cond", [128, NCH], F32)
            condb = sbuf(ph, "condb", [128, NCH], BF16)
            CB = 1152
            NCB = 9 * D // CB
            wblk = [sbuf(ph, "wada%d" % i, [128, NCH, CB], BF16) for i in range(2)]
            brow = [sbuf(ph, "brow%d" % i, [1, CB], F32) for i in range(2)]
            mrow = [sbuf(ph, "mrow%d" % i, [1, CB], F32) for i in range(2)]
            pm = psum(ph, "pm", [1, 384], F32)
            p.op("sp", lambda: nc.sync.dma_start(out=cond[:], in_=cT), writes=["cond"], dma=True)
            p.op("act", lambda: nc.scalar.activation(out=condb[:], in_=cond[:], func=AF.Silu),
                 reads=["cond"], writes=["condb"])

            def loads(cb):
                wb, br = wblk[cb % 2], brow[cb % 2]
                src_ = w_ada[L, :, cb * CB:(cb + 1) * CB].rearrange("(k p) n -> p k n", p=128)
                p.op("pool", lambda: nc.gpsimd.dma_start(out=wb[:], in_=src_), writes=[("wblk", cb % 2)], dma=True)
                p.op("sp", lambda: nc.sync.dma_start(out=br[:], in_=b_ada[L:L + 1, cb * CB:(cb + 1) * CB]),
                     writes=[("brow", cb % 2)], dma=True)

            def chunk(cb):
                if cb == 0:
                    loads(0)
                    loads(1)
                    return
                cb -= 1
                wb, br, mr = wblk[cb % 2], brow[cb % 2], mrow[cb % 2]
                wk, bk, mk = ("wblk", cb % 2), ("brow", cb % 2), ("mrow", cb % 2)
                for sbi in range(CB // 384):
                    for k in range(NCH):
                        p.op("pe", lambda k=k, sbi=sbi: nc.tensor.matmul(
                            pm[:], condb[:, k:k + 1], wb[:, k, sbi * 384:(sbi + 1) * 384],
                            start=(k == 0), stop=(k == NCH - 1)),
                            reads=["condb", wk], writes=["pm"])
                    p.op("dve", lambda sbi=sbi: nc.vector.tensor_tensor(
                        out=mr[:, sbi * 384:(sbi + 1) * 384], in0=pm[:], in1=br[:, sbi * 384:(sbi + 1) * 384], op=ALU.add),
                        reads=["pm", bk], writes=[mk])
                p.op("sp", lambda: nc.sync.dma_start(out=mod_d[L:L + 1, cb * CB:(cb + 1) * CB], in_=mr[:]),
                     reads=[mk], writes=[("mod_d", L, cb)], dma=True)
                if cb + 2 < NCB:
                    loads(cb + 2)
            return [lambda cb=cb: chunk(cb) for cb in range(NCB + 1)]

        def phase_mod(layers):
            for L in layers:
                with ExitStack() as ph:
                    for ch in make_mod_chunks(ph, L):
                        ch()
                    p.flush()

        def load_in_vectors(ph, L, j):
            s1 = sbuf(ph, "s1", [128, NCH], F32)
            sh = sbuf(ph, "sh", [128, NCH], F32)
            o = 3 * j * D
            p.op("sp", lambda: nc.sync.dma_start(
                out=sh[:], in_=mod_d[L, o:o + D].rearrange("(c p) -> p c", p=128),
                allow_slow_non_contiguous=True), writes=["sh"], dma=True)
            p.op("sp", lambda: nc.sync.dma_start(
                out=s1[:], in_=mod_d[L, o + D:o + 2 * D].rearrange("(c p) -> p c", p=128),
                allow_slow_non_contiguous=True), writes=["s1"], dma=True)
            p.op("dve", lambda: nc.vector.tensor_scalar(out=s1[:], in0=s1[:], scalar1=1.0, scalar2=None,
                                                       op0=ALU.add), reads=["s1"], writes=["s1"])
            return s1, sh

        def load_out_vectors(ph, L, j, rw):
            G = sbuf(ph, "G", [128, D], F32)
            gb = sbuf(ph, "gb", [128, D], F32)
            bb = sbuf(ph, "bb", [128, D], F32)
            o = 3 * j * D
            p.op("sp", lambda: nc.sync.dma_start(
                out=G[:], in_=mod_d[L:L + 1, o + 2 * D:o + 3 * D].partition_broadcast(128)),
                writes=["G"], dma=True)
            p.op("sp", lambda: nc.sync.dma_start(
                out=gb[:], in_=ln_g[L, j:j + 1, :].partition_broadcast(128)), writes=["gb"], dma=True)
            p.op("sp", lambda: nc.sync.dma_start(
                out=bb[:], in_=ln_b[L, j:j + 1, :].partition_broadcast(128)), writes=["bb"], dma=True)
            p.op("dve", lambda: nc.vector.tensor_scalar(out=G[:], in0=G[:], scalar1=1.0, scalar2=rw / ALPHA,
                                                       op0=ALU.add, op1=ALU.mult), reads=["G"], writes=["G"])
            return G, gb, bb

        def load_mod_vectors(ph, L, j, rw):
            s1 = sbuf(ph, "s1", [128, NCH], F32)
            sh = sbuf(ph, "sh", [128, NCH], F32)
            G = sbuf(ph, "G", [128, D], F32)
            gb = sbuf(ph, "gb", [128, D], F32)
            bb = sbuf(ph, "bb", [128, D], F32)
            o = 3 * j * D
            p.op("sp", lambda: nc.sync.dma_start(
                out=sh[:], in_=mod_d[L, o:o + D].rearrange("(c p) -> p c", p=128),
                allow_slow_non_contiguous=True), writes=["sh"], dma=True)
            p.op("sp", lambda: nc.sync.dma_start(
                out=s1[:], in_=mod_d[L, o + D:o + 2 * D].rearrange("(c p) -> p c", p=128),
                allow_slow_non_contiguous=True), writes=["s1"], dma=True)
            p.op("sp", lambda: nc.sync.dma_start(
                out=G[:], in_=mod_d[L:L + 1, o + 2 * D:o + 3 * D].partition_broadcast(128)),
                writes=["G"], dma=True)
            p.op("sp", lambda: nc.sync.dma_start(
                out=gb[:], in_=ln_g[L, j:j + 1, :].partition_broadcast(128)), writes=["gb"], dma=True)
            p.op("sp", lambda: nc.sync.dma_start(
                out=bb[:], in_=ln_b[L, j:j + 1, :].partition_broadcast(128)), writes=["bb"], dma=True)
            p.op("dve", lambda: nc.vector.tensor_scalar(out=s1[:], in0=s1[:], scalar1=1.0, scalar2=None,
                                                       op0=ALU.add), reads=["s1"], writes=["s1"])
            p.op("dve", lambda: nc.vector.tensor_scalar(out=G[:], in0=G[:], scalar1=1.0, scalar2=rw / ALPHA,
                                                       op0=ALU.add, op1=ALU.mult), reads=["G"], writes=["G"])
            return s1, sh, G, gb, bb

        def make_epilogue(ph, G, gb, bb, src, dst):
            xres = [sbuf(ph, "xres%d" % i, [128, D], F32) for i in range(2)]
            tmpA = [sbuf(ph, "tmpA%d" % i, [128, D], F32) for i in range(2)]
            st6 = [sbuf(ph, "st6_%d" % i, [128, 2, 6], F32) for i in range(2)]
            mv = [sbuf(ph, "mv%d" % i, [128, 2], F32) for i in range(2)]
            sm = [sbuf(ph, "sm%d" % i, [128, 4], F32) for i in range(2)]
            nh = sbuf(ph, "neghalf", [128, 1], F32)
            p.op("pool", lambda: nc.gpsimd.memset(nh[:], -0.5), writes=["neghalf"])
            cnt = [0]
            pending = [None]

            def stage_a(t, po_list, par):
                xr, ta, s6, m, s = xres[par], tmpA[par], st6[par], mv[par], sm[par]
                kx, kt, ks = ("xres", par), ("tmpA", par), ("stat", par)
                p.op("sp", lambda: nc.sync.dma_start(out=xr[:], in_=src[t * 128:(t + 1) * 128, :]),
                     reads=[("x", t)], writes=[kx], dma=True)
                for (pa, pk, c0, w) in po_list:
                    p.op("dve", lambda pa=pa, c0=c0, w=w: nc.vector.tensor_tensor(
                        out=ta[:, c0:c0 + w], in0=pa, in1=G[:, c0:c0 + w], op=ALU.mult),
                        reads=[pk, "G"], writes=[kt])
                p.op("dve", lambda: nc.vector.tensor_tensor(out=ta[:], in0=ta[:], in1=xr[:], op=ALU.add),
                     reads=[kt, kx], writes=[kt])
                for hh in range(2):
                    p.op("dve", lambda hh=hh: nc.vector.bn_stats(out=s6[:, hh, :], in_=ta[:, hh * 512:(hh + 1) * 512]),
                         reads=[kt], writes=[ks])
                p.op("dve", lambda: nc.vector.bn_aggr(out=m[:], in_=s6[:].rearrange("p a b -> p (a b)")),
                     reads=[ks], writes=[ks])
                p.op("pool", lambda: nc.gpsimd.tensor_scalar(out=s[:, 0:1], in0=m[:, 1:2], scalar1=LN_EPS,
                                                            scalar2=None, op0=ALU.add),
                     reads=[ks], writes=[ks])
                p.op("pool", lambda: nc.gpsimd.tensor_tensor(out=s[:, 1:2], in0=s[:, 0:1], in1=nh[:], op=ALU.pow),
                     reads=[ks, "neghalf"], writes=[ks])
                p.op("pool", lambda: nc.gpsimd.tensor_scalar(out=s[:, 2:3], in0=m[:, 0:1], scalar1=-1.0,
                                                            scalar2=s[:, 1:2], op0=ALU.mult, op1=ALU.mult),
                     reads=[ks], writes=[ks])

            def stage_b(t, par):
                xr, ta, s = xres[par], tmpA[par], sm[par]
                kx, kt, ks = ("xres", par), ("tmpA", par), ("stat", par)
                p.op("act", lambda: nc.scalar.activation(out=xr[:], in_=ta[:], func=AF.Identity,
                                                         bias=s[:, 2:3], scale=s[:, 1:2]),
                     reads=[kt, ks, kx], writes=[kx])
                p.op("dve", lambda: nc.vector.tensor_tensor(out=xr[:], in0=xr[:], in1=gb[:], op=ALU.mult),
                     reads=[kx, "gb"], writes=[kx])
                p.op("pool", lambda: nc.gpsimd.tensor_tensor(out=xr[:], in0=xr[:], in1=bb[:], op=ALU.add),
                     reads=[kx, "bb"], writes=[kx])
                p.op("sp", lambda: nc.sync.dma_start(out=dst[t * 128:(t + 1) * 128, :], in_=xr[:]),
                     reads=[kx], writes=[("x", t), ("xo", t)], dma=True)

            def epi(t, po_list):
                par = cnt[0] % 2
                cnt[0] += 1
                stage_a(t, po_list, par)
                if pending[0] is not None:
                    stage_b(*pending[0])
                pending[0] = (t, par)

            def finish():
                if pending[0] is not None:
                    stage_b(*pending[0])
                    pending[0] = None
            epi.finish = finish
            return epi

        def make_hT_builder(ph, s1, sh, src, width):
            nt = width // 128
            xb = sbuf(ph, "xb", [128, nt, D], BF16)
            hT = sbuf(ph, "hT", [128, NCH, width], BF16)
            pT = [psum(ph, "pT%d" % i, [128, width], BF16) for i in range(2)]

            pre = set()

            def loads(tb):
                for t in range(nt):
                    tt = tb * nt + t
                    p.op("pool", lambda t=t, tt=tt: nc.gpsimd.dma_start(out=xb[:, t, :], in_=src[tt * 128:(tt + 1) * 128, :]),
                         reads=[("x", tt)], writes=[("xb", t)], dma=True)

            def preload(tb):
                pre.add(tb)
                loads(tb)

            def mk(tb):
                if tb in pre:
                    pre.discard(tb)
                else:
                    loads(tb)
                for c in range(NCH):
                    pt = pT[c % 2]
                    pk = ("pT", c % 2)
                    for t in range(nt):
                        p.op("pe", lambda c=c, t=t, pt=pt: nc.tensor.transpose(
                            pt[:, t * 128:(t + 1) * 128], xb[:, t, c * 128:(c + 1) * 128], ident[:]),
                            reads=[("xb", t), "ident"], writes=[pk])
                    p.op("act", lambda c=c, pt=pt: nc.scalar.activation(
                        out=hT[:, c, :], in_=pt[:], func=AF.Identity, bias=sh[:, c:c + 1], scale=s1[:, c:c + 1]),
                        reads=[pk, "s1", "sh"], writes=[("hT", c)])
            mk.preload = preload
            return hT, mk

        def phase_ffn(L, j, src, dst):
            fi = 0 if j == 0 else 1
            with ExitStack() as ph:
                win = sbuf(ph, "win", [128, NCH, 2 * DFF], BF16)
                wout = sbuf(ph, "wout", [128, NHC, D], BF16)
                act = sbuf(ph, "act", [128, NHC, 512], BF16)
                sg = [sbuf(ph, "sg%d" % i, [128, 512], BF16) for i in range(2)]
                s1, sh, G, gb, bb = load_mod_vectors(ph, L, j, 0.5)
                hT, mk_hT = make_hT_builder(ph, s1, sh, src, 512)
                mk_hT.preload(0)
                epi = make_epilogue(ph, G, gb, bb, src, dst)
                pg = [psum(ph, "pg%d" % i, [128, 512], F32) for i in range(2)]
                pu = [psum(ph, "pu%d" % i, [128, 512], F32) for i in range(2)]
                po = [psum(ph, "po%d" % i, [128, 512], F32) for i in range(2)]
                GRP = [(0, 2), (2, 6), (6, 14), (14, 22)]
                grp_of = {}
                for gi, (ma, mb) in enumerate(GRP):
                    for m in range(ma, mb):
                        grp_of[m] = gi
                    for gu in range(2):
                        c0 = gu * DFF + ma * 128
                        wd_ = (mb - ma) * 128
                        for k in range(NCH):
                            p.op("pool", lambda c0=c0, k=k, wd_=wd_: nc.gpsimd.dma_start(
                                out=win[:, k, c0:c0 + wd_], in_=w_ffn_in[L, fi, k * 128:(k + 1) * 128, c0:c0 + wd_]),
                                writes=[("win", gi)], dma=True)
                for m0 in range(0, NHC, 2):
                    p.op("pool", lambda m0=m0: nc.gpsimd.dma_start(
                        out=wout[:, m0:m0 + 2, :],
                        in_=w_ffn_out[L, fi, m0 * 128:(m0 + 2) * 128, :].rearrange("(m p) n -> p m n", p=128)),
                        writes=[("wout", m0 // 2)], dma=True)
                mk_hT(0)
                for tb in range(NB):
                    for m in range(NHC):
                        grp = grp_of[m]
                        a, b = pg[m % 2], pu[m % 2]
                        ka, kb = ("pg", m % 2), ("pu", m % 2)
                        for k in range(NCH):
                            p.op("pe", lambda a=a, k=k, m=m: nc.tensor.matmul(
                                a[:], win[:, k, m * 128:(m + 1) * 128], hT[:, k, :],
                                start=(k == 0), stop=(k == NCH - 1)),
                                reads=[("win", grp), ("hT", k)], writes=[ka])
                        for k in range(NCH):
                            p.op("pe", lambda b=b, k=k, m=m: nc.tensor.matmul(
                                b[:], win[:, k, DFF + m * 128:DFF + (m + 1) * 128], hT[:, k, :],
                                start=(k == 0), stop=(k == NCH - 1)),
                                reads=[("win", grp), ("hT", k)], writes=[kb])
                        s = sg[m % 2]
                        p.op("act", lambda a=a, s=s: nc.scalar.activation(out=s[:], in_=a[:], func=AF.Silu),
                             reads=[ka], writes=[("sg", m % 2)])
                        p.op("dve", lambda b=b, s=s, m=m: nc.vector.tensor_tensor(
                            out=act[:, m, :], in0=b[:], in1=s[:], op=ALU.mult),
                            reads=[kb, ("sg", m % 2)], writes=[("act", m)])
                    if tb + 1 < NB:
                        mk_hT(tb + 1)
                    for t in range(4):
                        tt = tb * 4 + t
                        for hh in range(2):
                            for m in range(NHC):
                                p.op("pe", lambda hh=hh, m=m, t=t: nc.tensor.matmul(
                                    po[hh][:], act[:, m, t * 128:(t + 1) * 128], wout[:, m, hh * 512:(hh + 1) * 512],
                                    start=(m == 0), stop=(m == NHC - 1)),
                                    reads=[("act", m), ("wout", m // 2)], writes=[("po", hh)])
                        epi(tt, [(po[0][:], ("po", 0), 0, 512), (po[1][:], ("po", 1), 512, 512)])
                epi.finish()
                p.flush()


        def phase_outproj(L, w_out_d, src, dst, mod_next=None):
            with ExitStack() as ph:
                mod_ch = make_mod_chunks(ph, mod_next) if mod_next is not None else []
                per_blk = (len(mod_ch) + NB - 1) // NB
                wo = sbuf(ph, "wo", [128, NCH, D], BF16)
                for c0 in range(0, NCH, 2):
                    p.op("pool", lambda c0=c0: nc.gpsimd.dma_start(
                        out=wo[:, c0:c0 + 2, :], in_=w_out_d[c0 * 128:(c0 + 2) * 128, :].rearrange("(c p) n -> p c n", p=128)),
                        writes=[("wo", c0 // 2)], dma=True)
                G, gb, bb = load_out_vectors(ph, L, 1, 1.0)
                epi = make_epilogue(ph, G, gb, bb, src, dst)
                aT = [sbuf(ph, "aT%d" % i, [128, NCH, 512], BF16) for i in range(2)]
                po = [psum(ph, "po%d" % i, [128, 512], F32) for i in range(4)]
                for tb in range(NB):
                    a = aT[tb % 2]
                    ka = ("aT", tb % 2)
                    p.op("sp", lambda a=a, tb=tb: nc.sync.dma_start(
                        out=a[:], in_=attnT_d[:, tb * 512:(tb + 1) * 512].rearrange("(c p) s -> p c s", p=128)),
                        writes=[ka], dma=True)
                    for t in range(4):
                        tt = tb * 4 + t
                        pl = []
                        for hh in range(2):
                            pi = (tt % 2) * 2 + hh
                            for c in range(NCH):
                                p.op("pe", lambda pi=pi, c=c, t=t, a=a, hh=hh: nc.tensor.matmul(
                                    po[pi][:], a[:, c, t * 128:(t + 1) * 128], wo[:, c, hh * 512:(hh + 1) * 512],
                                    start=(c == 0), stop=(c == NCH - 1)),
                                    reads=[ka, ("wo", c // 2)], writes=[("po", pi)])
                            pl.append((po[pi][:], ("po", pi), hh * 512, 512))
                        epi(tt, pl)
                    for _ in range(per_blk):
                        if mod_ch:
                            mod_ch.pop(0)()
                while mod_ch:
                    mod_ch.pop(0)()
                epi.finish()
                p.flush()

        def phase_fox_a(L, r, src):
            with ExitStack() as ph:
                w = sbuf(ph, "wfox", [128, NCH, 3088], BF16)
                for grp, (a, b) in enumerate([(0, 1024), (1024, 2048), (2048, 3072), (3072, 3088)]):
                    for k in range(NCH):
                        p.op("pool", lambda a=a, b=b, k=k: nc.gpsimd.dma_start(
                            out=w[:, k, a:b], in_=fox_w_in[r, k * 128:(k + 1) * 128, a:b]),
                            writes=[("w", grp)], dma=True)
                s1, sh = load_in_vectors(ph, L, 1)
                hT, mk_hT = make_hT_builder(ph, s1, sh, src, 512)
                qst = [sbuf(ph, "qst%d" % i, [128, 512], BF16) for i in range(4)]
                vst = [sbuf(ph, "vst%d" % i, [128, 16, VW], BF16) for i in range(2)]
                for i in range(2):
                    p.op("pool", lambda i=i: nc.gpsimd.memset(vst[i][:, :, 64:VW], 1.0), writes=[("vst", i)])
                fb = sbuf(ph, "fb", [16, 1], F32)
                p.op("sp", lambda: nc.sync.dma_start(out=fb[:], in_=fox_f_bias[r, :].rearrange("(h o) -> h o", o=1)),
                     writes=["fb"], dma=True)
                p.op("dve", lambda: nc.vector.tensor_scalar(out=fb[:], in0=fb[:], scalar1=-1.0, scalar2=None,
                                                           op0=ALU.mult), reads=["fb"], writes=["fb"])
                ones16 = sbuf(ph, "ones16", [16, 512], F32)
                p.op("pool", lambda: nc.gpsimd.memset(ones16[:], 1.0), writes=["ones16"])
                cumneg = sbuf(ph, "cumneg", [16, S], F32)
                ef = sbuf(ph, "ef", [16, 512], F32)
                r1 = sbuf(ph, "r1", [16, 512], F32)
                r2 = sbuf(ph, "r2", [16, 512], F32)
                c3 = sbuf(ph, "c3", [16, 3, 512], BF16)
                ckst = sbuf(ph, "ckst", [128, NT, 16], F32)
                pq = [psum(ph, "pq%d" % i, [128, 512], F32) for i in range(4)]
                pf = psum(ph, "pf", [16, 512], F32)
                ptr = psum(ph, "ptr", [128, 16], F32)
                cnt = [0]

                def proj_fm(col0, grp, c):
                    i = cnt[0] % 4
                    cnt[0] += 1
                    for k in range(NCH):
                        p.op("pe", lambda i=i, k=k: nc.tensor.matmul(
                            pq[i][:], w[:, k, col0:col0 + 128], hT[:, k, :], start=(k == 0), stop=(k == NCH - 1)),
                            reads=[("w", grp), ("hT", k)], writes=[("pq", i)])
                    return i

                for tb in range(NB):
                    mk_hT(tb)
                    for which, (base, dram) in enumerate([(0, QT_d), (1024, KT_d)]):
                        for c in range(NCH):
                            i = proj_fm(base + c * 128, which, c)
                            if which == 0:
                                p.op("act", lambda i=i: nc.scalar.copy(out=qst[i][:], in_=pq[i][:]),
                                     reads=[("pq", i)], writes=[("qst", i)])
                            else:
                                p.op("dve", lambda i=i: nc.vector.tensor_copy(out=qst[i][:], in_=pq[i][:]),
                                     reads=[("pq", i)], writes=[("qst", i)])
                            p.op("sp", lambda i=i, c=c, dram=dram, tb=tb: nc.sync.dma_start(
                                out=dram[c * 128:(c + 1) * 128, tb * 512:(tb + 1) * 512], in_=qst[i][:]),
                                reads=[("qst", i)], writes=[("qkd", which, c, tb)], dma=True)
                    for t in range(4):
                        tt = tb * 4 + t
                        vs = vst[tt % 2]
                        for hh in range(2):
                            i = cnt[0] % 4
                            cnt[0] += 1
                            for k in range(NCH):
                                p.op("pe", lambda i=i, k=k, t=t, hh=hh: nc.tensor.matmul(
                                    pq[i][:], hT[:, k, t * 128:(t + 1) * 128], w[:, k, 2048 + hh * 512:2048 + (hh + 1) * 512],
                                    start=(k == 0), stop=(k == NCH - 1)),
                                    reads=[("w", 2), ("hT", k)], writes=[("pq", i)])
                            eng = "dve" if hh == 0 else "act"
                            if hh == 0:
                                p.op("dve", lambda i=i, vs=vs, hh=hh: nc.vector.tensor_copy(
                                    out=vs[:, hh * 8:(hh + 1) * 8, 0:64], in_=pq[i][:].rearrange("p (h d) -> p h d", d=64)),
                                    reads=[("pq", i)], writes=[("vst", tt % 2)])
                            else:
                                p.op("act", lambda i=i, vs=vs, hh=hh: nc.scalar.copy(
                                    out=vs[:, hh * 8:(hh + 1) * 8, 0:64], in_=pq[i][:].rearrange("p (h d) -> p h d", d=64)),
                                    reads=[("pq", i)], writes=[("vst", tt % 2)])
                        p.op("sp", lambda vs=vs, tt=tt: nc.sync.dma_start(
                            out=V_d[tt * 128:(tt + 1) * 128, :], in_=vs[:].rearrange("p h d -> p (h d)")),
                            reads=[("vst", tt % 2)], writes=[("vd", tt)], dma=True)
                    for k in range(NCH):
                        p.op("pe", lambda k=k: nc.tensor.matmul(
                            pf[:], w[:, k, 3072:3088], hT[:, k, :], start=(k == 0), stop=(k == NCH - 1)),
                            reads=[("w", 3), ("hT", k)], writes=["pf"])
                    p.op("act", lambda: nc.scalar.activation(out=ef[:], in_=pf[:], func=AF.Exp, bias=fb[:], scale=-1.0),
                         reads=["pf", "fb"], writes=["ef"])
                    p.op("act", lambda: nc.scalar.activation(out=ef[:], in_=ef[:], func=AF.Ln, bias=1.0, scale=1.0),
                         reads=["ef"], writes=["ef"])
                    blk = slice(tb * 512, (tb + 1) * 512)
                    init = 0.0 if tb == 0 else cumneg[:, tb * 512 - 1:tb * 512]
                    p.op("dve", lambda blk=blk, init=init: nc.vector.tensor_tensor_scan(
                        out=cumneg[:, blk], data0=ones16[:], data1=ef[:], initial=init, op0=ALU.mult, op1=ALU.add),
                        reads=["ef", "ones16", "cumneg"], writes=["cumneg"])
                    p.op("dve", lambda blk=blk: nc.vector.tensor_scalar(out=r1[:], in0=cumneg[:, blk], scalar1=-8.0,
                                                                      scalar2=None, op0=ALU.mult),
                         reads=["cumneg"], writes=["r1"])
                    p.op("dve", lambda: nc.vector.tensor_copy(out=c3[:, 0, :], in_=r1[:]), reads=["r1"], writes=["c3"])
                    p.op("dve", lambda: nc.vector.tensor_tensor(out=r2[:], in0=r1[:], in1=c3[:, 0, :], op=ALU.subtract),
                         reads=["r1", "c3"], writes=["r2"])
                    p.op("dve", lambda: nc.vector.tensor_copy(out=c3[:, 1, :], in_=r2[:]), reads=["r2"], writes=["c3"])
                    p.op("dve", lambda: nc.vector.tensor_tensor(out=r1[:], in0=r2[:], in1=c3[:, 1, :], op=ALU.subtract),
                         reads=["r2", "c3"], writes=["r1"])
                    p.op("dve", lambda: nc.vector.tensor_copy(out=c3[:, 2, :], in_=r1[:]), reads=["r1"], writes=["c3"])
                    p.op("sp", lambda blk=blk: nc.sync.dma_start(out=cq3_d[:, :, blk], in_=c3[:]),
                         reads=["c3"], writes=[("cq3d", tb)], dma=True)
                    for t in range(4):
                        tt = tb * 4 + t
                        p.op("pe", lambda tt=tt: nc.tensor.transpose(
                            ptr[:], cumneg[:, tt * 128:(tt + 1) * 128], identf[0:16, 0:16]),
                            reads=["cumneg", "identf"], writes=["ptr"])
                        p.op("act", lambda tt=tt: nc.scalar.copy(out=ckst[:, tt, :], in_=ptr[:]),
                             reads=["ptr"], writes=["ckst"])
                p.op("sp", lambda: nc.sync.dma_start(out=ck_d.rearrange("(t p) h -> p t h", p=128), in_=ckst[:]),
                     reads=["ckst"], writes=["ckd"], dma=True)
                p.flush()

        def phase_fox_b(L, r):
            with ExitStack() as ph:
                Vall = sbuf(ph, "Vall", [128, NT, 16 * VW], BF16)
                ckT = sbuf(ph, "ckT", [128, NT, 16], F32)
                Qx = [sbuf(ph, "Qx%d" % i, [67, S], BF16) for i in range(2)]
                Kx = [sbuf(ph, "Kx%d" % i, [67, S], BF16) for i in range(2)]
                pt = [sbuf(ph, "pt%d" % i, [128, 512], BF16) for i in range(3)]
                apair = [sbuf(ph, "apair%d" % i, [128, NT, 128], BF16) for i in range(2)]
                rec = [sbuf(ph, "rec%d" % i, [128, 4], F32) for i in range(2)]
                tst = [sbuf(ph, "tst%d" % i, [128, 512], BF16) for i in range(2)]
                ps = [psum(ph, "ps%d" % i, [128, 512], F32) for i in range(3)]
                pacc = [psum(ph, "pacc%d" % i, [128, 4, 65], F32) for i in range(2)]
                ptr = psum(ph, "ptrb", [128, 512], BF16)
                for t0 in range(0, NT, 4):
                    p.op("sp", lambda t0=t0: nc.sync.dma_start(
                        out=Vall[:, t0:t0 + 4, :], in_=V_d[t0 * 128:(t0 + 4) * 128, :].rearrange("(t p) f -> p t f", p=128)),
                        writes=["Vall"], dma=True)
                p.op("sp", lambda: nc.sync.dma_start(out=ckT[:], in_=ck_d.rearrange("(t p) h -> p t h", p=128)),
                     writes=["ckT"], dma=True)
                for i in range(2):
                    p.op("pool", lambda i=i: nc.gpsimd.memset(Kx[i][64:67, :], 1.0), writes=[("Kx1", i)])
                cnt = 0
                fin = 0
                units = []

                def fox_loads(h):
                    par = h % 2
                    q_, k_ = Qx[par], Kx[par]
                    p.op("sp", lambda: nc.sync.dma_start(out=q_[0:64, :], in_=QT_d[h * 64:(h + 1) * 64, :]),
                         writes=[("Qx", par)], dma=True)
                    p.op("sp", lambda: nc.sync.dma_start(out=q_[64:67, :], in_=cq3_d[h]),
                         writes=[("Qx", par)], dma=True)
                    p.op("sp", lambda: nc.sync.dma_start(out=k_[0:64, :], in_=KT_d[h * 64:(h + 1) * 64, :]),
                         writes=[("Kx", par)], dma=True)

                fox_loads(0)
                for h in range(16):
                    par = h % 2
                    hp = h // 2
                    q_, k_ = Qx[par], Kx[par]
                    rq = [("Qx", par), ("Kx", par), ("Kx1", par)]
                    for qb in range(NB):
                        pa = pacc[fin % 2]
                        kpa = ("pacc", fin % 2)
                        rc = rec[fin % 2]
                        krc = ("rec", fin % 2)
                        fin += 1
                        for j in range(4 * qb + 4):
                            c0 = max(0, j - 4 * qb) * 128
                            i = cnt % 3
                            cnt += 1
                            psi, pti = ps[i], pt[i]
                            kps, kpt = ("ps", i), ("pt", i)
                            ks = slice(j * 128, (j + 1) * 128)
                            q0 = qb * 512
                            with p.capture() as front:
                                if qb == 0 and j == 0 and h + 1 < 16:
                                    fox_loads(h + 1)
                                if j >= 4 * qb:
                                    p.op("pe", lambda psi=psi, k_=k_, q_=q_, ks=ks, c0=c0, q0=q0: nc.tensor.matmul(
                                        psi[:, c0:c0 + 128], k_[:, ks], q_[:, q0 + c0:q0 + c0 + 128], start=True, stop=False,
                                        skip_group_check=True), reads=rq, writes=[kps])
                                    p.op("pe", lambda psi=psi, c0=c0: nc.tensor.matmul(
                                        psi[:, c0:c0 + 128], ident[:], masks[:, 0, :], start=False, stop=True,
                                        skip_group_check=True), reads=["ident", "masks"], writes=[kps])
                                    if c0 + 128 < 512:
                                        p.op("pe", lambda psi=psi, k_=k_, q_=q_, ks=ks, c0=c0, q0=q0: nc.tensor.matmul(
                                            psi[:, c0 + 128:512], k_[:, ks], q_[:, q0 + c0 + 128:q0 + 512], start=False, stop=True,
                                            skip_group_check=True), reads=rq, writes=[kps])
                                else:
                                    p.op("pe", lambda psi=psi, k_=k_, q_=q_, ks=ks, q0=q0: nc.tensor.matmul(
                                        psi[:, :], k_[:, ks], q_[:, q0:q0 + 512], start=True, stop=True),
                                        reads=rq, writes=[kps])
                            with p.capture() as back:
                                p.op("act", lambda psi=psi, pti=pti, c0=c0, j=j, h=h: nc.scalar.activation(
                                    out=pti[:, c0:512], in_=psi[:, c0:512], func=AF.Exp, bias=ckT[:, j, h:h + 1], scale=0.125),
                                    reads=[kps, "ckT"], writes=[kpt])
                                for t in range(c0 // 128, 4):
                                    p.op("pe", lambda pa=pa, pti=pti, t=t, j=j, h=h, qb=qb: nc.tensor.matmul(
                                        pa[:, t, :], pti[:, t * 128:(t + 1) * 128], Vall[:, j, h * VW:h * VW + 65],
                                        start=(j == 0 and t == 0), stop=(j == 4 * qb + t), skip_group_check=True),
                                        reads=[kpt, "Vall"], writes=[kpa])
                                if j == 4 * qb + 3:
                                    ap_ = apair[hp % 2]
                                    kap = ("apair", hp % 2)
                                    p.op("dve", lambda rc=rc, pa=pa: nc.vector.reciprocal(out=rc[:], in_=pa[:, :, 64]),
                                         reads=[kpa], writes=[krc])
                                    for t in range(4):
                                        p.op("dve", lambda ap_=ap_, pa=pa, rc=rc, t=t, qb=qb, par=par: nc.vector.tensor_scalar(
                                            out=ap_[:, qb * 4 + t, par * 64:(par + 1) * 64], in0=pa[:, t, 0:64],
                                            scalar1=rc[:, t:t + 1], scalar2=None, op0=ALU.mult),
                                            reads=[kpa, krc], writes=[kap])
                                    if par == 1 and qb == NB - 1:
                                        emit_pair_transposes(ap_, kap, ptr, tst, hp * 128)
                            units.append((front, back))
                for cap in p.pipelined(units, 2):
                    p.splice(cap)
                p.flush()

        def mixer_fox(L, r, src, dst, mod_next=None):
            phase_fox_a(L, r, src)
            phase_fox_b(L, r)
            phase_outproj(L, fox_w_out[r], src, dst, mod_next)


        def make_rope_proj(ph, w, wsw, hT, pq, wkey, wskey):
            t1 = [sbuf(ph, "rt1_%d" % i, [128, 512], F32) for i in range(2)]
            t2 = [sbuf(ph, "rt2_%d" % i, [128, 512], F32) for i in range(2)]
            cnt = [0]

            def fn(col0, col0s, cs, sn, cskeys, out_ap, out_key, M=128):
                i = cnt[0] % 2
                cnt[0] += 1
                pa, pb = pq[2 * i], pq[2 * i + 1]
                ka, kb = ("pq", 2 * i), ("pq", 2 * i + 1)
                for k in range(NCH):
                    p.op("pe", lambda k=k: nc.tensor.matmul(pa[0:M, :], w[:, k, col0:col0 + M], hT[:, k, :],
                                                             start=(k == 0), stop=(k == NCH - 1)),
                         reads=[wkey, ("hT", k)], writes=[ka])
                for k in range(NCH):
                    p.op("pe", lambda k=k: nc.tensor.matmul(pb[0:M, :], wsw[:, k, col0s:col0s + M], hT[:, k, :],
                                                             start=(k == 0), stop=(k == NCH - 1)),
                         reads=[wskey, ("hT", k)], writes=[kb])
                p.op("dve", lambda: nc.vector.tensor_tensor(out=t1[i][0:M, :], in0=pa[0:M, :], in1=cs, op=ALU.mult),
                     reads=[ka] + cskeys, writes=[("rt1", i)])
                p.op("dve", lambda: nc.vector.tensor_tensor(out=t2[i][0:M, :], in0=pb[0:M, :], in1=sn, op=ALU.mult),
                     reads=[kb] + cskeys, writes=[("rt2", i)])
                p.op("pool", lambda: nc.gpsimd.tensor_tensor(out=out_ap, in0=t1[i][0:M, :], in1=t2[i][0:M, :], op=ALU.add),
                     reads=[("rt1", i), ("rt2", i)], writes=[out_key])
            return fn

        def load_w(wt, dram, ncols, key, step=1024):
            for a in range(0, ncols, step):
                b = min(ncols, a + step)
                for k in range(NCH):
                    p.op("pool", lambda a=a, b=b, k=k: nc.gpsimd.dma_start(
                        out=wt[:, k, a:b], in_=dram[k * 128:(k + 1) * 128, a:b]), writes=[key], dma=True)

        def emit_pair_transposes(apair_t, kap, ptr, tst, row0, scale_ap=None, scale_key=None):
            for qb in range(NB):
                ts_ = tst[qb % 2]
                kts = ("tst", qb % 2)
                for t in range(4):
                    p.op("pe", lambda qb=qb, t=t: nc.tensor.transpose(
                        ptr[:, t * 128:(t + 1) * 128], apair_t[:, qb * 4 + t, :], ident[:]),
                        reads=[kap, "ident"], writes=["ptrb"])
                if scale_ap is None:
                    p.op("dve", lambda ts_=ts_: nc.vector.tensor_copy(out=ts_[:], in_=ptr[:]),
                         reads=["ptrb"], writes=[kts])
                else:
                    p.op("act", lambda ts_=ts_: nc.scalar.activation(out=ts_[:], in_=ptr[:], func=AF.Copy, scale=scale_ap),
                         reads=["ptrb", scale_key], writes=[kts])
                p.op("sp", lambda ts_=ts_, qb=qb: nc.sync.dma_start(
                    out=attnT_d[row0:row0 + 128, qb * 512:(qb + 1) * 512], in_=ts_[:]),
                    reads=[kts], writes=[("attnT", row0, qb)], dma=True)

        def phase_swa_a(L, r, src):
            with ExitStack() as ph:
                w = sbuf(ph, "wswa", [128, NCH, 1280], BF16)
                wsw = sbuf(ph, "wswas", [128, NCH, 1152], BF16)
                load_w(w, swa_w_in[r], 1280, "w")
                load_w(wsw, swa_w_sw[r], 1152, "wsw")
                s1, sh = load_in_vectors(ph, L, 1)
                hT, mk_hT = make_hT_builder(ph, s1, sh, src, 512)
                qst = [sbuf(ph, "qst%d" % i, [128, 512], BF16) for i in range(4)]
                vst = [sbuf(ph, "vst%d" % i, [128, 2, VW], BF16) for i in range(2)]
                for i in range(2):
                    p.op("pool", lambda i=i: nc.gpsimd.memset(vst[i][:, :, 64:VW], 1.0), writes=[("vst", i)])
                cs = [sbuf(ph, "cs%d" % i, [128, 512], F32) for i in range(2)]
                sn = [sbuf(ph, "sn%d" % i, [128, 512], F32) for i in range(2)]
                pq = [psum(ph, "pq%d" % i, [128, 512], F32) for i in range(4)]
                pv = psum(ph, "pv", [128, 128], F32)
                rp = make_rope_proj(ph, w, wsw, hT, pq, "w", "wsw")
                qi = 0
                for tb in range(NB):
                    blk = slice(tb * 512, (tb + 1) * 512)
                    c_, s_ = cs[tb % 2], sn[tb % 2]
                    p.op("sp", lambda c_=c_, blk=blk: nc.sync.dma_start(out=c_[:], in_=cosF_d[:, blk]),
                         writes=[("cs", tb % 2)], dma=True)
                    p.op("sp", lambda s_=s_, blk=blk: nc.sync.dma_start(out=s_[:], in_=sinF_d[:, blk]),
                         writes=[("sn", tb % 2)], dma=True)
                    mk_hT(tb)
                    ck = [("cs", tb % 2), ("sn", tb % 2)]
                    for c in range(NCH + 1):
                        q_ = qst[qi % 4]
                        kq = ("qst", qi % 4)
                        qi += 1
                        rp(c * 128, c * 128, c_[:], s_[:], ck, q_[:], kq)
                        dram, row = (QT_d, c * 128) if c < NCH else (KT_d, 0)
                        p.op("sp", lambda q_=q_, dram=dram, row=row, blk=blk: nc.sync.dma_start(
                            out=dram[row:row + 128, blk], in_=q_[:]), reads=[kq], writes=[("qkd", qi)], dma=True)
                    for t in range(4):
                        tt = tb * 4 + t
                        vs = vst[tt % 2]
                        for k in range(NCH):
                            p.op("pe", lambda k=k, t=t: nc.tensor.matmul(
                                pv[:], hT[:, k, t * 128:(t + 1) * 128], w[:, k, 1152:1280],
                                start=(k == 0), stop=(k == NCH - 1)), reads=["w", ("hT", k)], writes=["pv"])
                        p.op("dve", lambda vs=vs: nc.vector.tensor_copy(
                            out=vs[:, :, 0:64], in_=pv[:].rearrange("p (h d) -> p h d", d=64)),
                            reads=["pv"], writes=[("vst", tt % 2)])
                        p.op("sp", lambda vs=vs, tt=tt: nc.sync.dma_start(
                            out=V_d[tt * 128:(tt + 1) * 128, 0:2 * VW], in_=vs[:].rearrange("p h d -> p (h d)")),
                            reads=[("vst", tt % 2)], writes=[("vd", tt)], dma=True)
                p.flush()

        def phase_swa_b(L, r):
            with ExitStack() as ph:
                Vall = sbuf(ph, "Vall", [128, NT, 2 * VW], BF16)
                Kdup = [sbuf(ph, "Kdup%d" % i, [128, S], BF16) for i in range(2)]
                Qc = [sbuf(ph, "Qc%d" % i, [128, S], BF16) for i in range(2)]
                pt = [sbuf(ph, "pt%d" % i, [128, 256], BF16) for i in range(3)]
                apair = [sbuf(ph, "apair%d" % i, [128, NT, 128], BF16) for i in range(2)]
                den = [sbuf(ph, "den%d" % i, [128, 2], F32) for i in range(4)]
                tst = [sbuf(ph, "tst%d" % i, [128, 512], BF16) for i in range(2)]
                esink = sbuf(ph, "esink", [128, 16], F32)
                ps = [psum(ph, "ps%d" % i, [128, 256], F32) for i in range(3)]
                pacc = [psum(ph, "pacc%d" % i, [128, 65], F32) for i in range(4)]
                ptr = psum(ph, "ptrb", [128, 512], BF16)
                p.op("sp", lambda: nc.sync.dma_start(
                    out=Vall[:], in_=V_d[:, 0:2 * VW].rearrange("(t p) f -> p t f", p=128)), writes=["Vall"], dma=True)
                for g_ in range(2):
                    for half in range(2):
                        p.op("sp", lambda g_=g_, half=half: nc.sync.dma_start(
                            out=Kdup[g_][half * 64:(half + 1) * 64, :], in_=KT_d[g_ * 64:(g_ + 1) * 64, :]),
                            writes=[("Kdup", g_)], dma=True)
                p.op("sp", lambda: nc.sync.dma_start(out=esink[:], in_=swa_sinks[r:r + 1, :].partition_broadcast(128)),
                     writes=["esink"], dma=True)
                p.op("act", lambda: nc.scalar.activation(out=esink[:], in_=esink[:], func=AF.Exp),
                     reads=["esink"], writes=["esink"])
                cnt = 0
                units = []

                def swa_loads(hp):
                    qc = Qc[hp % 2]
                    p.op("sp", lambda: nc.sync.dma_start(out=qc[:], in_=QT_d[hp * 128:(hp + 1) * 128, :]),
                         writes=[("Qc", hp % 2)], dma=True)

                swa_loads(0)
                for hp in range(8):
                    qc = Qc[hp % 2]
                    kqc = ("Qc", hp % 2)
                    ap_ = apair[hp % 2]
                    kap = ("apair", hp % 2)
                    for par in range(2):
                        h = 2 * hp + par
                        g_ = h // 8
                        kx = Kdup[g_][par * 64:(par + 1) * 64, :]
                        qx = qc[par * 64:(par + 1) * 64, :]
                        rq = [kqc, ("Kdup", g_)]
                        for j in range(NT):
                            i = cnt % 3
                            cnt += 1
                            psi, pti = ps[i], pt[i]
                            kps, kpt = ("ps", i), ("pt", i)
                            ks = slice(j * 128, (j + 1) * 128)
                            two = j + 1 < NT
                            wd = 256 if two else 128
                            with p.capture() as front:
                                if par == 0 and j == 0 and hp + 1 < 8:
                                    swa_loads(hp + 1)
                                p.op("pe", lambda psi=psi, kx=kx, qx=qx, ks=ks: nc.tensor.matmul(
                                    psi[:, 0:128], kx[:, ks], qx[:, ks], start=True, stop=False, skip_group_check=True),
                                    reads=rq, writes=[kps])
                                p.op("pe", lambda psi=psi: nc.tensor.matmul(
                                    psi[:, 0:128], ident[:], masks[:, 0, :], start=False, stop=True, skip_group_check=True),
                                    reads=["ident", "masks"], writes=[kps])
                                if two:
                                    ks2 = slice((j + 1) * 128, (j + 2) * 128)
                                    p.op("pe", lambda psi=psi, kx=kx, qx=qx, ks=ks, ks2=ks2: nc.tensor.matmul(
                                        psi[:, 128:256], kx[:, ks], qx[:, ks2], start=False, stop=False, skip_group_check=True),
                                        reads=rq, writes=[kps])
                                    p.op("pe", lambda psi=psi: nc.tensor.matmul(
                                        psi[:, 128:256], ident[:], masks[:, 1, :], start=False, stop=True, skip_group_check=True),
                                        reads=["ident", "masks"], writes=[kps])
                            with p.capture() as back:
                                p.op("act", lambda psi=psi, pti=pti, wd=wd: nc.scalar.activation(
                                    out=pti[:, 0:wd], in_=psi[:, 0:wd], func=AF.Exp, scale=0.125),
                                    reads=[kps], writes=[kpt])
                                a0 = pacc[j % 4]
                                p.op("pe", lambda a0=a0, pti=pti, j=j, g_=g_: nc.tensor.matmul(
                                    a0[:, :], pti[:, 0:128], Vall[:, j, g_ * VW:g_ * VW + 65],
                                    start=(j == 0), stop=True, skip_group_check=True),
                                    reads=[kpt, "Vall"], writes=[("pacc", j % 4)])
                                if two:
                                    a1 = pacc[(j + 1) % 4]
                                    p.op("pe", lambda a1=a1, pti=pti, j=j, g_=g_: nc.tensor.matmul(
                                        a1[:, :], pti[:, 128:256], Vall[:, j, g_ * VW:g_ * VW + 65],
                                        start=True, stop=False, skip_group_check=True),
                                        reads=[kpt, "Vall"], writes=[("pacc", (j + 1) % 4)])
                                dn = den[j % 4]
                                kdn = ("den", j % 4)
                                p.op("dve", lambda dn=dn, a0=a0, h=h: nc.vector.tensor_tensor(
                                    out=dn[:, 0:1], in0=a0[:, 64:65], in1=esink[:, h:h + 1], op=ALU.add),
                                    reads=[("pacc", j % 4), "esink"], writes=[kdn])
                                p.op("dve", lambda dn=dn: nc.vector.reciprocal(out=dn[:, 1:2], in_=dn[:, 0:1]),
                                     reads=[kdn], writes=[kdn])
                                p.op("dve", lambda dn=dn, a0=a0, j=j, par=par, ap_=ap_: nc.vector.tensor_scalar(
                                    out=ap_[:, j, par * 64:(par + 1) * 64], in0=a0[:, 0:64], scalar1=dn[:, 1:2],
                                    scalar2=None, op0=ALU.mult), reads=[("pacc", j % 4), kdn], writes=[kap])
                                if par == 1 and j == NT - 1:
                                    emit_pair_transposes(ap_, kap, ptr, tst, hp * 128)
                            units.append((front, back))
                for cap in p.pipelined(units, 2):
                    p.splice(cap)
                p.flush()

        def mixer_swa(L, r, src, dst, mod_next=None):
            phase_swa_a(L, r, src)
            phase_swa_b(L, r)
            phase_outproj(L, swa_w_out[r], src, dst, mod_next)

        def phase_diff_a(L, r, src):
            with ExitStack() as ph:
                w = sbuf(ph, "wdiff", [128, NCH, 3072], BF16)
                wsw = sbuf(ph, "wdiffs", [128, NCH, 2048], BF16)
                load_w(w, diff_w_in[r], 3072, "w")
                load_w(wsw, diff_w_sw[r], 2048, "wsw")
                s1, sh = load_in_vectors(ph, L, 1)
                hT, mk_hT = make_hT_builder(ph, s1, sh, src, 512)
                qst = [sbuf(ph, "qst%d" % i, [128, 512], BF16) for i in range(4)]
                vst = [sbuf(ph, "vst%d" % i, [128, 8, VW2], BF16) for i in range(2)]
                for i in range(2):
                    p.op("pool", lambda i=i: nc.gpsimd.memset(vst[i][:, :, 128:VW2], 1.0), writes=[("vst", i)])
                cs = [sbuf(ph, "cs%d" % i, [128, 512], F32) for i in range(2)]
                sn = [sbuf(ph, "sn%d" % i, [128, 512], F32) for i in range(2)]
                pq = [psum(ph, "pq%d" % i, [128, 512], F32) for i in range(4)]
                pv = [psum(ph, "pv%d" % i, [128, 512], F32) for i in range(2)]
                rp = make_rope_proj(ph, w, wsw, hT, pq, "w", "wsw")
                qi = 0
                for tb in range(NB):
                    blk = slice(tb * 512, (tb + 1) * 512)
                    c_, s_ = cs[tb % 2], sn[tb % 2]
                    p.op("sp", lambda c_=c_, blk=blk: nc.sync.dma_start(out=c_[:], in_=cosF_d[:, blk]),
                         writes=[("cs", tb % 2)], dma=True)
                    p.op("sp", lambda s_=s_, blk=blk: nc.sync.dma_start(out=s_[:], in_=sinF_d[:, blk]),
                         writes=[("sn", tb % 2)], dma=True)
                    mk_hT(tb)
                    ck = [("cs", tb % 2), ("sn", tb % 2)]
                    for c in range(2 * NCH):
                        q_ = qst[qi % 4]
                        kq = ("qst", qi % 4)
                        qi += 1
                        rp(c * 128, c * 128, c_[:], s_[:], ck, q_[:], kq)
                        dram, row = (QT_d, c * 128) if c < NCH else (KT_d, (c - NCH) * 128)
                        p.op("sp", lambda q_=q_, dram=dram, row=row, blk=blk: nc.sync.dma_start(
                            out=dram[row:row + 128, blk], in_=q_[:]), reads=[kq], writes=[("qkd", qi)], dma=True)
                    for t in range(4):
                        tt = tb * 4 + t
                        vs = vst[tt % 2]
                        for hh in range(2):
                            for k in range(NCH):
                                p.op("pe", lambda k=k, t=t, hh=hh: nc.tensor.matmul(
                                    pv[hh][:], hT[:, k, t * 128:(t + 1) * 128], w[:, k, 2048 + hh * 512:2048 + (hh + 1) * 512],
                                    start=(k == 0), stop=(k == NCH - 1)), reads=["w", ("hT", k)], writes=[("pv", hh)])
                            if hh == 0:
                                p.op("dve", lambda vs=vs, hh=hh: nc.vector.tensor_copy(
                                    out=vs[:, hh * 4:(hh + 1) * 4, 0:128], in_=pv[hh][:].rearrange("p (h d) -> p h d", d=128)),
                                    reads=[("pv", hh)], writes=[("vst", tt % 2)])
                            else:
                                p.op("act", lambda vs=vs, hh=hh: nc.scalar.copy(
                                    out=vs[:, hh * 4:(hh + 1) * 4, 0:128], in_=pv[hh][:].rearrange("p (h d) -> p h d", d=128)),
                                    reads=[("pv", hh)], writes=[("vst", tt % 2)])
                        p.op("sp", lambda vs=vs, tt=tt: nc.sync.dma_start(
                            out=V_d[tt * 128:(tt + 1) * 128, 0:8 * VW2], in_=vs[:].rearrange("p h d -> p (h d)")),
                            reads=[("vst", tt % 2)], writes=[("vd", tt)], dma=True)
                p.flush()

        def phase_diff_b(L, r):
            lam_init = 0.8 - 0.6 * math.exp(-0.3 * L)
            with ExitStack() as ph:
                Vall = sbuf(ph, "Vall", [128, NT, 8 * VW2], BF16)
                Qc = [sbuf(ph, "Qc%d" % i, [128, S], BF16) for i in range(2)]
                Kc = [sbuf(ph, "Kc%d" % i, [128, S], BF16) for i in range(2)]
                pt = [sbuf(ph, "pt%d" % i, [128, 512], BF16) for i in range(3)]
                lt = sbuf(ph, "lamt", [128, 256], F32)
                lsm = sbuf(ph, "lsm", [128, 8], F32)
                sub = sbuf(ph, "subln", [128, 1], F32)
                onesb = sbuf(ph, "onesb", [128, 128], BF16)
                r1 = sbuf(ph, "r1", [128, 512], F32)
                o_ = sbuf(ph, "o_", [128, 512], F32)
                r2 = sbuf(ph, "r2", [128, 512], F32)
                t2 = sbuf(ph, "t2", [128, 512], F32)
                sq = sbuf(ph, "sq", [128, 512], BF16)
                rs = sbuf(ph, "rs", [128, 512], F32)
                ot = [sbuf(ph, "ot%d" % i, [128, 512], BF16) for i in range(2)]
                ps = [psum(ph, "ps%d" % i, [128, 512], F32) for i in range(3)]
                oacc = [psum(ph, "oacc%d" % i, [128, 512], F32) for i in range(2)]
                dacc = [psum(ph, "dacc%d" % i, [128, 512], F32) for i in range(2)]
                pss = psum(ph, "pss", [128, 512], F32)
                p.op("pool", lambda: nc.gpsimd.memset(onesb[:], 1.0), writes=["onesb"])
                onesf = sbuf(ph, "onesf", [128, 128], F32)
                p.op("pool", lambda: nc.gpsimd.memset(onesf[:], 1.0), writes=["onesf"])
                dsum = [sbuf(ph, "dsum%d" % i, [128, 512], F32) for i in range(2)]
                epsT = sbuf(ph, "epsT", [128, 1], F32)
                p.op("pool", lambda: nc.gpsimd.memset(epsT[:], RMS_EPS), writes=["epsT"])
                for t0 in range(0, NT, 4):
                    p.op("sp", lambda t0=t0: nc.sync.dma_start(
                        out=Vall[:, t0:t0 + 4, :],
                        in_=V_d[t0 * 128:(t0 + 4) * 128, 0:8 * VW2].rearrange("(t p) f -> p t f", p=128)),
                        writes=["Vall"], dma=True)
                p.op("sp", lambda: nc.sync.dma_start(out=lt[:], in_=diff_lambda[r:r + 1, :].partition_broadcast(128)),
                     writes=["lt"], dma=True)
                p.op("sp", lambda: nc.sync.dma_start(out=sub[:], in_=diff_subln[r, :].rearrange("(p o) -> p o", o=1)),
                     writes=["sub"], dma=True)
                p.op("dve", lambda: nc.vector.tensor_scalar(out=sub[:], in0=sub[:], scalar1=1.0 - lam_init, scalar2=None,
                                                           op0=ALU.mult), reads=["sub"], writes=["sub"])
                p.op("dve", lambda: nc.vector.tensor_tensor(out=lt[:, 0:64], in0=lt[:, 0:64], in1=lt[:, 64:128], op=ALU.mult),
                     reads=["lt"], writes=["lt"])
                p.op("dve", lambda: nc.vector.tensor_tensor(out=lt[:, 128:192], in0=lt[:, 128:192], in1=lt[:, 192:256], op=ALU.mult),
                     reads=["lt"], writes=["lt"])
                p.op("dve", lambda: nc.vector.reduce_sum(out=lsm[:, 0:1], in_=lt[:, 0:64], axis=AX.X), reads=["lt"], writes=["lsm"])
                p.op("dve", lambda: nc.vector.reduce_sum(out=lsm[:, 1:2], in_=lt[:, 128:192], axis=AX.X), reads=["lt"], writes=["lsm"])
                p.op("act", lambda: nc.scalar.activation(out=lsm[:, 2:4], in_=lsm[:, 0:2], func=AF.Exp), reads=["lsm"], writes=["lsm"])
                p.op("dve", lambda: nc.vector.tensor_tensor(out=lsm[:, 4:5], in0=lsm[:, 3:4], in1=lsm[:, 2:3], op=ALU.subtract),
                     reads=["lsm"], writes=["lsm"])
                p.op("dve", lambda: nc.vector.tensor_scalar(out=lsm[:, 5:6], in0=lsm[:, 4:5], scalar1=-lam_init, scalar2=None,
                                                           op0=ALU.add), reads=["lsm"], writes=["lsm"])
                cnt = 0
                fin = 0
                units = []

                def diff_loads(h):
                    qc, kc = Qc[h % 2], Kc[h % 2]
                    p.op("sp", lambda: nc.sync.dma_start(out=qc[:], in_=QT_d[h * 128:(h + 1) * 128, :]),
                         writes=[("Qc", h % 2)], dma=True)
                    p.op("sp", lambda: nc.sync.dma_start(out=kc[:], in_=KT_d[h * 128:(h + 1) * 128, :]),
                         writes=[("Kc", h % 2)], dma=True)

                diff_loads(0)
                for h in range(8):
                    qc, kc = Qc[h % 2], Kc[h % 2]
                    rq = [("Qc", h % 2), ("Kc", h % 2)]
                    for qb in range(NB):
                        for c in range(2):
                            kx = kc[c * 64:(c + 1) * 64, :]
                            qx = qc[c * 64:(c + 1) * 64, :]
                            oa, da = oacc[c], dacc[c]
                            koa, kda = ("oacc", c), ("dacc", c)
                            for j in range(4 * qb + 4):
                                c0 = max(0, j - 4 * qb) * 128
                                i = cnt % 3
                                cnt += 1
                                psi, pti = ps[i], pt[i]
                                kps, kpt = ("ps", i), ("pt", i)
                                ks = slice(j * 128, (j + 1) * 128)
                                q0 = qb * 512
                                with p.capture() as front:
                                    if qb == 0 and c == 0 and j == 0 and h + 1 < 8:
                                        diff_loads(h + 1)
                                    if j >= 4 * qb:
                                        p.op("pe", lambda psi=psi, kx=kx, qx=qx, ks=ks, c0=c0, q0=q0: nc.tensor.matmul(
                                            psi[:, c0:c0 + 128], kx[:, ks], qx[:, q0 + c0:q0 + c0 + 128], start=True, stop=False,
                                            skip_group_check=True), reads=rq, writes=[kps])
                                        p.op("pe", lambda psi=psi, c0=c0: nc.tensor.matmul(
                                            psi[:, c0:c0 + 128], ident[:], masks[:, 0, :], start=False, stop=True,
                                            skip_group_check=True), reads=["ident", "masks"], writes=[kps])
                                        if c0 + 128 < 512:
                                            p.op("pe", lambda psi=psi, kx=kx, qx=qx, ks=ks, c0=c0, q0=q0: nc.tensor.matmul(
                                                psi[:, c0 + 128:512], kx[:, ks], qx[:, q0 + c0 + 128:q0 + 512], start=False, stop=True,
                                                skip_group_check=True), reads=rq, writes=[kps])
                                    else:
                                        p.op("pe", lambda psi=psi, kx=kx, qx=qx, ks=ks, q0=q0: nc.tensor.matmul(
                                            psi[:, :], kx[:, ks], qx[:, q0:q0 + 512], start=True, stop=True), reads=rq, writes=[kps])
                                with p.capture() as back:
                                    p.op("act", lambda psi=psi, pti=pti, c0=c0: nc.scalar.activation(
                                        out=pti[:, c0:512], in_=psi[:, c0:512], func=AF.Exp, scale=0.125),
                                        reads=[kps], writes=[kpt])
                                    last = (j == 4 * qb + 3)
                                    p.op("pe", lambda oa=oa, pti=pti, j=j, h=h, c0=c0, last=last: nc.tensor.matmul(
                                        oa[:, c0:512], Vall[:, j, h * VW2:h * VW2 + 128], pti[:, c0:512],
                                        start=(j == 0), stop=last, skip_group_check=True),
                                        reads=[kpt, "Vall"], writes=[koa])
                                    ds = dsum[c]
                                    kds = ("dsum", c)
                                    if j == 0:
                                        p.op("dve", lambda ds=ds, pti=pti: nc.vector.tensor_copy(out=ds[:], in_=pti[:]),
                                             reads=[kpt], writes=[kds])
                                    else:
                                        p.op("dve", lambda ds=ds, pti=pti, c0=c0: nc.vector.tensor_tensor(
                                            out=ds[:, c0:512], in0=ds[:, c0:512], in1=pti[:, c0:512], op=ALU.add),
                                            reads=[kpt, kds], writes=[kds])
                                    if last:
                                        p.op("pe", lambda da=da, ds=ds: nc.tensor.matmul(da[:], onesf[:], ds[:], start=True, stop=True),
                                             reads=[kds, "onesf"], writes=[kda])
                                    if c == 0 and last:
                                        p.op("act", lambda: nc.scalar.activation(out=r1[:], in_=dacc[0][:], func=AF.Ln),
                                             reads=[("dacc", 0)], writes=["r1"])
                                        p.op("act", lambda: nc.scalar.activation(out=r1[:], in_=r1[:], func=AF.Exp, scale=-1.0),
                                             reads=["r1"], writes=["r1"])
                                        p.op("dve", lambda: nc.vector.tensor_tensor(out=o_[:], in0=oacc[0][:], in1=r1[:], op=ALU.mult),
                                             reads=[("oacc", 0), "r1"], writes=["o_"])
                                    if c == 1 and last:
                                        ot_ = ot[fin % 2]
                                        kot = ("ot", fin % 2)
                                        fin += 1
                                        p.op("act", lambda: nc.scalar.activation(out=r2[:], in_=dacc[1][:], func=AF.Ln),
                                             reads=[("dacc", 1)], writes=["r2"])
                                        p.op("act", lambda: nc.scalar.activation(out=r2[:], in_=r2[:], func=AF.Exp, scale=-1.0),
                                             reads=["r2"], writes=["r2"])
                                        p.op("dve", lambda: nc.vector.tensor_tensor(out=t2[:], in0=oacc[1][:], in1=r2[:], op=ALU.mult),
                                             reads=[("oacc", 1), "r2"], writes=["t2"])
                                        p.op("dve", lambda: nc.vector.scalar_tensor_tensor(
                                            out=o_[:], in0=t2[:], scalar=lsm[:, 5:6], in1=o_[:], op0=ALU.mult, op1=ALU.add),
                                            reads=["t2", "lsm", "o_"], writes=["o_"])
                                        p.op("pool", lambda: nc.gpsimd.tensor_tensor(out=sq[:], in0=o_[:], in1=o_[:], op=ALU.mult),
                                             reads=["o_"], writes=["sq"])
                                        p.op("pe", lambda: nc.tensor.matmul(pss[:], onesb[:], sq[:], start=True, stop=True),
                                             reads=["sq", "onesb"], writes=["pss"])
                                        p.op("act", lambda: nc.scalar.activation(out=rs[:], in_=pss[:], func=AF.Ln,
                                                                                 bias=epsT[:, 0:1], scale=1.0 / 128),
                                             reads=["pss", "epsT"], writes=["rs"])
                                        p.op("act", lambda: nc.scalar.activation(out=rs[:], in_=rs[:], func=AF.Exp, scale=-0.5),
                                             reads=["rs"], writes=["rs"])
                                        p.op("dve", lambda ot_=ot_: nc.vector.scalar_tensor_tensor(
                                            out=ot_[:], in0=o_[:], scalar=sub[:, 0:1], in1=rs[:], op0=ALU.mult, op1=ALU.mult),
                                            reads=["o_", "sub", "rs"], writes=[kot])
                                        p.op("sp", lambda ot_=ot_, h=h, qb=qb: nc.sync.dma_start(
                                            out=attnT_d[h * 128:(h + 1) * 128, qb * 512:(qb + 1) * 512], in_=ot_[:]),
                                            reads=[kot], writes=[("attnT", h, qb)], dma=True)
                                units.append((front, back))
                for cap in p.pipelined(units, 2):
                    p.splice(cap)
                p.flush()

        def mixer_diff(L, r, src, dst, mod_next=None):
            phase_diff_a(L, r, src)
            phase_diff_b(L, r)
            phase_outproj(L, diff_w_out[r], src, dst, mod_next)


        def phase_dsa_a(L, r, src):
            with ExitStack() as ph:
                w = sbuf(ph, "wdsa", [128, NCH, 1448], BF16)
                wsw = sbuf(ph, "wdsas", [128, NCH, 1312], BF16)
                load_w(w, dsa_w_in[r], 1448, "w")
                load_w(wsw, dsa_w_sw[r], 1312, "wsw")
                wkv = sbuf(ph, "wkv", [128, 128], BF16)
                wkvs = sbuf(ph, "wkvs", [128, 64], BF16)
                kvn = sbuf(ph, "kvn", [128, 1], F32)
                p.op("pool", lambda: nc.gpsimd.dma_start(out=wkv[:], in_=dsa_w_kv_up[r]), writes=["wkv"], dma=True)
                p.op("pool", lambda: nc.gpsimd.dma_start(out=wkvs[:], in_=dsa_w_kv_sw[r]), writes=["wkvs"], dma=True)
                p.op("sp", lambda: nc.sync.dma_start(out=kvn[:], in_=dsa_kv_norm[r, :].rearrange("(p o) -> p o", o=1)),
                     writes=["kvn"], dma=True)
                s1, sh = load_in_vectors(ph, L, 1)
                hT, mk_hT = make_hT_builder(ph, s1, sh, src, 512)
                qst = [sbuf(ph, "qst%d" % i, [128, 512], BF16) for i in range(4)]
                vst = [sbuf(ph, "vst%d" % i, [128, VW], BF16) for i in range(2)]
                for i in range(2):
                    p.op("pool", lambda i=i: nc.gpsimd.memset(vst[i][:, 64:VW], 1.0), writes=[("vst", i)])
                cs = [sbuf(ph, "cs%d" % i, [128, 512], F32) for i in range(2)]
                sn = [sbuf(ph, "sn%d" % i, [128, 512], F32) for i in range(2)]
                csi = [sbuf(ph, "csi%d" % i, [128, 512], F32) for i in range(2)]
                sni = [sbuf(ph, "sni%d" % i, [128, 512], F32) for i in range(2)]
                ckn = [sbuf(ph, "ckn%d" % i, [128, 128], BF16) for i in range(2)]
                ckvT = sbuf(ph, "ckvT", [128, 512], BF16)
                wist = sbuf(ph, "wist", [128, NT, 8], F32)
                fs = [sbuf(ph, "fs%d" % i, [128, 4], F32) for i in range(2)]
                jk = sbuf(ph, "junk", [128, 128], F32)
                nh = sbuf(ph, "neghalf3", [128, 1], F32)
                kt1 = sbuf(ph, "kt1", [64, 512], F32)
                kt2 = sbuf(ph, "kt2", [64, 512], F32)
                p.op("pool", lambda: nc.gpsimd.memset(nh[:], -0.5), writes=["nh"])
                pq = [psum(ph, "pq%d" % i, [128, 512], F32) for i in range(4)]
                pv = psum(ph, "pv", [128, 128], F32)
                pT2 = psum(ph, "pT2", [128, 512], BF16)
                rp = make_rope_proj(ph, w, wsw, hT, pq, "w", "wsw")
                qi = 0
                for tb in range(NB):
                    blk = slice(tb * 512, (tb + 1) * 512)
                    b2 = tb % 2
                    for tl, dr, nm_ in ((cs, cosF_d, "cs"), (sn, sinF_d, "sn"), (csi, cosI_d, "csi"), (sni, sinI_d, "sni")):
                        p.op("sp", lambda tl=tl, dr=dr, blk=blk, b2=b2: nc.sync.dma_start(out=tl[b2][:], in_=dr[:, blk]),
                             writes=[(nm_, b2)], dma=True)
                    mk_hT(tb)
                    ck = [("cs", b2), ("sn", b2)]
                    cki = [("csi", b2), ("sni", b2)]
                    for c in range(NCH + 3):
                        q_ = qst[qi % 4]
                        kq = ("qst", qi % 4)
                        qi += 1
                        if c < NCH:
                            rp(c * 128, c * 128, cs[b2][:], sn[b2][:], ck, q_[:], kq)
                            p.op("sp", lambda q_=q_, c=c, blk=blk: nc.sync.dma_start(
                                out=QT_d[c * 128:(c + 1) * 128, blk], in_=q_[:]), reads=[kq], writes=[("qkd", qi)], dma=True)
                        elif c < NCH + 2:
                            ci = c - NCH
                            rp(1152 + ci * 128, 1024 + ci * 128, csi[b2][:], sni[b2][:], cki, q_[:], kq)
                            p.op("sp", lambda q_=q_, ci=ci, blk=blk: nc.sync.dma_start(
                                out=QI_d[ci * 128:(ci + 1) * 128, blk], in_=q_[:]), reads=[kq], writes=[("qkd", qi)], dma=True)
                        else:
                            rp(1408, 1280, csi[b2][0:32, :], sni[b2][0:32, :], cki, q_[0:32, :], kq, M=32)
                            p.op("sp", lambda q_=q_, blk=blk: nc.sync.dma_start(
                                out=KI_d[:, blk], in_=q_[0:32, :]), reads=[kq], writes=[("qkd", qi)], dma=True)
                    for t in range(4):
                        tt = tb * 4 + t
                        f_ = fs[tt % 2]
                        kf = ("fs", tt % 2)
                        cn = ckn[tt % 2]
                        kcn = ("ckn", tt % 2)
                        for k in range(NCH):
                            p.op("pe", lambda k=k, t=t: nc.tensor.matmul(
                                pv[:], hT[:, k, t * 128:(t + 1) * 128], w[:, k, 1024:1152],
                                start=(k == 0), stop=(k == NCH - 1)), reads=["w", ("hT", k)], writes=["pv"])
                        p.op("act", lambda f_=f_: nc.scalar.activation(out=jk[:], in_=pv[:], func=AF.Square, accum_out=f_[:, 0:1]),
                             reads=["pv"], writes=[kf, "junk"])
                        p.op("pool", lambda f_=f_: nc.gpsimd.tensor_scalar(out=f_[:, 1:2], in0=f_[:, 0:1], scalar1=1.0 / 128,
                                                                          scalar2=RMS_EPS, op0=ALU.mult, op1=ALU.add),
                             reads=[kf], writes=[kf])
                        p.op("pool", lambda f_=f_: nc.gpsimd.tensor_tensor(out=f_[:, 2:3], in0=f_[:, 1:2], in1=nh[:], op=ALU.pow),
                             reads=[kf, "nh"], writes=[kf])
                        p.op("dve", lambda f_=f_, cn=cn: nc.vector.tensor_scalar(out=cn[:], in0=pv[:], scalar1=f_[:, 2:3],
                                                                               scalar2=None, op0=ALU.mult),
                             reads=["pv", kf], writes=[kcn])
                        p.op("pe", lambda cn=cn, t=t: nc.tensor.transpose(pT2[:, t * 128:(t + 1) * 128], cn[:], ident[:]),
                             reads=[kcn, "ident"], writes=["pT2"])
                        for k in range(NCH):
                            p.op("pe", lambda k=k, t=t: nc.tensor.matmul(
                                pq[3][:, 0:8], hT[:, k, t * 128:(t + 1) * 128], w[:, k, 1440:1448],
                                start=(k == 0), stop=(k == NCH - 1)), reads=["w", ("hT", k)], writes=[("pq", 3)])
                        p.op("dve", lambda tt=tt: nc.vector.tensor_scalar(out=wist[:, tt, :], in0=pq[3][:, 0:8], scalar1=1.0 / 16,
                                                                        scalar2=None, op0=ALU.mult),
                             reads=[("pq", 3)], writes=["wist"])
                    p.op("act", lambda: nc.scalar.activation(out=ckvT[:], in_=pT2[:], func=AF.Copy, scale=kvn[:, 0:1]),
                         reads=["pT2", "kvn"], writes=["ckvT"])
                    p.op("pe", lambda: nc.tensor.matmul(pq[0][0:64, :], wkv[:, 0:64], ckvT[:], start=True, stop=True),
                         reads=["wkv", "ckvT"], writes=[("pq", 0)])
                    p.op("pe", lambda: nc.tensor.matmul(pq[1][0:64, :], wkvs[:, 0:64], ckvT[:], start=True, stop=True),
                         reads=["wkvs", "ckvT"], writes=[("pq", 1)])
                    p.op("dve", lambda b2=b2: nc.vector.tensor_tensor(out=kt1[:], in0=pq[0][0:64, :], in1=cs[b2][0:64, :], op=ALU.mult),
                         reads=[("pq", 0)] + ck, writes=["kt1"])
                    p.op("dve", lambda b2=b2: nc.vector.tensor_tensor(out=kt2[:], in0=pq[1][0:64, :], in1=sn[b2][0:64, :], op=ALU.mult),
                         reads=[("pq", 1)] + ck, writes=["kt2"])
                    q_ = qst[qi % 4]
                    kq = ("qst", qi % 4)
                    qi += 1
                    p.op("pool", lambda q_=q_: nc.gpsimd.tensor_tensor(out=q_[0:64, :], in0=kt1[:], in1=kt2[:], op=ALU.add),
                         reads=["kt1", "kt2"], writes=[kq])
                    p.op("sp", lambda q_=q_, blk=blk: nc.sync.dma_start(out=KT_d[0:64, blk], in_=q_[0:64, :]),
                         reads=[kq], writes=[("qkd", qi)], dma=True)
                    for t in range(4):
                        tt = tb * 4 + t
                        vs = vst[tt % 2]
                        p.op("pe", lambda t=t: nc.tensor.matmul(pv[:, 0:64], ckvT[:, t * 128:(t + 1) * 128], wkv[:, 64:128],
                                                                start=True, stop=True),
                             reads=["wkv", "ckvT"], writes=["pv"])
                        p.op("dve", lambda vs=vs: nc.vector.tensor_copy(out=vs[:, 0:64], in_=pv[:, 0:64]),
                             reads=["pv"], writes=[("vst", tt % 2)])
                        p.op("sp", lambda vs=vs, tt=tt: nc.sync.dma_start(out=V_d[tt * 128:(tt + 1) * 128, 0:VW], in_=vs[:]),
                             reads=[("vst", tt % 2)], writes=[("vd", tt)], dma=True)
                p.op("sp", lambda: nc.sync.dma_start(out=WI_d.rearrange("(t p) h -> p t h", p=128), in_=wist[:]),
                     reads=["wist"], writes=["wid"], dma=True)
                p.flush()

        def phase_dsa_b(L, r):
            NIT = 20
            KSEL = min(256, S // 4)
            with ExitStack() as ph:
                KT2 = sbuf(ph, "KT2", [128, S], BF16)
                Vall = sbuf(ph, "Vall", [128, NT, VW], BF16)
                KI = sbuf(ph, "KI", [32, S], BF16)
                WI = sbuf(ph, "WI", [128, NT, 8], F32)
                maskT = [sbuf(ph, "maskT%d" % i, [128, NT, 512], BF16) for i in range(2)]
                ablk = sbuf(ph, "ablk", [128, 4, D], BF16)
                QTb = [sbuf(ph, "QTb%d" % i, [128, NCH, 512], BF16) for i in range(2)]
                QIb = [sbuf(ph, "QIb%d" % i, [32, 8, 512], BF16) for i in range(2)]
                sc = sbuf(ph, "sc", [128, S], F32)
                junk = sbuf(ph, "junkb", [128, S], BF16)
                nm = sbuf(ph, "nm", [128, S], BF16)
                tmp = [sbuf(ph, "tmp%d" % i, [128, 512], F32) for i in range(2)]
                pt = [sbuf(ph, "pt%d" % i, [128, 512], BF16) for i in range(3)]
                st = sbuf(ph, "st", [128, 8], F32)
                rec = [sbuf(ph, "rec%d" % i, [128, 4], F32) for i in range(2)]
                tst = [sbuf(ph, "tst%d" % i, [128, 512], BF16) for i in range(2)]
                mge = sbuf(ph, "mge", [128, 128], F32)
                psc = [psum(ph, "psc%d" % i, [128, 512], F32) for i in range(2)]
                ps = [psum(ph, "ps%d" % i, [128, 512], F32) for i in range(2)]
                pacc = [psum(ph, "pacc%d" % i, [128, 4, 65], F32) for i in range(2)]
                ptrm = psum(ph, "ptrm", [128, 512], BF16)
                ptra = psum(ph, "ptra", [128, 512], BF16)
                for half in range(2):
                    p.op("sp", lambda half=half: nc.sync.dma_start(out=KT2[half * 64:(half + 1) * 64, :], in_=KT_d[0:64, :]),
                         writes=["KT2"], dma=True)
                p.op("sp", lambda: nc.sync.dma_start(out=Vall[:], in_=V_d[:, 0:VW].rearrange("(t p) f -> p t f", p=128)),
                     writes=["Vall"], dma=True)
                p.op("sp", lambda: nc.sync.dma_start(out=KI[:], in_=KI_d), writes=["KI"], dma=True)
                p.op("sp", lambda: nc.sync.dma_start(out=WI[:], in_=WI_d.rearrange("(t p) h -> p t h", p=128)),
                     writes=["WI"], dma=True)
                p.op("dve", lambda: nc.vector.tensor_copy(out=mge[:], in_=masks[:, 2, :]), reads=["masks"], writes=["mge"])
                neg1 = sbuf(ph, "neg1", [128, 4], F32)
                p.op("pool", lambda: nc.gpsimd.memset(neg1[:], -1.0), writes=["neg1"])
                cA = [0]

                def step_a(qb):
                    caps = []

                    def unit(cost=1.0):
                        c = p.capture()
                        c.cost = cost
                        caps.append(c)
                        return c
                    mt = maskT[qb % 2]
                    kmt = ("maskT", qb % 2)
                    qib = QIb[qb % 2]
                    kqi = ("QIb", qb % 2)
                    with unit():
                        p.op("sp", lambda: nc.sync.dma_start(
                            out=qib[:], in_=QI_d[:, qb * 512:(qb + 1) * 512].rearrange("(h d) s -> d h s", d=32)),
                            writes=[kqi], dma=True)
                    for t in range(4):
                        tt = 4 * qb + t
                        nk = (tt + 1) * 128
                        dg = slice(tt * 128, (tt + 1) * 128)
                        if (tt + 1) * 128 <= KSEL:
                            with unit():
                                if tt > 0:
                                    p.op("pool", lambda tt=tt: nc.gpsimd.memset(nm[:, 0:tt * 128], 0.0), writes=["nm"])
                                p.op("pool", lambda dg=dg: nc.gpsimd.tensor_copy(out=nm[:, dg], in_=masks[:, 2, :]),
                                     reads=["masks"], writes=["nm"])
                        else:
                            nkb = (nk + 511) // 512
                            sck = [("sc", kb) for kb in range(nkb)]
                            for kb in range(nkb):
                                wd = min(512, nk - kb * 512)
                                cols = slice(kb * 512, kb * 512 + wd)
                                for h in range(8):
                                    i = cA[0] % 2
                                    cA[0] += 1
                                    pc = psc[i]
                                    kpc = ("psc", i)
                                    with unit(0.9):
                                        p.op("pe", lambda pc=pc, h=h, t=t, cols=cols, wd=wd: nc.tensor.matmul(
                                            pc[:, 0:wd], qib[:, h, t * 128:(t + 1) * 128], KI[:, cols], start=True, stop=True),
                                            reads=[kqi, "KI"], writes=[kpc])
                                        if h == 0:
                                            p.op("dve", lambda pc=pc, cols=cols, wd=wd, tt=tt: nc.vector.tensor_scalar(
                                                out=sc[:, cols], in0=pc[:, 0:wd], scalar1=0.0, scalar2=WI[:, tt, 0:1],
                                                op0=ALU.max, op1=ALU.mult), reads=[kpc, "WI", "nmdone"], writes=[("sc", kb)])
                                        else:
                                            tm = tmp[i]
                                            ktm = ("tmp", i)
                                            p.op("dve", lambda pc=pc, tm=tm, wd=wd, tt=tt, h=h: nc.vector.tensor_scalar(
                                                out=tm[:, 0:wd], in0=pc[:, 0:wd], scalar1=0.0, scalar2=WI[:, tt, h:h + 1],
                                                op0=ALU.max, op1=ALU.mult), reads=[kpc, "WI"], writes=[ktm])
                                            if h <= 5:
                                                p.op("pool", lambda tm=tm, cols=cols, wd=wd: nc.gpsimd.tensor_tensor(
                                                    out=sc[:, cols], in0=sc[:, cols], in1=tm[:, 0:wd], op=ALU.add),
                                                    reads=[ktm, ("sc", kb)], writes=[("sc", kb)])
                                            else:
                                                p.op("dve", lambda tm=tm, cols=cols, wd=wd: nc.vector.tensor_tensor(
                                                    out=sc[:, cols], in0=sc[:, cols], in1=tm[:, 0:wd], op=ALU.add),
                                                    reads=[ktm, ("sc", kb)], writes=[("sc", kb)])
                            with unit(0.5 + nk / 960.0):
                                p.op("dve", lambda nk=nk: nc.vector.reduce_max(out=st[:, 0:1], in_=sc[:, 0:nk], axis=AX.X,
                                                                              apply_absolute_value=True),
                                     reads=sck, writes=["st"])
                                p.op("pool", lambda dg=dg: nc.gpsimd.tensor_tensor(out=sc[:, dg], in0=sc[:, dg], in1=mge[:], op=ALU.add),
                                     reads=sck + ["mge", "st"], writes=sck)
                                p.op("dve", lambda: nc.vector.tensor_scalar(out=st[:, 1:2], in0=st[:, 0:1], scalar1=-1.0, scalar2=None,
                                                                           op0=ALU.mult), reads=["st"], writes=["st"])
                                p.op("dve", lambda: nc.vector.tensor_scalar(out=st[:, 2:3], in0=st[:, 0:1], scalar1=2.0002, scalar2=1e-6,
                                                                           op0=ALU.mult, op1=ALU.add), reads=["st"], writes=["st"])
                            for n_ in range(1, NIT + 1):
                                f = 2.0 ** (-n_)
                                with unit(0.8 + nk / 960.0):
                                    p.op("dve", lambda f=f: nc.vector.tensor_scalar(out=st[:, 3:4], in0=st[:, 2:3], scalar1=f,
                                                                                  scalar2=st[:, 1:2], op0=ALU.mult, op1=ALU.add),
                                         reads=["st"], writes=["st"])
                                    p.op("dve", lambda nk=nk: nc.vector.tensor_scalar(
                                        out=junk[:, 0:nk], in0=sc[:, 0:nk], scalar1=st[:, 3:4], scalar2=0.0,
                                        op0=ALU.is_ge, op1=ALU.add, accum_out=st[:, 4:5]),
                                        reads=sck + ["st"], writes=["junk", "st"])
                                    p.op("dve", lambda: nc.vector.tensor_scalar(out=st[:, 5:6], in0=st[:, 4:5], scalar1=KSEL - 0.5,
                                                                               scalar2=st[:, 2:3], op0=ALU.is_ge, op1=ALU.mult),
                                         reads=["st"], writes=["st"])
                                    p.op("dve", lambda f=f: nc.vector.scalar_tensor_tensor(out=st[:, 1:2], in0=st[:, 5:6], scalar=f,
                                                                                         in1=st[:, 1:2], op0=ALU.mult, op1=ALU.add),
                                         reads=["st"], writes=["st"])
                            with unit(0.3 + nk / 1900.0):
                                p.op("dve", lambda nk=nk: nc.vector.tensor_scalar(
                                    out=nm[:, 0:nk], in0=sc[:, 0:nk], scalar1=st[:, 1:2], scalar2=NEG, op0=ALU.is_lt, op1=ALU.mult),
                                    reads=sck + ["st"], writes=["nm", "nmdone"])
                        for j0 in range(0, tt + 1, 4):
                            n4 = min(4, tt + 1 - j0)
                            with unit():
                                for i4 in range(n4):
                                    p.op("pe", lambda j0=j0, i4=i4: nc.tensor.transpose(
                                        ptrm[:, i4 * 128:(i4 + 1) * 128], nm[:, (j0 + i4) * 128:(j0 + i4 + 1) * 128], ident[:]),
                                        reads=["nm", "ident"], writes=["ptrm"])
                                p.op("dve", lambda j0=j0, n4=n4, t=t: nc.vector.tensor_copy(
                                    out=mt[:, j0:j0 + n4, t * 128:(t + 1) * 128],
                                    in_=ptrm[:, 0:n4 * 128].rearrange("p (a b) -> p a b", b=128)),
                                    reads=["ptrm"], writes=[kmt])
                    return caps

                cB = [0, 0]

                def step_b(qb):
                    units = []
                    mt = maskT[qb % 2]
                    kmt = ("maskT", qb % 2)
                    qtb = QTb[qb % 2]
                    kqt = ("QTb", qb % 2)
                    first = True
                    for h in range(16):
                        par = h % 2
                        kx = KT2[par * 64:(par + 1) * 64, :]
                        qx = qtb[par * 64:(par + 1) * 64, h // 2, :]
                        pa = pacc[cB[1] % 2]
                        kpa = ("pacc", cB[1] % 2)
                        rc = rec[cB[1] % 2]
                        krc = ("rec", cB[1] % 2)
                        cB[1] += 1
                        for j in range(4 * qb + 4):
                            c0 = max(0, j - 4 * qb) * 128
                            i = cB[0] % 2
                            ip = cB[0] % 3
                            cB[0] += 1
                            psi, pti = ps[i], pt[ip]
                            kps, kpt = ("ps", i), ("pt", ip)
                            ks = slice(j * 128, (j + 1) * 128)
                            with p.capture() as front:
                                if first:
                                    first = False
                                    p.op("sp", lambda: nc.sync.dma_start(
                                        out=qtb[:], in_=QT_d[:, qb * 512:(qb + 1) * 512].rearrange("(c p) s -> p c s", p=128)),
                                        writes=[kqt], dma=True)
                                p.op("pe", lambda psi=psi, kx=kx, qx=qx, ks=ks, c0=c0: nc.tensor.matmul(
                                    psi[:, c0:512], kx[:, ks], qx[:, c0:512], start=True, stop=False, skip_group_check=True),
                                    reads=["KT2", kqt], writes=[kps])
                                p.op("pe", lambda psi=psi, j=j, c0=c0: nc.tensor.matmul(
                                    psi[:, c0:512], ident[:], mt[:, j, c0:512], start=False, stop=True, skip_group_check=True),
                                    reads=["ident", kmt], writes=[kps])
                            with p.capture() as back:
                                p.op("act", lambda psi=psi, pti=pti, c0=c0: nc.scalar.activation(
                                    out=pti[:, c0:512], in_=psi[:, c0:512], func=AF.Exp, scale=0.125),
                                    reads=[kps], writes=[kpt])
                                for t in range(c0 // 128, 4):
                                    p.op("pe", lambda pa=pa, pti=pti, t=t, j=j: nc.tensor.matmul(
                                        pa[:, t, :], pti[:, t * 128:(t + 1) * 128], Vall[:, j, 0:65],
                                        start=(j == 0 and t == 0), stop=(j == 4 * qb + t), skip_group_check=True),
                                        reads=[kpt, "Vall"], writes=[kpa])
                                if j == 4 * qb + 3:
                                    p.op("act", lambda rc=rc, pa=pa: nc.scalar.copy(out=rc[:], in_=pa[:, :, 64]),
                                         reads=[kpa], writes=[krc])
                                    p.op("pool", lambda rc=rc: nc.gpsimd.tensor_tensor(out=rc[:], in0=rc[:], in1=neg1[:], op=ALU.pow),
                                         reads=[krc, "neg1"], writes=[krc])
                                    for t in range(4):
                                        p.op("act", lambda pa=pa, rc=rc, t=t, h=h: nc.scalar.activation(
                                            out=ablk[:, t, h * 64:(h + 1) * 64], in_=pa[:, t, 0:64], func=AF.Copy,
                                            scale=rc[:, t:t + 1]),
                                            reads=[kpa, krc], writes=[("ablk", h // 2)])
                                    if h == 15:
                                        for c in range(NCH):
                                            ts_ = tst[c % 2]
                                            kts = ("tst", c % 2)
                                            for t in range(4):
                                                p.op("pe", lambda c=c, t=t: nc.tensor.transpose(
                                                    ptra[:, t * 128:(t + 1) * 128], ablk[:, t, c * 128:(c + 1) * 128], ident[:]),
                                                    reads=[("ablk", c), "ident"], writes=["ptra"])
                                            p.op("act", lambda ts_=ts_: nc.scalar.copy(out=ts_[:], in_=ptra[:]), reads=["ptra"], writes=[kts])
                                            p.op("sp", lambda ts_=ts_, c=c: nc.sync.dma_start(
                                                out=attnT_d[c * 128:(c + 1) * 128, qb * 512:(qb + 1) * 512], in_=ts_[:]),
                                                reads=[kts], writes=[("attnT", c, qb)], dma=True)
                            units.append((front, back))
                    lst = p.pipelined(units, 1)
                    for c_ in lst:
                        c_.cost = 0.4
                    return lst

                for cap in step_a(0):
                    p.splice(cap)
                for qb in range(NB):
                    lb = step_b(qb)
                    la = step_a(qb + 1) if qb + 1 < NB else []
                    p.merge(la, lb)
                p.flush()

        def mixer_dsa(L, r, src, dst, mod_next=None):
            phase_dsa_a(L, r, src)
            phase_dsa_b(L, r)
            phase_outproj(L, dsa_w_out[r], src, dst, mod_next)

        layers = cfg["layers"]
        phase_mod(layers[:1])
        cur = x_ext
        plan = []
        for L in layers:
            for j in range(3):
                plan.append((L, j))
                if cfg.get("stop_after") == (L, j):
                    break
            else:
                continue
            break
        for idx, (L, j) in enumerate(plan):
            dst = out_ext if idx == len(plan) - 1 else xs
            if j in (0, 2):
                phase_ffn(L, j, cur, dst)
            else:
                mname = cfg.get("mixer_of", {0: "dsa", 1: "fox", 2: "swa", 3: "diff"})[L]
                {"fox": mixer_fox, "swa": mixer_swa, "diff": mixer_diff, "dsa": mixer_dsa}[mname](L, 0, cur, dst, (L + 1) if (L + 1) in layers else None)
            cur = dst
        g.stats = dict(n_ops=p.n_ops, n_wait=p.n_wait)
    return nc, g


def make_consts(S):
    pp = np.arange(128)[:, None]
    ff = np.arange(128)[None, :]
    m = np.zeros((128, 3, 128), np.float32)
    m[:, 0, :] = np.where(pp <= ff, 0.0, NEG)
    m[:, 1, :] = np.where(pp > ff, 0.0, NEG)
    m[:, 2, :] = np.where(pp >= ff, 0.0, NEG)
    inv = (np.float32(ROPE_THETA) ** (-np.arange(0, 16, 2, dtype=np.float32) / np.float32(16))).astype(np.float32)
    ang = (np.arange(S, dtype=np.float32)[:, None] * inv[None, :]).astype(np.float32)
    cosF = np.ones((128, S), np.float32)
    sinF = np.zeros((128, S), np.float32)
    for hh in range(2):
        cosF[hh * 64:hh * 64 + 8] = np.cos(ang).T
        cosF[hh * 64 + 8:hh * 64 + 16] = np.cos(ang).T
        sinF[hh * 64:hh * 64 + 8] = -np.sin(ang).T
        sinF[hh * 64 + 8:hh * 64 + 16] = np.sin(ang).T
    invi = (np.float32(ROPE_THETA) ** (-np.arange(0, 8, 2, dtype=np.float32) / np.float32(8))).astype(np.float32)
    angi = (np.arange(S, dtype=np.float32)[:, None] * invi[None, :]).astype(np.float32)
    cosI = np.ones((128, S), np.float32)
    sinI = np.zeros((128, S), np.float32)
    for hh in range(4):
        cosI[hh * 32:hh * 32 + 4] = np.cos(angi).T
        cosI[hh * 32 + 4:hh * 32 + 8] = np.cos(angi).T
        sinI[hh * 32:hh * 32 + 4] = -np.sin(angi).T
        sinI[hh * 32 + 4:hh * 32 + 8] = np.sin(angi).T
    return dict(ident=_bf(np.eye(128, dtype=np.float32)), identf=np.eye(128, dtype=np.float32), masks=_bf(m),
                cosF=cosF, sinF=sinF, cosI=cosI, sinI=sinI)


def _swap_cols(w, head_dim, half):
    n = w.shape[-1]
    idx = np.arange(n)
    d = idx % head_dim
    src = np.where(d < half, idx + half, np.where(d < 2 * half, idx - half, idx))
    return np.ascontiguousarray(w[..., src])


SHARED_KEYS = ("ln_g", "ln_b", "w_ada", "b_ada", "w_ffn_in", "w_ffn_out",
               "fox_w_in", "fox_f_bias", "fox_w_out", "swa_w_in", "swa_sinks", "swa_w_out",
               "diff_w_in", "diff_subln", "diff_w_out", "dsa_w_in", "dsa_kv_norm", "dsa_w_kv_up", "dsa_w_out")


def make_shared(inputs, S):
    shared = {k: np.ascontiguousarray(inputs[k]) for k in SHARED_KEYS}
    shared.update(make_consts(S))
    shared["swa_w_sw"] = _swap_cols(inputs["swa_w_in"][:, :, :1152], 64, 8)
    shared["diff_w_sw"] = _swap_cols(inputs["diff_w_in"][:, :, :2048], 64, 8)
    dw = inputs["dsa_w_in"]
    shared["dsa_w_sw"] = np.ascontiguousarray(np.concatenate(
        [_swap_cols(dw[:, :, 0:1024], 64, 8), _swap_cols(dw[:, :, 1152:1408], 32, 4), _swap_cols(dw[:, :, 1408:1440], 32, 4)],
        axis=-1))
    shared["dsa_w_kv_sw"] = _swap_cols(inputs["dsa_w_kv_up"][:, :, 0:64], 64, 8)
    shared["diff_lambda"] = np.ascontiguousarray(inputs["diff_lambda"].reshape(1, 256))
    return shared


def make_in_map(inputs, b, S, shared=None):
    m = dict(shared if shared is not None else make_shared(inputs, S))
    m["x"] = np.ascontiguousarray(inputs["x"][b, :S])
    m["cT"] = np.ascontiguousarray(inputs["c"][b].reshape(NCH, 128).T)
    return m


def kernel(**inputs):
    S = inputs["x"].shape[1]
    B = inputs["x"].shape[0]
    cfg = dict(layers=[0, 1, 2, 3], stop_after=None)
    nc, g = build(S, cfg)
    shared = make_shared(inputs, S)
    in_maps = [make_in_map(inputs, b, S, shared) for b in range(B)]
    res = run_bass_kernel_spmd(nc, in_maps, core_ids=list(range(B)))
    return np.stack([r["out"] for r in res.results], axis=0)
```
